# Optimizing a Trainium2 kernel written in Bass

```python
import math
import jax
import jax.numpy as jnp
from jax import lax
import numpy as np

D_MODEL = 2048
BATCH = 16
SEQ = 2048
DEPTH = 4

GRID_W = 64
CTX_LEN = 256
N_MIXERS = 3
N_ATTN_LAYERS = (DEPTH + 2) // 3
N_SSD_LAYERS = (DEPTH + 1) // 3
N_DN_LAYERS = DEPTH // 3
NORM_EPS = 1e-6
CONV_K = 5

N_HEADS = 16
N_KV_HEADS = 4
HEAD_DIM = 128
GQA_GROUP = N_HEADS // N_KV_HEADS
ATT_Q = N_HEADS * HEAD_DIM
ATT_KV = N_KV_HEADS * HEAD_DIM
ATT_IN = 2 * ATT_Q + 2 * ATT_KV
Q_BLOCK = 128
ROPE_FREQS = HEAD_DIM // 4
ROPE_THETA = 10000.0
ATT_SCALE = HEAD_DIM ** -0.5

SSD_D_INNER = 2 * D_MODEL
SSD_HEAD_DIM = 64
SSD_HEADS = SSD_D_INNER // SSD_HEAD_DIM
SSD_GROUPS = 8
SSD_HEADS_PER_GROUP = SSD_HEADS // SSD_GROUPS
SSD_STATE = 128
SSD_CONV_DIM = SSD_D_INNER + 2 * SSD_GROUPS * SSD_STATE
SSD_IN = SSD_D_INNER + SSD_CONV_DIM + 2 * SSD_HEADS
SSD_CHUNK = 128

DN_K_HEADS = 16
DN_V_HEADS = 32
DN_V_PER_K = DN_V_HEADS // DN_K_HEADS
DN_HEAD_DIM = 128
DN_QK_DIM = DN_K_HEADS * DN_HEAD_DIM
DN_V_DIM = DN_V_HEADS * DN_HEAD_DIM
DN_CONV_DIM = 2 * DN_QK_DIM + DN_V_DIM
DN_IN = DN_CONV_DIM + DN_V_DIM + 4 * DN_V_HEADS
DN_CHUNK = 64

DT_MIN = 1e-3
DT_MAX = 1e-1

kernel_name = "hybrid_attn_ssd_deltanet_diffusion_trunk"


def rms_norm(x, w):
    xf = x.astype(jnp.float32)
    y = xf * lax.rsqrt(jnp.mean(xf * xf, axis=-1, keepdims=True) + NORM_EPS)
    return (y * w.astype(jnp.float32)).astype(x.dtype)


def l2_normalize(x):
    xf = x.astype(jnp.float32)
    return xf * lax.rsqrt(jnp.sum(xf * xf, axis=-1, keepdims=True) + NORM_EPS)


def flip_seq(t):
    return jnp.flip(t, axis=1)


def depthwise_conv(x, w):
    ch = x.shape[-1]
    return lax.conv_general_dilated(
        x, w[:, None, :].astype(x.dtype), window_strides=(1,),
        padding=[(CONV_K // 2, CONV_K // 2)],
        dimension_numbers=("NWC", "WIO", "NWC"), feature_group_count=ch)


def modulation(cond, w, b):
    m = jax.nn.silu(cond) @ w + b
    return jnp.split(m, 3, axis=-1)


def axial_rope_tables(n_rows):
    rr, cc = jnp.meshgrid(jnp.arange(n_rows), jnp.arange(GRID_W), indexing="ij")
    row = rr.reshape(-1).astype(jnp.float32)
    col = cc.reshape(-1).astype(jnp.float32)
    inv = 1.0 / (ROPE_THETA ** (jnp.arange(ROPE_FREQS, dtype=jnp.float32) / ROPE_FREQS))
    ang = jnp.concatenate([row[:, None] * inv, col[:, None] * inv], axis=-1)
    return jnp.cos(ang), jnp.sin(ang)


def apply_axial_rope(t, cos, sin):
    b, l, h, d = t.shape
    tf = t.astype(jnp.float32).reshape(b, l, h, 2, 2, ROPE_FREQS)
    t1, t2 = tf[..., 0, :], tf[..., 1, :]
    cs = cos.reshape(l, 1, 2, ROPE_FREQS)
    sn = sin.reshape(l, 1, 2, ROPE_FREQS)
    out = jnp.stack([t1 * cs - t2 * sn, t2 * cs + t1 * sn], axis=-2)
    return out.reshape(b, l, h, d).astype(t.dtype)


def gqa_attend(q, k, v):
    s = jnp.einsum("bqhgd,bkhd->bhgqk", q, k, preferred_element_type=jnp.float32) * ATT_SCALE
    p = jax.nn.softmax(s, axis=-1).astype(v.dtype)
    return jnp.einsum("bhgqk,bkhd->bqhgd", p, v)


def attention_mixer(h_lat, h_ctx, w_in, q_norm, k_norm, w_out, cos, sin, need_ctx):
    def project(h):
        b, l = h.shape[:2]
        q, k, v, gate = jnp.split(h @ w_in, [ATT_Q, ATT_Q + ATT_KV, ATT_Q + 2 * ATT_KV], axis=-1)
        q = rms_norm(q.reshape(b, l, N_HEADS, HEAD_DIM), q_norm)
        k = rms_norm(k.reshape(b, l, N_KV_HEADS, HEAD_DIM), k_norm)
        v = v.reshape(b, l, N_KV_HEADS, HEAD_DIM)
        return q, k, v, gate

    ql, kl, vl, gl = project(h_lat)
    qc, kc, vc, gc = project(h_ctx)
    ql = apply_axial_rope(ql, cos, sin)
    kl = apply_axial_rope(kl, cos, sin)
    k_all = jnp.concatenate([kc, kl], axis=1)
    v_all = jnp.concatenate([vc, vl], axis=1)
    b, l = h_lat.shape[:2]
    nb = l // Q_BLOCK
    qb = jnp.moveaxis(ql.reshape(b, nb, Q_BLOCK, N_KV_HEADS, GQA_GROUP, HEAD_DIM), 1, 0)
    ob = lax.map(lambda q_blk: gqa_attend(q_blk, k_all, v_all), qb)
    o_lat = jnp.moveaxis(ob, 0, 1).reshape(b, l, ATT_Q)
    y_lat = (o_lat * jax.nn.silu(gl)) @ w_out
    y_ctx = None
    if need_ctx:
        lc = h_ctx.shape[1]
        o_ctx = gqa_attend(qc.reshape(b, lc, N_KV_HEADS, GQA_GROUP, HEAD_DIM), kc, vc)
        y_ctx = (o_ctx.reshape(b, lc, ATT_Q) * jax.nn.silu(gc)) @ w_out
    return y_lat, y_ctx


def ssd_scan(x, dt, a_neg, bm, cm, h0):
    b, l = x.shape[:2]
    n = l // SSD_CHUNK
    tril = jnp.tril(jnp.ones((SSD_CHUNK, SSD_CHUNK), dtype=bool))[:, :, None, None]

    def chunks(t):
        return jnp.moveaxis(t.astype(jnp.float32).reshape((b, n, SSD_CHUNK) + t.shape[2:]), 1, 0)

    def step(h, inp):
        xc, dtc, bc, cc = inp
        acum = jnp.cumsum(dtc * a_neg, axis=1)
        seg = acum[:, :, None] - acum[:, None, :]
        decay = jnp.where(tril, jnp.exp(jnp.where(tril, seg, 0.0)), 0.0)
        xdt = xc * dtc[..., None]
        cb = jnp.einsum("bign,bjgn->bijg", cc, bc)
        y = jnp.einsum("bijge,bjgep->bigep", cb[..., None] * decay, xdt)
        y = y + jnp.einsum("bign,bgepn,bige->bigep", cc, h, jnp.exp(acum))
        a_last = acum[:, -1]
        h_new = jnp.exp(a_last)[..., None, None] * h + jnp.einsum(
            "bjge,bjgn,bjgep->bgepn", jnp.exp(a_last[:, None] - acum), bc, xdt)
        return h_new, y

    h_fin, ys = lax.scan(step, h0, (chunks(x), chunks(dt), chunks(bm), chunks(cm)))
    return jnp.moveaxis(ys, 0, 1).reshape(x.shape), h_fin


def ssd_mixer(h_lat, h_ctx, w_in, conv_w, conv_b, dt_bias, a_log, d_skip, norm_w, w_out, need_ctx):
    a_neg = -jnp.exp(a_log.astype(jnp.float32)).reshape(2, SSD_GROUPS, SSD_HEADS_PER_GROUP)
    d = d_skip.astype(jnp.float32).reshape(SSD_GROUPS, SSD_HEADS_PER_GROUP, 1)

    def project(h):
        b, l = h.shape[:2]
        z, xbc, dt = jnp.split(h @ w_in, [SSD_D_INNER, SSD_D_INNER + SSD_CONV_DIM], axis=-1)
        xbc = jax.nn.silu(depthwise_conv(xbc, conv_w) + conv_b)
        xs, bm, cm = jnp.split(xbc, [SSD_D_INNER, SSD_D_INNER + SSD_GROUPS * SSD_STATE], axis=-1)
        xs = xs.reshape(b, l, SSD_GROUPS, SSD_HEADS_PER_GROUP, SSD_HEAD_DIM)
        bm = bm.reshape(b, l, SSD_GROUPS, SSD_STATE)
        cm = cm.reshape(b, l, SSD_GROUPS, SSD_STATE)
        dt = jax.nn.softplus(dt.astype(jnp.float32).reshape(b, l, 2, SSD_HEADS) + dt_bias.astype(jnp.float32))
        dt = dt.reshape(b, l, 2, SSD_GROUPS, SSD_HEADS_PER_GROUP)
        return z, xs, bm, cm, dt

    def bidir(xs, bm, cm, dt, h_fwd, h_bwd):
        y_f, hf = ssd_scan(xs, dt[:, :, 0], a_neg[0], bm, cm, h_fwd)
        y_b, hb = ssd_scan(flip_seq(xs), flip_seq(dt[:, :, 1]), a_neg[1], flip_seq(bm), flip_seq(cm), h_bwd)
        return y_f + flip_seq(y_b), hf, hb

    def finish(y, xs, z):
        b, l = z.shape[:2]
        y = (y + d * xs.astype(jnp.float32)).reshape(b, l, SSD_D_INNER).astype(z.dtype)
        return rms_norm(y * jax.nn.silu(z), norm_w) @ w_out

    zc, xc, bc, cc, dtc = project(h_ctx)
    h0 = jnp.zeros((xc.shape[0], SSD_GROUPS, SSD_HEADS_PER_GROUP, SSD_HEAD_DIM, SSD_STATE), jnp.float32)
    yc, hc_f, hc_b = bidir(xc, bc, cc, dtc, h0, h0)
    zl, xl, bl, cl, dtl = project(h_lat)
    yl, _, _ = bidir(xl, bl, cl, dtl, hc_f, hc_b)
    y_lat = finish(yl, xl, zl)
    y_ctx = finish(yc, xc, zc) if need_ctx else None
    return y_lat, y_ctx


def delta_scan(q, k, v, g, beta, s0):
    b, l = q.shape[:2]
    n = l // DN_CHUNK
    dv = v.shape[-1]
    tril = jnp.tril(jnp.ones((DN_CHUNK, DN_CHUNK), dtype=bool))
    strict = jnp.tril(jnp.ones((DN_CHUNK, DN_CHUNK), dtype=bool), k=-1)

    def chunks(t):
        return jnp.moveaxis(t.astype(jnp.float32).reshape((b, n, DN_CHUNK) + t.shape[2:]), 1, 0)

    def step(s, inp):
        qc, kc, vc, gc, bc = inp
        g_t = jnp.transpose(jnp.cumsum(gc, axis=1), (0, 2, 3, 1))
        b_t = jnp.transpose(bc, (0, 2, 3, 1))
        seg = g_t[..., :, None] - g_t[..., None, :]
        decay = jnp.where(tril, jnp.exp(jnp.where(tril, seg, 0.0)), 0.0)
        kk = jnp.einsum("bihd,bjhd->bhij", kc, kc)
        qk = jnp.einsum("bihd,bjhd->bhij", qc, kc)
        lmat = jnp.where(strict, kk[:, :, None] * decay * b_t[..., :, None], 0.0)
        kh = jnp.swapaxes(kc, 1, 2)[:, :, None]
        qh = jnp.swapaxes(qc, 1, 2)[:, :, None]
        vh = jnp.transpose(vc, (0, 2, 3, 1, 4))
        rhs = jnp.concatenate([vh * b_t[..., None], kh * (b_t * jnp.exp(g_t))[..., None]], axis=-1)
        sol = lax.linalg.triangular_solve(lmat, rhs, left_side=True, lower=True, unit_diagonal=True)
        u, w = sol[..., :dv], sol[..., dv:]
        v_new = u - w @ s
        o = (qh * jnp.exp(g_t)[..., None]) @ s + (qk[:, :, None] * decay) @ v_new
        g_last = g_t[..., -1]
        s_new = jnp.exp(g_last)[..., None, None] * s + jnp.swapaxes(
            kh * jnp.exp(g_last[..., None] - g_t)[..., None], -1, -2) @ v_new
        return s_new, o

    s_fin, os = lax.scan(step, s0, (chunks(q), chunks(k), chunks(v), chunks(g), chunks(beta)))
    o = jnp.transpose(os, (1, 0, 4, 2, 3, 5)).reshape(v.shape)
    return o, s_fin


def deltanet_mixer(h_lat, h_ctx, w_in, conv_w, dt_bias, a_log, norm_w, w_out, need_ctx):
    a_neg = -jnp.exp(a_log.astype(jnp.float32))

    def project(h):
        b, l = h.shape[:2]
        qkv, z, a, bb = jnp.split(
            h @ w_in, [DN_CONV_DIM, DN_CONV_DIM + DN_V_DIM, DN_CONV_DIM + DN_V_DIM + 2 * DN_V_HEADS], axis=-1)
        qkv = jax.nn.silu(depthwise_conv(qkv, conv_w))
        q, k, v = jnp.split(qkv, [DN_QK_DIM, 2 * DN_QK_DIM], axis=-1)
        q = l2_normalize(q.reshape(b, l, DN_K_HEADS, DN_HEAD_DIM)) * (DN_HEAD_DIM ** -0.5)
        k = l2_normalize(k.reshape(b, l, DN_K_HEADS, DN_HEAD_DIM))
        v = v.reshape(b, l, DN_K_HEADS, DN_V_PER_K, DN_HEAD_DIM)
        g = a_neg * jax.nn.softplus(a.astype(jnp.float32).reshape(b, l, 2, DN_V_HEADS) + dt_bias.astype(jnp.float32))
        g = g.reshape(b, l, 2, DN_K_HEADS, DN_V_PER_K)
        beta = jax.nn.sigmoid(bb.astype(jnp.float32)).reshape(b, l, 2, DN_K_HEADS, DN_V_PER_K)
        return q, k, v, z, g, beta

    def bidir(q, k, v, g, beta, s_fwd, s_bwd):
        o_f, sf = delta_scan(q, k, v, g[:, :, 0], beta[:, :, 0], s_fwd)
        o_b, sb = delta_scan(flip_seq(q), flip_seq(k), flip_seq(v), flip_seq(g[:, :, 1]),
                             flip_seq(beta[:, :, 1]), s_bwd)
        return o_f + flip_seq(o_b), sf, sb

    def finish(o, z):
        b, l = z.shape[:2]
        zh = z.reshape(b, l, DN_K_HEADS, DN_V_PER_K, DN_HEAD_DIM)
        y = rms_norm(o.astype(z.dtype), norm_w) * jax.nn.silu(zh)
        return y.reshape(b, l, DN_V_DIM) @ w_out

    qc, kc, vc, zc, gc, bc = project(h_ctx)
    s0 = jnp.zeros((qc.shape[0], DN_K_HEADS, DN_V_PER_K, DN_HEAD_DIM, DN_HEAD_DIM), jnp.float32)
    oc, sc_f, sc_b = bidir(qc, kc, vc, gc, bc, s0, s0)
    ql, kl, vl, zl, gl, bl = project(h_lat)
    ol, _, _ = bidir(ql, kl, vl, gl, bl, sc_f, sc_b)
    y_lat = finish(ol, zl)
    y_ctx = finish(oc, zc) if need_ctx else None
    return y_lat, y_ctx


def setup_inputs(seed: int = 0) -> dict:
    key = jax.random.key(seed)
    ks = jax.random.split(key, 26)
    f32 = jnp.float32

    def dense(k, shape, fan_in):
        return jax.random.normal(k, shape, f32) * fan_in ** -0.5

    def gain(k, shape):
        return 1.0 + 0.02 * jax.random.normal(k, shape, f32)

    def small(k, shape):
        return 0.02 * jax.random.normal(k, shape, f32)

    def dt_bias_init(k, shape):
        u = jax.random.uniform(k, shape, f32)
        dt = jnp.exp(u * (math.log(DT_MAX) - math.log(DT_MIN)) + math.log(DT_MIN))
        return dt + jnp.log(-jnp.expm1(-dt))

    def a_log_init(k, shape):
        return jnp.log(jax.random.uniform(k, shape, f32, 1.0, 16.0))

    return {
        "x": jax.random.normal(ks[0], (BATCH, SEQ, D_MODEL), f32),
        "c": jax.random.normal(ks[1], (BATCH, D_MODEL), f32),
        "ctx": jax.random.normal(ks[2], (BATCH, CTX_LEN, D_MODEL), f32),
        "c_ctx": jax.random.normal(ks[3], (D_MODEL,), f32),
        "ada_w": dense(ks[4], (DEPTH, D_MODEL, 3 * D_MODEL), D_MODEL),
        "ada_b": small(ks[5], (DEPTH, 3 * D_MODEL)),
        "pre_norm_w": gain(ks[6], (DEPTH, D_MODEL)),
        "post_norm_w": gain(ks[7], (DEPTH, D_MODEL)),
        "attn_w_in": dense(ks[8], (N_ATTN_LAYERS, D_MODEL, ATT_IN), D_MODEL),
        "attn_q_norm": gain(ks[9], (N_ATTN_LAYERS, HEAD_DIM)),
        "attn_k_norm": gain(ks[10], (N_ATTN_LAYERS, HEAD_DIM)),
        "attn_w_out": dense(ks[11], (N_ATTN_LAYERS, ATT_Q, D_MODEL), ATT_Q),
        "ssd_w_in": dense(ks[12], (N_SSD_LAYERS, D_MODEL, SSD_IN), D_MODEL),
        "ssd_conv_w": dense(ks[13], (N_SSD_LAYERS, CONV_K, SSD_CONV_DIM), CONV_K),
        "ssd_conv_b": small(ks[14], (N_SSD_LAYERS, SSD_CONV_DIM)),
        "ssd_dt_bias": dt_bias_init(ks[15], (N_SSD_LAYERS, 2, SSD_HEADS)),
        "ssd_a_log": a_log_init(ks[16], (N_SSD_LAYERS, 2, SSD_HEADS)),
        "ssd_d": gain(ks[17], (N_SSD_LAYERS, SSD_HEADS)),
        "ssd_norm_w": gain(ks[18], (N_SSD_LAYERS, SSD_D_INNER)),
        "ssd_w_out": dense(ks[19], (N_SSD_LAYERS, SSD_D_INNER, D_MODEL), SSD_D_INNER),
        "dn_w_in": dense(ks[20], (N_DN_LAYERS, D_MODEL, DN_IN), D_MODEL),
        "dn_conv_w": dense(ks[21], (N_DN_LAYERS, CONV_K, DN_CONV_DIM), CONV_K),
        "dn_dt_bias": dt_bias_init(ks[22], (N_DN_LAYERS, 2, DN_V_HEADS)),
        "dn_a_log": a_log_init(ks[23], (N_DN_LAYERS, 2, DN_V_HEADS)),
        "dn_norm_w": gain(ks[24], (N_DN_LAYERS, DN_HEAD_DIM)),
        "dn_w_out": dense(ks[25], (N_DN_LAYERS, DN_V_DIM, D_MODEL), DN_V_DIM),
    }


def reference(x, c, ctx, c_ctx, ada_w, ada_b, pre_norm_w, post_norm_w,
              attn_w_in, attn_q_norm, attn_k_norm, attn_w_out,
              ssd_w_in, ssd_conv_w, ssd_conv_b, ssd_dt_bias, ssd_a_log, ssd_d, ssd_norm_w, ssd_w_out,
              dn_w_in, dn_conv_w, dn_dt_bias, dn_a_log, dn_norm_w, dn_w_out):
    n_tok = x.shape[1]
    rows = n_tok // GRID_W
    cos, sin = axial_rope_tables(rows)
    for i in range(DEPTH):
        need_ctx = i < DEPTH - 1
        shift, scale, gate = modulation(c, ada_w[i], ada_b[i])
        shift_c, scale_c, gate_c = modulation(c_ctx, ada_w[i], ada_b[i])
        h_lat = rms_norm(x, pre_norm_w[i]) * (1.0 + scale[:, None]) + shift[:, None]
        h_ctx = rms_norm(ctx, pre_norm_w[i]) * (1.0 + scale_c) + shift_c
        kind, j = i % N_MIXERS, i // N_MIXERS
        if kind == 0:
            y_lat, y_ctx = attention_mixer(h_lat, h_ctx, attn_w_in[j], attn_q_norm[j], attn_k_norm[j],
                                           attn_w_out[j], cos, sin, need_ctx)
        elif kind == 1:
            y_lat, y_ctx = ssd_mixer(h_lat, h_ctx, ssd_w_in[j], ssd_conv_w[j], ssd_conv_b[j], ssd_dt_bias[j],
                                     ssd_a_log[j], ssd_d[j], ssd_norm_w[j], ssd_w_out[j], need_ctx)
        else:
            y_lat, y_ctx = deltanet_mixer(h_lat, h_ctx, dn_w_in[j], dn_conv_w[j], dn_dt_bias[j], dn_a_log[j],
                                          dn_norm_w[j], dn_w_out[j], need_ctx)
        x = x + gate[:, None] * rms_norm(y_lat, post_norm_w[i])
        if need_ctx:
            ctx = ctx + gate_c * rms_norm(y_ctx, post_norm_w[i])
    return x
```

```python
import numpy as np
from contextlib import ExitStack
import concourse.bass as bass
import concourse.mybir as mybir
from concourse.bass_utils import run_bass_kernel_spmd

F32 = mybir.dt.float32
BF16 = mybir.dt.bfloat16
AF = mybir.ActivationFunctionType
ALU = mybir.AluOpType
AX = mybir.AxisListType


class Res:
    __slots__ = ("name", "lw", "rd", "excl")

    def __init__(self, name=""):
        self.name = name
        self.excl = False
        self.lw = None
        self.rd = []


class Sched:
    ENGS = ("pe", "act", "dve", "pool", "sp")
    N_DMA_SEMS = 12

    def __init__(self, nc, stack):
        self.nc = nc
        self.eng = {"pe": nc.tensor, "act": nc.scalar, "dve": nc.vector,
                    "pool": nc.gpsimd, "sp": nc.sync}
        self.sem = {e: stack.enter_context(nc.semaphore("s_" + e)) for e in self.ENGS}
        self.cnt = {e: 0 for e in self.ENGS}
        self.dsem = {q: [stack.enter_context(nc.semaphore(f"d_{q}{i}")) for i in range(self.N_DMA_SEMS)]
                     for q in ("sp", "pool", "act")}
        self.dval = {q: [0] * self.N_DMA_SEMS for q in self.dsem}
        self.drr = {q: 0 for q in self.dsem}
        self.waited = {e: {} for e in self.ENGS}
        self.semobj = {}
        self.out_tokens = []
        self.nwaits = 0
        self.nops = 0

    def _wait(self, e, tok):
        key, val = tok
        w = self.waited[e]
        if w.get(key, 0) >= val:
            return
        w[key] = val
        self.eng[e].wait_ge(self.semobj[key], val)
        self.nwaits += 1

    def _deps(self, e, reads, writes):
        toks = {}
        def add(t):
            if t is None:
                return
            k, v = t
            if toks.get(k, 0) < v:
                toks[k] = v
        for r in reads:
            add(r.lw)
            if r.excl:
                for t in r.rd:
                    if t[0] != "c_" + e:
                        add(t)
        for w in writes:
            add(w.lw)
            for t in w.rd:
                add(t)
        for k, v in toks.items():
            if e == "pe" and k == "c_pe":
                continue
            self._wait(e, (k, v))

    def _commit(self, tok, reads, writes):
        for r in reads:
            r.rd.append(tok)
            if len(r.rd) > 64:
                m = {}
                for k, v in r.rd:
                    if m.get(k, 0) < v:
                        m[k] = v
                r.rd = list(m.items())
        for w in writes:
            w.lw = tok
            w.rd = []

    def op(self, e, fn, reads=(), writes=()):
        self._deps(e, reads, writes)
        ins = fn(self.eng[e])
        self.cnt[e] += 1
        key = "c_" + e
        self.semobj[key] = self.sem[e]
        ins.then_inc(self.sem[e], 1)
        tok = (key, self.cnt[e])
        self._commit(tok, reads, writes)
        self.nops += 1
        return tok

    def dma(self, q, out, in_, reads=(), writes=(), is_output=False, **kw):
        self._deps(q, reads, writes)
        i = self.drr[q]
        self.drr[q] = (i + 1) % self.N_DMA_SEMS
        key = f"d_{q}{i}"
        self.semobj[key] = self.dsem[q][i]
        if self.dval[q][i] > 0:
            self._wait(q, (key, self.dval[q][i]))
        ins = self.eng[q].dma_start(out=out, in_=in_, **kw)
        self.dval[q][i] += 16
        ins.then_inc(self.dsem[q][i], 16)
        tok = (key, self.dval[q][i])
        self._commit(tok, reads, writes)
        if is_output:
            self.out_tokens.append(tok)
        self.nops += 1
        return tok

    def barrier(self):
        toks = [("c_" + e, self.cnt[e]) for e in self.ENGS if self.cnt[e] > 0]
        for q in self.dsem:
            for i in range(self.N_DMA_SEMS):
                if self.dval[q][i] > 0:
                    toks.append((f"d_{q}{i}", self.dval[q][i]))
        for e in self.ENGS:
            for t in toks:
                if t[0] == "c_" + e:
                    continue
                self._wait(e, t)

    def finish(self):
        self.barrier()


class Tl:
    __slots__ = ("t", "res")

    def __init__(self, t):
        self.t = t
        self.res = Res()

    def __getitem__(self, k):
        return self.t[k]


class StopBuild(Exception):
    pass


class Ctx:
    def __init__(self, nc, S, stack):
        self.nc, self.S, self.stack = nc, S, stack
        self.n = 0

    def sub(self):
        st = ExitStack()
        c = Ctx2(self, st)
        self.open = getattr(self, "open", [])
        self.open.append(c)
        return c


class Ctx2:
    def __init__(self, parent, st):
        self.nc, self.S = parent.nc, parent.S
        self.st = st
        self.parent = parent

    def sb(self, shape, dt, name=None):
        self.parent.n += 1
        return Tl(self.st.enter_context(self.nc.sbuf_tensor(f"{name or 't'}_{self.parent.n}", list(shape), dt)))

    def ps(self, shape, dt=F32, name=None):
        self.parent.n += 1
        nbytes = int(np.prod(shape[1:])) * (4 if dt == F32 else 2)
        assert nbytes == 2048, ("psum tiles must be exactly one bank", shape)
        t = Tl(self.st.enter_context(self.nc.psum_tensor(f"{name or 'p'}_{self.parent.n}", list(shape), dt)))
        t.res.excl = True
        return t

    def close(self):
        self.S.barrier()
        self.st.close()
        self.parent.open.remove(self)
        import os
        self.parent.nclose = getattr(self.parent, "nclose", 0) + 1
        if int(os.environ.get("STOP", "0")) == self.parent.nclose:
            raise StopBuild()


def R_(tiles):
    return [t.res for t in tiles]


class Ops:
    def __init__(self, S):
        self.S = S

    def mm(self, out, lhsT, rhs, start=True, stop=True, rd=(), wr=()):
        return self.S.op("pe", lambda e: e.matmul(out, lhsT=lhsT, rhs=rhs, start=start, stop=stop), R_(rd), R_(wr))

    def tr(self, out, in_, ident, rd=(), wr=()):
        return self.S.op("pe", lambda e: e.transpose(out=out, in_=in_, identity=ident), R_(rd), R_(wr))

    def act(self, out, in_, func, rd=(), wr=(), **kw):
        return self.S.op("act", lambda e: e.activation(out=out, in_=in_, func=func, **kw), R_(rd), R_(wr))

    def tt(self, eng, out, in0, in1, op, rd=(), wr=()):
        return self.S.op(eng, lambda e: e.tensor_tensor(out=out, in0=in0, in1=in1, op=op), R_(rd), R_(wr))

    def ts(self, eng, out, in0, s1, s2, op0, op1=None, rd=(), wr=()):
        if op1 is None:
            return self.S.op(eng, lambda e: e.tensor_scalar(out=out, in0=in0, scalar1=s1, scalar2=None, op0=op0), R_(rd), R_(wr))
        return self.S.op(eng, lambda e: e.tensor_scalar(out=out, in0=in0, scalar1=s1, scalar2=s2, op0=op0, op1=op1), R_(rd), R_(wr))

    def stt(self, eng, out, in0, scalar, in1, op0, op1, rd=(), wr=()):
        return self.S.op(eng, lambda e: e.scalar_tensor_tensor(out=out, in0=in0, scalar=scalar, in1=in1, op0=op0, op1=op1), R_(rd), R_(wr))

    def cp(self, eng, out, in_, rd=(), wr=()):
        if eng == "act":
            return self.S.op("act", lambda e: e.copy(out=out, in_=in_), R_(rd), R_(wr))
        return self.S.op(eng, lambda e: e.tensor_copy(out=out, in_=in_), R_(rd), R_(wr))

    def recip(self, out, in_, rd=(), wr=()):
        return self.S.op("dve", lambda e: e.reciprocal(out=out, in_=in_), R_(rd), R_(wr))

    def memset(self, eng, out, val, wr=()):
        return self.S.op(eng, lambda e: e.memset(out, val), (), R_(wr))

    def dma(self, q, out, in_, rd=(), wr=(), **kw):
        return self.S.dma(q, out, in_, R_(rd), R_(wr), **kw)


class Rot:
    def __init__(self, tiles):
        self.tiles, self.i = tiles, 0

    def next(self):
        t = self.tiles[self.i % len(self.tiles)]
        self.i += 1
        return t


D = 2048
KC = D // 128
CTXL = 256
SEQL = 2048
NTOK = CTXL + SEQL
NT = NTOK // 128
EPS = 1e-6
DEPTH = 4
N_HEADS, N_KV = 16, 4
ATT_SCALE = 128 ** -0.5


def seq_blocks():
    return [(0, CTXL, True)] + [(CTXL + 512 * i, 512, False) for i in range(SEQL // 512)]


class Prog:
    def __init__(self, n_seq, layers, full_out=False):
        self.n_seq, self.layers, self.full_out = n_seq, layers, full_out
        self.nrow = n_seq + 1
        nc = self.nc = bass.Bass("TRN2", target_bir_lowering=False)
        self.inputs = {}
        self.stack = ExitStack()
        self.S = Sched(nc, self.stack)
        self.o = Ops(self.S)
        self.ctx = Ctx(nc, self.S, self.stack)
        self.g = Ctx2(self.ctx, self.stack)

    def inp(self, name, shape, dt=F32):
        t = self.nc.dram_tensor(name, list(shape), dt, kind="ExternalInput").ap()
        self.inputs[name] = t
        return t

    def scratch(self, name, shape, dt):
        return self.nc.dram_tensor(name, list(shape), dt).ap()

    def build(self):
        nc, S, o, g = self.nc, self.S, self.o, self.g
        ns = self.n_seq
        self.xin = self.inp("xin", [ns, D, NTOK])
        self.cT_d = self.inp("cT", [128, KC, self.nrow])
        wout = NTOK if self.full_out else SEQL
        self.yout = nc.dram_tensor("yout", [ns, D, wout], F32, kind="ExternalOutput").ap()
        self.xs = self.scratch("xs", [ns, D, NTOK], F32)
        self.zT = self.scratch("zT", [4096, NTOK], BF16)
        self.ident_d = self.inp("ident", [128, 128])
        self.ident = g.sb([128, 128], BF16, "ident")
        o.dma("pool", self.ident[:], self.ident_d[:, :], wr=[self.ident])
        self.identf = g.sb([128, 128], F32, "identf")
        o.dma("sp", self.identf[:], self.ident_d[:, :], wr=[self.identf])
        self.ones = g.sb([128, 128], BF16, "ones")
        o.memset("pool", self.ones[:], 1.0, wr=[self.ones])
        self.onesf = g.sb([128, 128], F32, "onesf")
        o.memset("pool", self.onesf[:], 1.0, wr=[self.onesf])
        self.eps_t = g.sb([128, 1], F32, "eps")
        o.memset("pool", self.eps_t[:], EPS, wr=[self.eps_t])
        self.m4_t = g.sb([128, 1], F32, "m4")
        o.memset("pool", self.m4_t[:], -4.0, wr=[self.m4_t])
        self.sc = g.sb([128, KC, self.nrow], F32, "sc")
        o.dma("sp", self.sc[:], self.cT_d[:, :, :], wr=[self.sc])
        o.act(self.sc[:], self.sc[:], AF.Silu, rd=[self.sc], wr=[self.sc])
        self.A = g.sb([128, KC, self.nrow], F32, "modA")
        self.SH = g.sb([128, KC, self.nrow], F32, "modS")
        self.G = g.sb([128, KC, self.nrow], F32, "modG")
        try:
            self.build_layers()
        except StopBuild:
            for c in reversed(list(self.ctx.open)):
                c.st.close()
        S.finish()
        self.stack.close()
        return nc

    def build_layers(self):
        nc, S, o, g = self.nc, self.S, self.o, self.g
        ns = self.n_seq
        first = True
        for li in self.layers:
            kind, j = li % 3, li // 3
            need_ctx = li < DEPTH - 1 or self.full_out
            last = (li == self.layers[-1])
            self.modulation(li)
            for s in range(ns):
                src = self.xin[s] if first else self.xs[s]
                if last:
                    dst = self.yout[s]
                    dst_off = 0 if self.full_out else CTXL
                else:
                    dst, dst_off = self.xs[s], 0
                if kind == 0:
                    self.attn_layer(li, j, s, src, dst, dst_off, need_ctx)
                elif kind == 1:
                    self.ssd_layer(li, j, s, src, dst, dst_off, need_ctx)
                else:
                    self.dn_layer(li, j, s, src, dst, dst_off, need_ctx)
            first = False

    def modulation(self, li):
        o, S = self.o, self.S
        nr = self.nrow
        p = self.ctx.sub()
        w_d = self.inp(f"ada_w{li}", [12, 128, 4, KC, 128])
        b_d = self.inp(f"ada_b{li}", [128, 48])
        pre_d = self.inp(f"pre_w{li}", [128, KC])
        post_d = self.inp(f"post_w{li}", [128, KC])
        bt = p.sb([128, 48], F32)
        pre = p.sb([128, KC], F32)
        post = p.sb([128, KC], F32)
        o.dma("sp", bt[:], b_d[:, :], wr=[bt])
        o.dma("sp", pre[:], pre_d[:, :], wr=[pre])
        o.dma("sp", post[:], post_d[:, :], wr=[post])
        wts = Rot([p.sb([128, 4, KC, 128], F32) for _ in range(2)])
        pm = p.ps([128, 128, 4], F32)
        for og in range(12):
            wt = wts.next()
            o.dma("sp", wt[:], w_d[og], wr=[wt])
            for jj in range(4):
                oc = og * 4 + jj
                for kc in range(KC):
                    o.mm(pm[:, oc, 0:nr], lhsT=wt[:, jj, kc, :], rhs=self.sc[:, kc, :],
                         start=(kc == 0), stop=(kc == KC - 1), rd=[wt, self.sc], wr=[pm])
        mt = p.sb([128, 48, nr], F32)
        o.tt("dve", mt[:], pm[:, 0:48, 0:nr], bt[:].unsqueeze(2).to_broadcast([128, 48, nr]), ALU.add, rd=[pm, bt], wr=[mt])
        o.cp("dve", self.SH[:], mt[:, 0:KC, :], rd=[mt], wr=[self.SH])
        o.stt("dve", self.A[:], mt[:, KC:2 * KC, :], 1.0, pre[:].unsqueeze(2).to_broadcast([128, KC, nr]),
              ALU.add, ALU.mult, rd=[mt, pre], wr=[self.A])
        o.tt("dve", self.G[:], mt[:, 2 * KC:3 * KC, :], post[:].unsqueeze(2).to_broadcast([128, KC, nr]), ALU.mult,
             rd=[mt, post], wr=[self.G])
        p.close()

    def rstd_from_ssq(self, p, pq, w, n, rstd):
        o = self.o
        o.act(rstd[:, :w], pq[:, :w], AF.Ln, rd=[pq], wr=[rstd], bias=self.eps_t[:, 0:1], scale=1.0 / n)
        o.act(rstd[:, :w], rstd[:, :w], AF.Exp, rd=[rstd], wr=[rstd], scale=-0.5)

    def phase1(self, p, s, src, hT):
        o = self.o
        xts = Rot([p.sb([128, KC, 512], F32, "xt") for _ in range(2)])
        sq = p.sb([128, KC, 512], BF16, "sq")
        pq = p.ps([128, 512], F32, "pq")
        rstd = p.sb([128, 512], F32, "rstd")
        tmps = Rot([p.sb([128, 512], F32, "tmp") for _ in range(2)])
        srcv = src.rearrange("(kc p) t -> p kc t", p=128)
        for (t0, w, is_ctx) in seq_blocks():
            row = self.n_seq if is_ctx else s
            xt = xts.next()
            o.dma("sp", xt[:, :, :w], srcv[:, :, t0:t0 + w], wr=[xt])
            o.act(sq[:, :, :w], xt[:, :, :w], AF.Square, rd=[xt], wr=[sq])
            for kc in range(KC):
                o.mm(pq[:, :w], lhsT=self.ones[:], rhs=sq[:, kc, :w], start=(kc == 0), stop=(kc == KC - 1),
                     rd=[self.ones, sq], wr=[pq])
            self.rstd_from_ssq(p, pq, w, D, rstd)
            for kc in range(KC):
                tmp = tmps.next()
                o.stt("dve", tmp[:, :w], xt[:, kc, :w], self.A[:, kc, row:row + 1], rstd[:, :w], ALU.mult, ALU.mult,
                      rd=[xt, self.A, rstd], wr=[tmp])
                o.act(hT[:, kc, t0:t0 + w], tmp[:, :w], AF.Identity, rd=[tmp, self.SH], wr=[hT],
                      bias=self.SH[:, kc, row:row + 1], scale=1.0)

    def gemm_f(self, p, hT, kcn, w_d, n_oc, epilogue, blocks=None, grp=4):
        o = self.o
        blocks = blocks or seq_blocks()
        wts = Rot([p.sb([128, grp, kcn, 128], BF16, "wt") for _ in range(2)])
        pss = Rot([p.ps([128, 512], F32, "pg") for _ in range(2)])
        for og in range(0, n_oc, grp):
            wt = wts.next()
            gn = min(grp, n_oc - og)
            o.dma("pool", wt[:, :gn], w_d[og:og + gn].rearrange("g p k j -> p g k j"), wr=[wt])
            for jj in range(gn):
                for blk in blocks:
                    t0, w, _ = blk
                    ps = pss.next()
                    for kc in range(kcn):
                        o.mm(ps[:, :w], lhsT=wt[:, jj, kc, :], rhs=hT[:, kc, t0:t0 + w], start=(kc == 0),
                             stop=(kc == kcn - 1), rd=[wt, hT], wr=[ps])
                    epilogue(og + jj, blk, ps)

    def gemm_t(self, p, hT, kcn, w_d, n_g, epilogue, gw=512):
        o = self.o
        wts = Rot([p.sb([128, kcn, gw], BF16, "wtt") for _ in range(2)])
        pss = Rot([p.ps([128, 512], F32, "pgt") for _ in range(2)])
        for gi in range(n_g):
            wt = wts.next()
            o.dma("pool", wt[:], w_d[gi], wr=[wt])
            for tt in range(NT):
                ps = pss.next()
                for kc in range(kcn):
                    o.mm(ps[:, :gw], lhsT=hT[:, kc, tt * 128:(tt + 1) * 128], rhs=wt[:, kc, :], start=(kc == 0),
                         stop=(kc == kcn - 1), rd=[wt, hT], wr=[ps])
                epilogue(gi, tt, ps)

    def phase4(self, s, src, dst, dst_off, need_ctx, kcn, w_d):
        o = self.o
        p = self.ctx.sub()
        wo = p.sb([128, kcn, D], BF16, "wo")
        for h0 in range(0, kcn, 2):
            o.dma("pool", wo[:, h0:h0 + 2, :], w_d[:, h0:h0 + 2, :], wr=[wo])
        bw = 256 if kcn <= 16 else 128
        zbs = Rot([p.sb([128, kcn, bw], BF16, "zb") for _ in range(2)])
        xbs = Rot([p.sb([128, KC, bw], F32, "xb") for _ in range(2)])
        ysb = p.sb([128, KC, bw], F32, "ysb")
        sqs = Rot([p.sb([128, bw], BF16, "sq4") for _ in range(2)])
        pys = Rot([p.ps([128, 512], F32, "py") for _ in range(3)])
        pq = p.ps([128, 512], F32, "pq4")
        rstd = p.sb([128, bw], F32, "rstd4")
        tmps = Rot([p.sb([128, bw], F32, "tmp4") for _ in range(2)])
        srcv = src.rearrange("(kc p) t -> p kc t", p=128)
        dstv = dst.rearrange("(kc p) t -> p kc t", p=128)
        zv = self.zT[0:kcn * 128, :].rearrange("(kc p) t -> p kc t", p=128)
        blocks = [(t0, bw, t0 < CTXL) for t0 in range(0, NTOK, bw)]
        import os
        nb = int(os.environ.get("P4N", "999"))
        for (t0, w, is_ctx) in blocks[:nb]:
            if is_ctx and not need_ctx:
                continue
            row = self.n_seq if is_ctx else s
            zb, xb = zbs.next(), xbs.next()
            xo = xb
            o.dma("sp", zb[:], zv[:, :, t0:t0 + w], wr=[zb])
            o.dma("sp", xb[:], srcv[:, :, t0:t0 + w], wr=[xb])
            stg_ = int(os.environ.get("P4S", "9"))
            if stg_ < 2:
                continue
            for oc in range(KC):
                py = pys.next()
                for kc in range(kcn):
                    o.mm(py[:, :w], lhsT=wo[:, kc, oc * 128:(oc + 1) * 128], rhs=zb[:, kc, :], start=(kc == 0),
                         stop=(kc == kcn - 1), rd=[wo, zb], wr=[py])
                sq = sqs.next()
                px = int(os.environ.get("P4X", "7"))
                if px & 1:
                    o.act(sq[:], py[:, :w], AF.Square, rd=[py], wr=[sq])
                if px & 2:
                    o.cp("dve", ysb[:, oc, :], py[:, :w], rd=[py], wr=[ysb])
                if px & 4:
                    o.mm(pq[:, :w], lhsT=self.ones[:], rhs=sq[:], start=(oc == 0), stop=(oc == KC - 1), rd=[self.ones, sq], wr=[pq])
            if stg_ < 3:
                continue
            self.rstd_from_ssq(p, pq, w, D, rstd)
            if stg_ < 4:
                continue
            for oc in range(KC):
                tmp = tmps.next()
                o.stt("dve", tmp[:], ysb[:, oc, :], self.G[:, oc, row:row + 1], rstd[:], ALU.mult, ALU.mult,
                      rd=[ysb, self.G, rstd], wr=[tmp])
                o.tt("pool", xo[:, oc, :], tmp[:], xb[:, oc, :], ALU.add, rd=[tmp, xb], wr=[xo])
            if stg_ < 5:
                continue
            o.dma("sp", dstv[:, :, t0 - dst_off:t0 - dst_off + w], xo[:], rd=[xo], is_output=True)
        p.close()

    def attn_consts(self):
        if hasattr(self, "cosT"):
            return
        o, g = self.o, self.g
        cos_d = self.inp("rope_cos", [128, SEQL])
        sin_d = self.inp("rope_sin", [128, SEQL])
        perm_d = self.inp("rope_perm", [128, 128])
        self.cos_d, self.sin_d, self.perm_d = cos_d, sin_d, perm_d
        self.qT_d = self.scratch("qT", [D, NTOK], BF16)
        self.kT_d = self.scratch("kT", [512, NTOK], BF16)
        self.gT_d = self.scratch("gT", [D, NTOK], BF16)
        self.V_d = self.scratch("Vd", [NTOK, 512], BF16)

    def attn_layer(self, li, j, s, src, dst, dst_off, need_ctx):
        o = self.o
        self.attn_consts()
        if s == 0:
            self.aw = dict(
                wq=self.inp(f"attn_wq{j}", [16, 128, KC, 128]),
                wk=self.inp(f"attn_wk{j}", [4, 128, KC, 128]),
                wg=self.inp(f"attn_wg{j}", [16, 128, KC, 128]),
                wv=self.inp(f"attn_wv{j}", [1, 128, KC, 512]),
                wo=self.inp(f"attn_wo{j}", [128, KC, D]),
                qn=self.inp(f"attn_qn{j}", [128, 1]),
                kn=self.inp(f"attn_kn{j}", [128, 1]),
            )
        aw = self.aw
        p = self.ctx.sub()
        hT = p.sb([128, KC, NTOK], BF16, "hT")
        p1 = self.ctx.sub()
        self.phase1(p1, s, src, hT)
        p1.close()
        self.cosT = p.sb([128, SEQL], F32, "cosT")
        self.sinT = p.sb([128, SEQL], F32, "sinT")
        self.perm = p.sb([128, 128], BF16, "perm")
        o.dma("sp", self.cosT[:], self.cos_d[:, :], wr=[self.cosT])
        o.dma("sp", self.sinT[:], self.sin_d[:, :], wr=[self.sinT])
        o.dma("pool", self.perm[:], self.perm_d[:, :], wr=[self.perm])
        qn = p.sb([128, 1], F32, "qn")
        kn = p.sb([128, 1], F32, "kn")
        o.dma("sp", qn[:], aw["qn"][:, :], wr=[qn])
        o.dma("sp", kn[:], aw["kn"][:, :], wr=[kn])
        o.ts("dve", qn[:], qn[:], ATT_SCALE, None, ALU.mult, rd=[qn], wr=[qn])
        sqs = Rot([p.sb([128, 512], BF16, "sqa") for _ in range(2)])
        pqs = Rot([p.ps([128, 512], F32, "pqa") for _ in range(2)])
        prs = Rot([p.ps([128, 512], F32, "pra") for _ in range(2)])
        rstds = Rot([p.sb([128, 512], F32, "rstda") for _ in range(2)])
        qns = Rot([p.sb([128, 512], F32, "qna") for _ in range(2)])
        qnbs = Rot([p.sb([128, 512], BF16, "qnb") for _ in range(2)])
        t1s = Rot([p.sb([128, 512], F32, "t1a") for _ in range(2)])
        stg = Rot([p.sb([128, NTOK], BF16, "stg") for _ in range(2)])
        cur = {}

        def qk_epi(dst_d, nw):
            def epi(oc, blk, ps):
                t0, w, is_ctx = blk
                if t0 == 0:
                    cur["st"] = stg.next()
                st = cur["st"]
                sq, pq, rstd = sqs.next(), pqs.next(), rstds.next()
                o.act(sq[:, :w], ps[:, :w], AF.Square, rd=[ps], wr=[sq])
                o.mm(pq[:, :w], lhsT=self.ones[:], rhs=sq[:, :w], rd=[self.ones, sq], wr=[pq])
                self.rstd_from_ssq(p, pq, w, 128, rstd)
                if is_ctx:
                    o.stt("dve", st[:, t0:t0 + w], ps[:, :w], nw[:, 0:1], rstd[:, :w], ALU.mult, ALU.mult,
                          rd=[ps, nw, rstd], wr=[st])
                else:
                    qn_, qnb, pr, t1 = qns.next(), qnbs.next(), prs.next(), t1s.next()
                    o.stt("dve", qn_[:, :w], ps[:, :w], nw[:, 0:1], rstd[:, :w], ALU.mult, ALU.mult,
                          rd=[ps, nw, rstd], wr=[qn_])
                    o.cp("act", qnb[:, :w], qn_[:, :w], rd=[qn_], wr=[qnb])
                    o.mm(pr[:, :w], lhsT=self.perm[:], rhs=qnb[:, :w], rd=[self.perm, qnb], wr=[pr])
                    l0 = t0 - CTXL
                    o.tt("pool", t1[:, :w], qn_[:, :w], self.cosT[:, l0:l0 + w], ALU.mult, rd=[qn_, self.cosT], wr=[t1])
                    o.tt("dve", qn_[:, :w], pr[:, :w], self.sinT[:, l0:l0 + w], ALU.mult, rd=[pr, self.sinT], wr=[qn_])
                    o.tt("pool", st[:, t0:t0 + w], t1[:, :w], qn_[:, :w], ALU.add, rd=[t1, qn_], wr=[st])
                if t0 + w == NTOK:
                    o.dma("sp", dst_d[oc * 128:(oc + 1) * 128, :], st[:], rd=[st])
            return epi

        def g_epi(oc, blk, ps):
            t0, w, _ = blk
            if t0 == 0:
                cur["st"] = stg.next()
            st = cur["st"]
            o.act(st[:, t0:t0 + w], ps[:, :w], AF.Silu, rd=[ps], wr=[st])
            if t0 + w == NTOK:
                o.dma("sp", self.gT_d[oc * 128:(oc + 1) * 128, :], st[:], rd=[st])

        vst = Rot([p.sb([128, 512], BF16, "vst") for _ in range(3)])

        def v_epi(gi, tt, ps):
            st = vst.next()
            o.cp("dve", st[:], ps[:], rd=[ps], wr=[st])
            o.dma("sp", self.V_d[tt * 128:(tt + 1) * 128, :], st[:], rd=[st])

        pk = self.ctx.sub()
        self.gemm_f(pk, hT, KC, aw["wk"], 4, qk_epi(self.kT_d, kn))
        self.gemm_t(pk, hT, KC, aw["wv"], 1, v_epi)
        pk.close()
        pk = self.ctx.sub()
        self.gemm_f(pk, hT, KC, aw["wq"], 16, qk_epi(self.qT_d, qn))
        pk.close()
        pk = self.ctx.sub()
        self.gemm_f(pk, hT, KC, aw["wg"], 16, g_epi)
        pk.close()
        p.close()
        p = self.ctx.sub()
        kT = p.sb([128, N_KV, NTOK], BF16, "kT")
        V = p.sb([128, NT, 512], BF16, "V")
        o.dma("sp", kT[:], self.kT_d.rearrange("(h p) t -> p h t", p=128), wr=[kT])
        o.dma("sp", V[:], self.V_d.rearrange("(c p) n -> p c n", p=128), wr=[V])
        qbs = Rot([p.sb([128, N_HEADS, 512], BF16, "qb") for _ in range(2)])
        gbs = Rot([p.sb([128, N_HEADS, 512], BF16, "gb") for _ in range(2)])
        zbs = Rot([p.sb([128, N_HEADS, 512], BF16, "zb3") for _ in range(2)])
        pss = Rot([p.ps([128, 512], F32, "ps3") for _ in range(2)])
        pos = Rot([p.ps([128, 512], F32, "po3") for _ in range(2)])
        pls = Rot([p.ps([128, 512], F32, "pl3") for _ in range(2)])
        pts = Rot([p.sb([128, 512], BF16, "pt3") for _ in range(3)])
        rss = Rot([p.sb([128, 512], F32, "rs3") for _ in range(2)])
        tos = Rot([p.sb([128, 512], F32, "to3") for _ in range(2)])
        qv = self.qT_d.rearrange("(h p) t -> p h t", p=128)
        gv = self.gT_d.rearrange("(h p) t -> p h t", p=128)
        zv = self.zT[0:D, :].rearrange("(h p) t -> p h t", p=128)
        for (t0, w, is_ctx) in seq_blocks():
            if is_ctx and not need_ctx:
                continue
            nkc = CTXL // 128 if is_ctx else NT
            qb, gb, zb = qbs.next(), gbs.next(), zbs.next()
            o.dma("sp", qb[:, :, :w], qv[:, :, t0:t0 + w], wr=[qb])
            o.dma("sp", gb[:, :, :w], gv[:, :, t0:t0 + w], wr=[gb])
            for h in range(N_HEADS):
                kvh = h // (N_HEADS // N_KV)
                po, pl = pos.next(), pls.next()
                for c in range(nkc):
                    ps, pt = pss.next(), pts.next()
                    o.mm(ps[:, :w], lhsT=kT[:, kvh, c * 128:(c + 1) * 128], rhs=qb[:, h, :w], rd=[kT, qb], wr=[ps])
                    o.act(pt[:, :w], ps[:, :w], AF.Exp, rd=[ps], wr=[pt], bias=self.m4_t[:, 0:1], scale=1.0)
                    o.mm(po[:, :w], lhsT=V[:, c, kvh * 128:(kvh + 1) * 128], rhs=pt[:, :w], start=(c == 0),
                         stop=(c == nkc - 1), rd=[V, pt], wr=[po])
                    o.mm(pl[:, :w], lhsT=self.ones[:], rhs=pt[:, :w], start=(c == 0), stop=(c == nkc - 1),
                         rd=[self.ones, pt], wr=[pl])
                rs, to = rss.next(), tos.next()
                o.recip(rs[:, :w], pl[:, :w], rd=[pl], wr=[rs])
                o.tt("dve", to[:, :w], po[:, :w], rs[:, :w], ALU.mult, rd=[po, rs], wr=[to])
                o.tt("pool", zb[:, h, :w], to[:, :w], gb[:, h, :w], ALU.mult, rd=[to, gb], wr=[zb])
            o.dma("sp", zv[:, :, t0:t0 + w], zb[:, :, :w], rd=[zb])
        p.close()
        self.phase4(s, src, dst, dst_off, need_ctx, KC, aw["wo"])


def lhsT_tiles(w):
    K, N = w.shape
    return np.ascontiguousarray(w.reshape(K // 128, 128, N // 128, 128).transpose(2, 1, 0, 3))


def rhs_tiles(w, gw=512):
    K, N = w.shape
    return np.ascontiguousarray(w.reshape(K // 128, 128, N // gw, gw).transpose(2, 1, 0, 3))


def fm_vec(v):
    return np.ascontiguousarray(v.reshape(-1, 128).T)


def rope_tables():
    rr, cc = np.meshgrid(np.arange(SEQL // 64), np.arange(64), indexing="ij")
    row = rr.reshape(-1).astype(np.float32)
    col = cc.reshape(-1).astype(np.float32)
    inv = (1.0 / (np.float32(10000.0) ** (np.arange(32, dtype=np.float32) / np.float32(32)))).astype(np.float32)
    ang = np.concatenate([row[:, None] * inv, col[:, None] * inv], axis=-1).astype(np.float32)
    cos, sin = np.cos(ang).astype(np.float32), np.sin(ang).astype(np.float32)
    cosT = np.zeros((128, SEQL), np.float32)
    sinT = np.zeros((128, SEQL), np.float32)
    perm = np.zeros((128, 128), np.float32)
    for p in range(128):
        axis, half, f = p // 64, (p // 32) % 2, p % 32
        cosT[p] = cos[:, axis * 32 + f]
        sinT[p] = sin[:, axis * 32 + f] * (-1.0 if half == 0 else 1.0)
        partner = p + 32 if half == 0 else p - 32
        perm[partner, p] = 1.0
    return cosT, sinT, perm


def prep_consts():
    cosT, sinT, perm = rope_tables()
    c = {"ident": np.eye(128, dtype=np.float32), "rope_cos": cosT, "rope_sin": sinT, "rope_perm": perm}
    k = np.arange(128)[:, None]
    i = np.arange(128)[None, :]
    c["tri_f"] = (k <= i).astype(np.float32)
    c["tri_b"] = (k >= i).astype(np.float32)
    c["negm_f"] = np.where(i >= k, 0.0, NEG).astype(np.float32)
    c["negm_b"] = np.where(i <= k, 0.0, NEG).astype(np.float32)
    c["negs_f"] = np.where(i > k, 0.0, NEG).astype(np.float32)
    c["negs_b"] = np.where(i < k, 0.0, NEG).astype(np.float32)
    c["bdmask"] = ((k // 32) == (i // 32)).astype(np.float32)
    return c


def prep_weights(inp, layers):
    w = {}
    for li in layers:
        kind, j = li % 3, li // 3
        aw = inp["ada_w"][li]
        t = lhsT_tiles(aw)
        w[f"ada_w{li}"] = np.ascontiguousarray(t.reshape(12, 4, 128, KC, 128).transpose(0, 2, 1, 3, 4))
        w[f"ada_b{li}"] = fm_vec(inp["ada_b"][li])
        w[f"pre_w{li}"] = fm_vec(inp["pre_norm_w"][li])
        w[f"post_w{li}"] = fm_vec(inp["post_norm_w"][li])
        if kind == 0:
            wi = inp["attn_w_in"][j]
            w[f"attn_wq{j}"] = lhsT_tiles(wi[:, 0:2048])
            w[f"attn_wk{j}"] = lhsT_tiles(wi[:, 2048:2560])
            w[f"attn_wv{j}"] = rhs_tiles(wi[:, 2560:3072])
            w[f"attn_wg{j}"] = lhsT_tiles(wi[:, 3072:5120])
            wo = inp["attn_w_out"][j]
            w[f"attn_wo{j}"] = np.ascontiguousarray(wo.reshape(KC, 128, D).transpose(1, 0, 2))
            w[f"attn_qn{j}"] = np.ascontiguousarray(inp["attn_q_norm"][j].reshape(128, 1))
            w[f"attn_kn{j}"] = np.ascontiguousarray(inp["attn_k_norm"][j].reshape(128, 1))
        elif kind == 1:
            wi = inp["ssd_w_in"][j]
            w[f"ssd_wz{j}"] = rhs_tiles(wi[:, 0:4096])
            w[f"ssd_wx{j}"] = lhsT_tiles(wi[:, 4096:10240])
            w[f"ssd_wdt{j}"] = rhs_tiles(wi[:, 10240:10368], gw=128)
            cw = inp["ssd_conv_w"][j]
            w[f"ssd_cw{j}"] = np.ascontiguousarray(cw.reshape(5, 48, 128).transpose(2, 1, 0))
            w[f"ssd_cb{j}"] = fm_vec(inp["ssd_conv_b"][j])
            w[f"ssd_dtb{j}"] = np.ascontiguousarray(inp["ssd_dt_bias"][j].reshape(128))
            w[f"ssd_alog{j}"] = np.ascontiguousarray(inp["ssd_a_log"][j].reshape(128))
            w[f"ssd_d{j}"] = np.ascontiguousarray(inp["ssd_d"][j])
            w[f"ssd_nw{j}"] = np.ascontiguousarray(inp["ssd_norm_w"][j])
            w[f"ssd_wo{j}"] = np.ascontiguousarray(inp["ssd_w_out"][j].reshape(32, 128, D).transpose(1, 0, 2))
        else:
            wi = inp["dn_w_in"][j]
            w[f"dn_wx{j}"] = lhsT_tiles(wi[:, 0:8192])
            w[f"dn_wz{j}"] = rhs_tiles(wi[:, 8192:12288])
            w[f"dn_wab{j}"] = rhs_tiles(wi[:, 12288:12416], gw=128)
            cw = inp["dn_conv_w"][j]
            w[f"dn_cw{j}"] = np.ascontiguousarray(cw.reshape(5, 64, 128).transpose(2, 1, 0))
            w[f"dn_dtb{j}"] = np.ascontiguousarray(inp["dn_dt_bias"][j].reshape(64))
            w[f"dn_alog{j}"] = np.ascontiguousarray(inp["dn_a_log"][j].reshape(64))
            w[f"dn_nw{j}"] = np.ascontiguousarray(inp["dn_norm_w"][j])
            w[f"dn_wo{j}"] = np.ascontiguousarray(inp["dn_w_out"][j].reshape(32, 128, D).transpose(1, 0, 2))
    return w


def prep_seq(x, ctx, c, c_ctx):
    ns = x.shape[0]
    xin = np.empty((ns, D, NTOK), np.float32)
    xin[:, :, :CTXL] = ctx.transpose(0, 2, 1)
    xin[:, :, CTXL:] = x.transpose(0, 2, 1)
    rows = np.concatenate([c, c_ctx[None]], axis=0)
    cT = np.ascontiguousarray(rows.reshape(ns + 1, KC, 128).transpose(2, 1, 0))
    return xin, cT


_PROG_CACHE = {}


def run_prog(inp_np, per_core_seq, n_seq, layers, full_out, n_cores):
    prog = Prog(n_seq, layers, full_out)
    nc = prog.build()
    shared = dict(prep_consts())
    shared.update(prep_weights(inp_np, layers))
    in_maps = []
    for cidx in range(n_cores):
        m = {k: v for k, v in shared.items() if k in prog.inputs}
        xin, cT = per_core_seq[cidx]
        m["xin"], m["cT"] = xin, cT
        missing = set(prog.inputs) - set(m)
        assert not missing, missing
        in_maps.append(m)
    res = run_bass_kernel_spmd(nc, in_maps, core_ids=list(range(n_cores)))
    return [r["yout"] for r in res.results], res


def kernel(**inputs):
    inp = {k: np.asarray(v) for k, v in inputs.items()}
    n_cores, ns = 8, 2
    per_core = []
    for cidx in range(n_cores):
        sl = slice(cidx * ns, (cidx + 1) * ns)
        per_core.append(prep_seq(inp["x"][sl], inp["ctx"][sl], inp["c"][sl], inp["c_ctx"]))
    outs, _ = run_prog(inp, per_core, ns, list(range(DEPTH)), False, n_cores)
    y = np.concatenate([o_.transpose(0, 2, 1) for o_ in outs], axis=0)
    bad = [int((~np.isfinite(y[b])).sum()) for b in range(y.shape[0])]
    if any(bad):
        print("KERNEL non-finite counts per batch element:", bad, flush=True)
    return np.ascontiguousarray(y.astype(np.float32))


SSD_DI, SSD_H, SSD_G, SSD_N, SSD_P = 4096, 64, 8, 128, 64
NEG = -30000.0


def scan_consts(self):
    if hasattr(self, "tri"):
        return
    o, g = self.o, self.g
    self.tri, self.negm, self.negs = {}, {}, {}
    bd_d = self.inp("bdmask", [128, 128])
    self.bd = g.sb([128, 128], F32, "bdmask")
    o.dma("sp", self.bd[:], bd_d[:, :], wr=[self.bd])
    for d_ in ("f", "b"):
        for nm, store in (("tri", self.tri), ("negm", self.negm), ("negs", self.negs)):
            dd = self.inp(f"{nm}_{d_}", [128, 128])
            t = g.sb([128, 128], F32, f"{nm}{d_}")
            o.dma("sp", t[:], dd[:, :], wr=[t])
            store[d_] = t


def ssd_layer(self, li, j, s, src, dst, dst_off, need_ctx):
    o = self.o
    scan_consts(self)
    if s == 0:
        self.sw = dict(
            wz=self.inp(f"ssd_wz{j}", [8, 128, KC, 512]),
            wx=self.inp(f"ssd_wx{j}", [48, 128, KC, 128]),
            wdt=self.inp(f"ssd_wdt{j}", [1, 128, KC, 128]),
            cw=self.inp(f"ssd_cw{j}", [128, 48, 5]),
            cb=self.inp(f"ssd_cb{j}", [128, 48]),
            dtb=self.inp(f"ssd_dtb{j}", [128]),
            alog=self.inp(f"ssd_alog{j}", [128]),
            dsk=self.inp(f"ssd_d{j}", [64]),
            nw=self.inp(f"ssd_nw{j}", [SSD_DI]),
            wo=self.inp(f"ssd_wo{j}", [128, 32, D]),
        )
        self.sz_d = self.scratch("ssd_sz", [NTOK, SSD_DI], BF16)
        self.x_d = self.scratch("ssd_x", [NTOK, SSD_DI], BF16)
        self.B_d = self.scratch("ssd_B", [NTOK, 1024], BF16)
        self.BT_d = self.scratch("ssd_BT", [1024, NTOK], BF16)
        self.CT_d = self.scratch("ssd_CT", [1024, NTOK], BF16)
        self.dt_d = self.scratch("ssd_dt", [NTOK, 128], F32)
        self.yf_d = self.scratch("ssd_yf", [NTOK, SSD_DI], F32)
    sw = self.sw
    p = self.ctx.sub()
    hT = p.sb([128, KC, NTOK], BF16, "hT")
    p1 = self.ctx.sub()
    self.phase1(p1, s, src, hT)
    p1.close()
    pk = self.ctx.sub()
    zst = Rot([pk.sb([128, 512], BF16, "zst") for _ in range(3)])

    def z_epi(gi, tt, ps):
        st = zst.next()
        o.act(st[:], ps[:], AF.Silu, rd=[ps], wr=[st])
        o.dma("sp", self.sz_d[tt * 128:(tt + 1) * 128, gi * 512:(gi + 1) * 512], st[:], rd=[st])

    self.gemm_t(pk, hT, KC, sw["wz"], 8, z_epi)
    pk.close()
    pk = self.ctx.sub()
    dtb = pk.sb([128, 128], F32, "dtb")
    o.dma("sp", dtb[:], sw["dtb"].partition_broadcast(128), wr=[dtb])
    dst_ = Rot([pk.sb([128, 128], F32, "dtst") for _ in range(3)])

    def dt_epi(gi, tt, ps):
        st = dst_.next()
        o.tt("dve", st[:], ps[:, :128], dtb[:], ALU.add, rd=[ps, dtb], wr=[st])
        o.act(st[:], st[:], AF.Exp, rd=[st], wr=[st])
        o.act(st[:], st[:], AF.Ln, rd=[st], wr=[st], bias=1.0, scale=1.0)
        o.dma("sp", self.dt_d[tt * 128:(tt + 1) * 128, :], st[:], rd=[st])

    self.gemm_t(pk, hT, KC, sw["wdt"], 1, dt_epi, gw=128)
    pk.close()
    pk = self.ctx.sub()
    cw = pk.sb([128, 48, 5], F32, "cw")
    cb = pk.sb([128, 48], F32, "cb")
    o.dma("sp", cw[:], sw["cw"][:, :, :], wr=[cw])
    o.dma("sp", cb[:], sw["cb"][:, :], wr=[cb])
    self.conv_gemm(pk, hT, sw["wx"], 48, cw, cb,
                   tok_dst=lambda oc: (self.x_d, oc * 128) if oc < 32 else ((self.B_d, (oc - 32) * 128) if oc < 40 else None),
                   fm_dst=lambda oc: (self.BT_d, (oc - 32) * 128) if 32 <= oc < 40 else ((self.CT_d, (oc - 40) * 128) if oc >= 40 else None))
    pk.close()
    p.close()
    ssd_scan_phase(self, s)
    self.phase4(s, src, dst, dst_off, need_ctx, 32, sw["wo"])


def conv_gemm(self, pk, hT, w_d, n_oc, cw, cb, tok_dst, fm_dst, post=None):
    o = self.o
    segs = [(0, CTXL), (CTXL, SEQL)]
    bufs = Rot([[pk.sb([128, n + 4], F32, "cvb") for (_, n) in segs] for _ in range(2)])
    for pair in bufs.tiles:
        for t in pair:
            o.memset("pool", t[:], 0.0, wr=[t])
    accs = Rot([pk.sb([128, NTOK], F32, "acc") for _ in range(2)])
    rows = Rot([pk.sb([128, NTOK], BF16, "rowb") for _ in range(2)])
    ptr = Rot([pk.ps([128, 8, 128], BF16, "ptr") for _ in range(2)])
    tst = Rot([pk.sb([128, NT, 128], BF16, "tst") for _ in range(2)])
    cur = {}

    def epi(oc, blk, ps):
        t0, w, is_ctx = blk
        if t0 == 0:
            cur["buf"] = bufs.next()
        bc, bl = cur["buf"]
        if is_ctx:
            o.cp("act", bc[:, 2:2 + w], ps[:, :w], rd=[ps], wr=[bc])
        else:
            l0 = t0 - CTXL
            o.cp("act", bl[:, 2 + l0:2 + l0 + w], ps[:, :w], rd=[ps], wr=[bl])
        if t0 + w != NTOK:
            return
        acc, row = accs.next(), rows.next()
        for (sb_, (s0, n), eng) in ((bc, segs[0], "dve"), (bl, segs[1], "dve")):
            o.ts(eng, acc[:, s0:s0 + n], sb_[:, 0:n], cw[:, oc, 0:1], cb[:, oc:oc + 1], ALU.mult, ALU.add,
                 rd=[sb_, cw, cb], wr=[acc])
            for k in range(1, 5):
                o.stt(eng, acc[:, s0:s0 + n], sb_[:, k:k + n], cw[:, oc, k:k + 1], acc[:, s0:s0 + n], ALU.mult, ALU.add,
                      rd=[sb_, cw, acc], wr=[acc])
        if post is None:
            o.act(row[:], acc[:], AF.Silu, rd=[acc], wr=[row])
        else:
            o.act(acc[:], acc[:], AF.Silu, rd=[acc], wr=[acc])
            post(oc, acc, row)
        fd = fm_dst(oc)
        if fd is not None:
            o.dma("sp", fd[0][fd[1]:fd[1] + 128, :], row[:], rd=[row])
        td = tok_dst(oc)
        if td is not None:
            st = tst.next()
            for t8 in range(0, NT, 8):
                pt = ptr.next()
                n8 = min(8, NT - t8)
                for q in range(n8):
                    tt = t8 + q
                    o.tr(pt[:, q, :], row[:, tt * 128:(tt + 1) * 128], self.ident[:], rd=[row, self.ident], wr=[pt])
                o.cp("dve", st[:, t8:t8 + n8, :], pt[:, :n8, :], rd=[pt], wr=[st])
            o.dma("sp", td[0].rearrange("(c p) n -> p c n", p=128)[:, :, td[1]:td[1] + 128], st[:], rd=[st])

    self.gemm_f(pk, hT, KC, w_d, n_oc, epi)


Prog.ssd_layer = ssd_layer
Prog.conv_gemm = conv_gemm


def chunk_order(direction):
    if direction == "f":
        return list(range(NT))
    nc_ = CTXL // 128
    return list(range(nc_ - 1, -1, -1)) + list(range(NT - 1, nc_ - 1, -1))


def ssd_scan_phase(self, s):
    o = self.o
    sw = self.sw
    p = self.ctx.sub()
    H, G, E, P = SSD_H, SSD_G, 8, SSD_P
    aneg = p.sb([128, 128], F32, "aneg")
    o.dma("sp", aneg[:], sw["alog"].partition_broadcast(128), wr=[aneg])
    o.act(aneg[:], aneg[:], AF.Exp, rd=[aneg], wr=[aneg])
    o.ts("dve", aneg[:], aneg[:], -1.0, None, ALU.mult, rd=[aneg], wr=[aneg])
    dsk = p.sb([128, H], F32, "dsk")
    o.dma("sp", dsk[:], sw["dsk"].partition_broadcast(128), wr=[dsk])
    nwb = p.sb([128, SSD_DI], F32, "nwb")
    o.dma("sp", nwb[:], sw["nw"].partition_broadcast(128), wr=[nwb])
    hst = [p.sb([128, E * P], F32, f"hst{g_}") for g_ in range(G)]
    hbf = [p.sb([128, E * P], BF16, f"hbf{g_}") for g_ in range(G)]
    xcs = Rot([p.sb([128, H, P], BF16, "xc") for _ in range(2)])
    dts = Rot([p.sb([128, 128], F32, "dtc") for _ in range(2)])
    bts = Rot([p.sb([128, G, 128], BF16, "btc") for _ in range(2)])
    cts = Rot([p.sb([128, G, 128], BF16, "ctc") for _ in range(2)])
    bks = Rot([p.sb([128, G * 128], BF16, "bkc") for _ in range(2)])
    xdt = p.sb([128, H, P], BF16, "xdt")
    xw = p.sb([128, H, P], BF16, "xw")
    da = p.sb([128, H], F32, "da")
    acum = p.sb([128, H], F32, "acum")
    nacum = p.sb([128, H], F32, "nacum")
    eA = p.sb([128, H], F32, "eA")
    wdec = p.sb([128, H], F32, "wdec")
    etot = p.sb([128, H], F32, "etot")
    ych = Rot([p.sb([128, H, P], F32, "ych") for _ in range(2)])
    cbs = Rot([p.sb([128, 128], F32, "cbs") for _ in range(2)])
    Es = Rot([p.sb([128, 128], F32, "E") for _ in range(3)])
    LTs = Rot([p.sb([128, 128], BF16, "LT") for _ in range(3)])
    tmps = Rot([p.sb([128, E, P], F32, "tmpy") for _ in range(2)])
    yf = p.sb([128, H, P], F32, "yf")
    szc = p.sb([128, SSD_DI], BF16, "szc")
    un = p.sb([128, SSD_DI], BF16, "un")
    ssq = p.sb([128, 1], F32, "ssq")
    rstd = p.sb([128, 1], F32, "rstd1")
    junk = p.sb([128, SSD_DI], BF16, "junk")
    zst = Rot([p.sb([128, 32, 128], BF16, "zst3") for _ in range(2)])
    pmisc = p.ps([128, 512], F32, "pmisc")
    pcb = p.ps([128, 512], F32, "pcb")
    pR = Rot([p.ps([128, 4, 128], F32, "pR") for _ in range(2)])
    pys = Rot([p.ps([128, 512], F32, "pyi") for _ in range(2)])
    pYg = p.ps([128, 512], F32, "pYg")
    pHn = p.ps([128, 512], F32, "pHn")
    xv = self.x_d.rearrange("(c p) (h q) -> c p h q", p=128, q=P)
    dtv = self.dt_d.rearrange("(c p) n -> c p n", p=128)
    btv = self.BT_d.rearrange("(g n) t -> n g t", n=128)
    ctv = self.CT_d.rearrange("(g n) t -> n g t", n=128)
    bkv = self.B_d.rearrange("(c p) n -> c p n", p=128)
    yfv = self.yf_d.rearrange("(c p) (h q) -> c p h q", p=128, q=P)
    szv = self.sz_d.rearrange("(c p) n -> c p n", p=128)
    zv = self.zT.rearrange("(c p) t -> p c t", p=128)
    for di, d_ in enumerate(("f", "b")):
        tri, negm = self.tri[d_], self.negm[d_]
        self.S.barrier()
        for g_ in range(G):
            o.memset("pool", hst[g_][:], 0.0, wr=[hst[g_]])
            o.memset("pool", hbf[g_][:], 0.0, wr=[hbf[g_]])
        for c in chunk_order(d_):
            xc, dtc, btc, ctc, bkc, yc = xcs.next(), dts.next(), bts.next(), cts.next(), bks.next(), ych.next()
            tsl = slice(c * 128, (c + 1) * 128)
            o.dma("sp", xc[:], xv[c], wr=[xc])
            o.dma("sp", dtc[:], dtv[c], wr=[dtc])
            o.dma("sp", btc[:], btv[:, :, tsl], wr=[btc])
            o.dma("sp", ctc[:], ctv[:, :, tsl], wr=[ctc])
            o.dma("sp", bkc[:], bkv[c], wr=[bkc])
            dtd = dtc[:, di * H:(di + 1) * H]
            o.tt("dve", da[:], dtd, aneg[:, di * H:(di + 1) * H], ALU.mult, rd=[dtc, aneg], wr=[da])
            o.mm(pmisc[:, 0:H], lhsT=tri[:], rhs=da[:], rd=[tri, da], wr=[pmisc])
            o.mm(pmisc[:, H:2 * H], lhsT=self.onesf[:], rhs=da[:], rd=[self.onesf, da], wr=[pmisc])
            o.cp("dve", acum[:], pmisc[:, 0:H], rd=[pmisc], wr=[acum])
            o.ts("dve", nacum[:], pmisc[:, 0:H], -1.0, None, ALU.mult, rd=[pmisc], wr=[nacum])
            o.tt("dve", wdec[:], pmisc[:, H:2 * H], acum[:], ALU.subtract, rd=[pmisc, acum], wr=[wdec])
            o.act(etot[:], pmisc[:, H:2 * H], AF.Exp, rd=[pmisc], wr=[etot])
            o.act(eA[:], acum[:], AF.Exp, rd=[acum], wr=[eA])
            o.act(wdec[:], wdec[:], AF.Exp, rd=[wdec], wr=[wdec])
            o.tt("dve", wdec[:], wdec[:], dtd, ALU.mult, rd=[wdec, dtc], wr=[wdec])
            o.tt("dve", xdt[:], xc[:], dtd.unsqueeze(2).to_broadcast([128, H, P]), ALU.mult, rd=[xc, dtc], wr=[xdt])
            o.tt("pool", xw[:], xc[:], wdec[:].unsqueeze(2).to_broadcast([128, H, P]), ALU.mult, rd=[xc, wdec], wr=[xw])
            for g_ in range(G):
                cb_ = cbs.next()
                o.mm(pcb[:, 0:128], lhsT=btc[:, g_, :], rhs=ctc[:, g_, :], rd=[btc, ctc], wr=[pcb])
                o.cp("act", cb_[:], pcb[:, 0:128], rd=[pcb], wr=[cb_])
                py = pys.next()
                for e4 in range(0, E, 4):
                    pr = pR.next()
                    for q in range(4):
                        h = g_ * E + e4 + q
                        o.mm(pr[:, q, :], lhsT=da[:, h:h + 1].to_broadcast([128, 128]), rhs=tri[:], start=True, stop=False,
                             rd=[da, tri], wr=[pr])
                        o.mm(pr[:, q, :], lhsT=self.identf[:], rhs=negm[:], start=False, stop=True,
                             rd=[self.identf, negm], wr=[pr])
                    for q in range(4):
                        h = g_ * E + e4 + q
                        E_, LT = Es.next(), LTs.next()
                        o.act(E_[:], pr[:, q, :], AF.Exp, rd=[pr, nacum], wr=[E_], bias=nacum[:, h:h + 1], scale=1.0)
                        o.tt("dve", LT[:], E_[:], cb_[:], ALU.mult, rd=[E_, cb_], wr=[LT])
                        o.mm(py[:, (e4 + q) * P:(e4 + q + 1) * P], lhsT=LT[:], rhs=xdt[:, h, :], rd=[LT, xdt], wr=[py])
                o.mm(pYg[:], lhsT=ctc[:, g_, :], rhs=hbf[g_][:], rd=[ctc, hbf[g_]], wr=[pYg])
                tmp = tmps.next()
                o.tt("dve", tmp[:], pYg[:].rearrange("p (e q) -> p e q", q=P),
                     eA[:, g_ * E:(g_ + 1) * E].unsqueeze(2).to_broadcast([128, E, P]), ALU.mult, rd=[pYg, eA], wr=[tmp])
                o.tt("dve", yc[:, g_ * E:(g_ + 1) * E, :], tmp[:], py[:].rearrange("p (e q) -> p e q", q=P), ALU.add,
                     rd=[tmp, py], wr=[yc])
                o.mm(pHn[:], lhsT=bkc[:, g_ * 128:(g_ + 1) * 128], rhs=xw[:, g_ * E:(g_ + 1) * E, :].rearrange("p e q -> p (e q)"),
                     rd=[bkc, xw], wr=[pHn])
                o.tt("pool", hst[g_][:].rearrange("p (e q) -> p e q", q=P), hst[g_][:].rearrange("p (e q) -> p e q", q=P),
                     etot[:, g_ * E:(g_ + 1) * E].unsqueeze(2).to_broadcast([128, E, P]), ALU.mult, rd=[hst[g_], etot], wr=[hst[g_]])
                o.tt("dve", hst[g_][:], hst[g_][:], pHn[:], ALU.add, rd=[hst[g_], pHn], wr=[hst[g_]])
                o.cp("act", hbf[g_][:], hst[g_][:], rd=[hst[g_]], wr=[hbf[g_]])
            if d_ == "f":
                o.dma("sp", yfv[c], yc[:], rd=[yc])
                continue
            o.dma("sp", yf[:], yfv[c], wr=[yf])
            o.dma("sp", szc[:], szv[c], wr=[szc])
            o.tt("pool", yc[:], yc[:], yf[:], ALU.add, rd=[yc, yf], wr=[yc])
            o.tt("pool", yf[:], xc[:], dsk[:].unsqueeze(2).to_broadcast([128, H, P]), ALU.mult, rd=[xc, dsk], wr=[yf])
            o.tt("pool", yc[:], yc[:], yf[:], ALU.add, rd=[yc, yf], wr=[yc])
            ycf = yc[:].rearrange("p h q -> p (h q)")
            o.tt("dve", ycf, ycf, szc[:], ALU.mult, rd=[yc, szc], wr=[yc])
            o.act(junk[:], ycf, AF.Square, rd=[yc], wr=[junk, ssq], accum_out=ssq[:, 0:1])
            o.act(rstd[:], ssq[:], AF.Ln, rd=[ssq], wr=[rstd], bias=self.eps_t[:, 0:1], scale=1.0 / SSD_DI)
            o.act(rstd[:], rstd[:], AF.Exp, rd=[rstd], wr=[rstd], scale=-0.5)
            o.stt("dve", un[:], ycf, rstd[:, 0:1], nwb[:], ALU.mult, ALU.mult, rd=[yc, rstd, nwb], wr=[un])
            st = zst.next()
            for c8 in range(0, 32, 8):
                pt = pys.next()
                ptb = pt[:].bitcast(BF16).rearrange("p (a b) -> p a b", b=128)
                for q in range(8):
                    o.tr(ptb[:, q, :], un[:, (c8 + q) * 128:(c8 + q + 1) * 128], self.ident[:], rd=[un, self.ident], wr=[pt])
                o.cp("dve", st[:, c8:c8 + 8, :], ptb[:, 0:8, :], rd=[pt], wr=[st])
            o.dma("sp", zv[:, :, tsl], st[:], rd=[st])
    p.close()


DN_HK, DN_HV, DN_DV = 16, 32, 4096


def dn_layer(self, li, j, s, src, dst, dst_off, need_ctx):
    o = self.o
    scan_consts(self)
    if s == 0:
        self.dw = dict(
            wz=self.inp(f"dn_wz{j}", [8, 128, KC, 512]),
            wx=self.inp(f"dn_wx{j}", [64, 128, KC, 128]),
            wab=self.inp(f"dn_wab{j}", [1, 128, KC, 128]),
            cw=self.inp(f"dn_cw{j}", [128, 64, 5]),
            dtb=self.inp(f"dn_dtb{j}", [64]),
            alog=self.inp(f"dn_alog{j}", [64]),
            nw=self.inp(f"dn_nw{j}", [128]),
            wo=self.inp(f"dn_wo{j}", [128, 32, D]),
        )
        self.dsz_d = self.scratch("dn_sz", [NTOK, DN_DV], BF16)
        self.dqT_d = self.scratch("dn_qT", [2048, NTOK], BF16)
        self.dkT_d = self.scratch("dn_kT", [2048, NTOK], BF16)
        self.dk_d = self.scratch("dn_k", [NTOK, 2048], BF16)
        self.dv_d = self.scratch("dn_v", [NTOK, DN_DV], BF16)
        self.dgb_d = self.scratch("dn_gb", [NTOK, 128], F32)
        self.dof_d = self.scratch("dn_of", [NTOK, DN_DV], F32)
    dw = self.dw
    p = self.ctx.sub()
    hT = p.sb([128, KC, NTOK], BF16, "hT")
    p1 = self.ctx.sub()
    self.phase1(p1, s, src, hT)
    p1.close()
    pk = self.ctx.sub()
    zst = Rot([pk.sb([128, 512], BF16, "zst") for _ in range(3)])

    def z_epi(gi, tt, ps):
        st = zst.next()
        o.act(st[:], ps[:], AF.Silu, rd=[ps], wr=[st])
        o.dma("sp", self.dsz_d[tt * 128:(tt + 1) * 128, gi * 512:(gi + 1) * 512], st[:], rd=[st])

    self.gemm_t(pk, hT, KC, dw["wz"], 8, z_epi)
    pk.close()
    pk = self.ctx.sub()
    dtb = pk.sb([128, 64], F32, "dtb")
    aneg = pk.sb([128, 64], F32, "aneg")
    o.dma("sp", dtb[:], dw["dtb"].partition_broadcast(128), wr=[dtb])
    o.dma("sp", aneg[:], dw["alog"].partition_broadcast(128), wr=[aneg])
    o.act(aneg[:], aneg[:], AF.Exp, rd=[aneg], wr=[aneg])
    o.ts("dve", aneg[:], aneg[:], -1.0, None, ALU.mult, rd=[aneg], wr=[aneg])
    gst = Rot([pk.sb([128, 128], F32, "gst") for _ in range(3)])

    def ab_epi(gi, tt, ps):
        st = gst.next()
        o.tt("dve", st[:, 0:64], ps[:, 0:64], dtb[:], ALU.add, rd=[ps, dtb], wr=[st])
        o.act(st[:, 0:64], st[:, 0:64], AF.Exp, rd=[st], wr=[st])
        o.act(st[:, 0:64], st[:, 0:64], AF.Ln, rd=[st], wr=[st], bias=1.0, scale=1.0)
        o.tt("dve", st[:, 0:64], st[:, 0:64], aneg[:], ALU.mult, rd=[st, aneg], wr=[st])
        o.act(st[:, 64:128], ps[:, 64:128], AF.Sigmoid, rd=[ps], wr=[st])
        o.dma("sp", self.dgb_d[tt * 128:(tt + 1) * 128, :], st[:], rd=[st])

    self.gemm_t(pk, hT, KC, dw["wab"], 1, ab_epi, gw=128)
    pk.close()
    pk = self.ctx.sub()
    cw = pk.sb([128, 64, 5], F32, "cw")
    cb = pk.sb([128, 64], F32, "cb")
    o.dma("sp", cw[:], dw["cw"][:, :, :], wr=[cw])
    o.memset("pool", cb[:], 0.0, wr=[cb])
    sqr = pk.sb([128, NTOK], BF16, "sqr")
    rst = pk.sb([128, NTOK], F32, "rst")
    pl2 = Rot([pk.ps([128, 512], F32, "pl2") for _ in range(2)])

    def post(oc, acc, row):
        if oc >= 32:
            o.cp("act", row[:], acc[:], rd=[acc], wr=[row])
            return
        o.act(sqr[:], acc[:], AF.Square, rd=[acc], wr=[sqr])
        for t0 in range(0, NTOK, 512):
            w = min(512, NTOK - t0)
            ps = pl2.next()
            o.mm(ps[:, :w], lhsT=self.ones[:], rhs=sqr[:, t0:t0 + w], rd=[self.ones, sqr], wr=[ps])
            o.act(rst[:, t0:t0 + w], ps[:, :w], AF.Ln, rd=[ps], wr=[rst], bias=self.eps_t[:, 0:1], scale=1.0)
        o.act(rst[:], rst[:], AF.Exp, rd=[rst], wr=[rst], scale=-0.5)
        sc = (128 ** -0.5) if oc < 16 else 1.0
        o.stt("dve", row[:], acc[:], sc, rst[:], ALU.mult, ALU.mult, rd=[acc, rst], wr=[row])

    self.conv_gemm(pk, hT, dw["wx"], 64, cw, cb,
                   tok_dst=lambda oc: None if oc < 16 else ((self.dk_d, (oc - 16) * 128) if oc < 32 else (self.dv_d, (oc - 32) * 128)),
                   fm_dst=lambda oc: (self.dqT_d, oc * 128) if oc < 16 else ((self.dkT_d, (oc - 16) * 128) if oc < 32 else None),
                   post=post)
    pk.close()
    p.close()
    dn_scan_phase(self, s)
    self.phase4(s, src, dst, dst_off, need_ctx, 32, dw["wo"])


Prog.dn_layer = dn_layer


def dn_scan_phase(self, s):
    o = self.o
    dw = self.dw
    p = self.ctx.sub()
    HV, HK = DN_HV, DN_HK
    nwb = p.sb([128, 128], F32, "dnw")
    o.dma("sp", nwb[:], dw["nw"].partition_broadcast(128), wr=[nwb])
    Sf = [p.sb([128, 128], F32, f"Sf{h}") for h in range(HV)]
    Sb = [p.sb([128, 128], BF16, f"Sb{h}") for h in range(HV)]
    gbs = Rot([p.sb([128, 128], F32, "gbc") for _ in range(2)])
    kTs = Rot([p.sb([128, HK, 128], BF16, "kTc") for _ in range(2)])
    qTs = Rot([p.sb([128, HK, 128], BF16, "qTc") for _ in range(2)])
    kts = Rot([p.sb([128, HK, 128], BF16, "ktk") for _ in range(2)])
    vcs = Rot([p.sb([128, HV, 128], BF16, "vch") for _ in range(2)])
    ochs = Rot([p.sb([128, HV, 128], F32, "och") for _ in range(2)])
    sm = {n: p.sb([128, HV], F32, "sm_" + n) for n in ("gc", "ngc", "gcum", "ngcum", "eg", "egl", "etot", "nbeta", "bg", "beta")}
    of = p.sb([128, HV, 128], F32, "of")
    szc = p.sb([128, HV, 128], BF16, "szc")
    ssq = p.sb([128, HV], F32, "ssqd")
    rstd = p.sb([128, HV], F32, "rstdd")
    un = p.sb([128, HV, 128], BF16, "und")
    junk = un
    zst = Rot([p.sb([128, 32, 128], BF16, "zstd") for _ in range(2)])

    class Slot:
        pass

    slots = []
    for w_ in range(2):
        T = Slot()
        T.psq = p.ps([128, 2, 2, 128], F32, "psq")
        T.pch = p.ps([128, 4, 128], F32, "pch")
        T.pmx = p.ps([128, 2, 2, 128], F32, "pmx")
        T.pst = p.ps([128, 2, 2, 128], F32, "pst")
        T.kq = p.sb([128, 2, 128], F32, "kq")
        T.EL = p.sb([128, 2, 128], F32, "EL")
        T.EAT = p.sb([128, 2, 128], F32, "EAT")
        T.P = [p.sb([128, 2, 2, 128], F32, "Pm") for _ in range(2)]
        T.TTf = p.sb([128, 2, 128], F32, "TTf")
        T.TT = p.sb([128, 2, 128], F32, "TT")
        T.AT = p.sb([128, 2, 128], BF16, "AT")
        T.N = p.sb([128, 2, 128], F32, "Nn")
        T.NO = p.sb([128, 2, 128], F32, "NO")
        T.X0 = p.sb([128, 2, 128], F32, "X0")
        T.MM = p.sb([128, 2, 2, 128], F32, "MM")
        T.IMT = p.sb([128, 2, 128], F32, "IMT")
        T.M2 = p.sb([128, 2, 128], F32, "M2")
        T.W = p.sb([128, 2, 128], F32, "Wm")
        T.vb = p.sb([128, 2, 128], F32, "vb")
        T.kbg = p.sb([128, 2, 128], F32, "kbg")
        T.kdec = p.sb([128, 2, 128], BF16, "kdec")
        T.us = p.sb([128, 2, 128], F32, "us")
        T.wT = p.sb([128, 2, 128], BF16, "wTb")
        T.vn = p.sb([128, 2, 128], BF16, "vn")
        T.o1 = p.sb([128, 2, 128], F32, "o1")
        slots.append(T)

    gbv = self.dgb_d.rearrange("(c p) n -> c p n", p=128)
    kTv = self.dkT_d.rearrange("(h d) t -> d h t", d=128)
    qTv = self.dqT_d.rearrange("(h d) t -> d h t", d=128)
    ktv = self.dk_d.rearrange("(c p) (h d) -> c p h d", p=128, d=128)
    vv = self.dv_d.rearrange("(c p) (h d) -> c p h d", p=128, d=128)
    ofv = self.dof_d.rearrange("(c p) (h d) -> c p h d", p=128, d=128)
    szv = self.dsz_d.rearrange("(c p) (h d) -> c p h d", p=128, d=128)
    zv = self.zT.rearrange("(c p) t -> p c t", p=128)
    bc2 = lambda ap: ap.unsqueeze(1).to_broadcast([128, 2, 128])

    def unit(T, hk, d_, kTc, qTc, ktk, vch, och):
        tri, negm = self.tri[d_], self.negm[d_]
        negsL = self.negs["b" if d_ == "f" else "f"]
        hv0 = 2 * hk
        o.mm(T.pmx[:, 0, 0, :], lhsT=kTc[:, hk, :], rhs=kTc[:, hk, :], rd=[kTc], wr=[T.pmx])
        o.mm(T.pmx[:, 0, 1, :], lhsT=kTc[:, hk, :], rhs=qTc[:, hk, :], rd=[kTc, qTc], wr=[T.pmx])
        o.cp("act", T.kq[:], T.pmx[:, 0, :, :], rd=[T.pmx], wr=[T.kq])
        for r in range(2):
            hv = hv0 + r
            o.mm(T.psq[:, r, 0, :], lhsT=sm["ngc"][:, hv:hv + 1].to_broadcast([128, 128]), rhs=tri[:], start=True, stop=False,
                 rd=[sm["ngc"], tri], wr=[T.psq])
            o.mm(T.psq[:, r, 0, :], lhsT=self.identf[:], rhs=negsL[:], start=False, stop=True, rd=[self.identf, negsL], wr=[T.psq])
            o.mm(T.psq[:, r, 1, :], lhsT=sm["gc"][:, hv:hv + 1].to_broadcast([128, 128]), rhs=tri[:], start=True, stop=False,
                 rd=[sm["gc"], tri], wr=[T.psq])
            o.mm(T.psq[:, r, 1, :], lhsT=self.identf[:], rhs=negm[:], start=False, stop=True, rd=[self.identf, negm], wr=[T.psq])
        for r in range(2):
            hv = hv0 + r
            o.act(T.EL[:, r, :], T.psq[:, r, 0, :], AF.Exp, rd=[T.psq, sm["gcum"]], wr=[T.EL], bias=sm["gcum"][:, hv:hv + 1], scale=1.0)
            o.act(T.EAT[:, r, :], T.psq[:, r, 1, :], AF.Exp, rd=[T.psq, sm["ngcum"]], wr=[T.EAT], bias=sm["ngcum"][:, hv:hv + 1], scale=1.0)
        for r in range(2):
            hv = hv0 + r
            o.stt("dve", T.N[:, r, :], T.kq[:, 0, :], sm["nbeta"][:, hv:hv + 1], T.EL[:, r, :], ALU.mult, ALU.mult,
                  rd=[T.kq, sm["nbeta"], T.EL], wr=[T.N])
        o.tt("dve", T.AT[:], bc2(T.kq[:, 1, :]), T.EAT[:], ALU.mult, rd=[T.kq, T.EAT], wr=[T.AT])
        yield
        ND = T.P[0]
        o.tt("dve", ND[:, :, 0, :], T.N[:], bc2(self.bd[:]), ALU.mult, rd=[T.N, self.bd], wr=[ND])
        o.tt("pool", T.NO[:], T.N[:], ND[:, :, 0, :], ALU.subtract, rd=[T.N, ND], wr=[T.NO])
        for r in range(2):
            o.tr(T.pch[:, r, :], ND[:, r, 0, :], self.identf[:], rd=[ND, self.identf], wr=[T.pch])
        o.cp("act", ND[:, :, 1, :], T.pch[:, 0:2, :], rd=[T.pch], wr=[ND])
        o.tt("dve", T.TTf[:], T.pch[:, 0:2, :], bc2(self.identf[:]), ALU.add, rd=[T.pch, self.identf], wr=[T.TTf])
        yield
        Pc = ND
        NL = 4
        for k in range(1, NL + 1):
            Pn = T.P[k % 2]
            for r in range(2):
                o.mm(T.psq[:, r, 0, :], lhsT=Pc[:, r, 1, :], rhs=Pc[:, r, 0, :], rd=[Pc], wr=[T.psq])
                if k < NL:
                    o.mm(T.psq[:, r, 1, :], lhsT=Pc[:, r, 0, :], rhs=Pc[:, r, 1, :], rd=[Pc], wr=[T.psq])
            if k < NL:
                o.cp("act" if k % 2 else "dve", Pn[:], T.psq[:], rd=[T.psq], wr=[Pn])
            else:
                o.cp("act", Pn[:, :, 0, :], T.psq[:, :, 0, :], rd=[T.psq], wr=[Pn])
            yield
            for r in range(2):
                o.mm(T.pch[:, r, :], lhsT=Pn[:, r, 0, :], rhs=T.TTf[:, r, :], rd=[Pn, T.TTf], wr=[T.pch])
            o.tt("dve", T.TTf[:], T.TTf[:], T.pch[:, 0:2, :], ALU.add, rd=[T.TTf, T.pch], wr=[T.TTf])
            Pc = Pn
            yield
        X0T = T.TTf
        for r in range(2):
            o.tr(T.pch[:, r, :], X0T[:, r, :], self.identf[:], rd=[X0T, self.identf], wr=[T.pch])
        o.cp("act", T.X0[:], T.pch[:, 0:2, :], rd=[T.pch], wr=[T.X0])
        for r in range(2):
            o.mm(T.psq[:, r, 0, :], lhsT=X0T[:, r, :], rhs=T.NO[:, r, :], rd=[X0T, T.NO], wr=[T.psq])
            o.mm(T.psq[:, r, 1, :], lhsT=T.NO[:, r, :], rhs=X0T[:, r, :], rd=[X0T, T.NO], wr=[T.psq])
        o.cp("dve", T.MM[:], T.psq[:], rd=[T.psq], wr=[T.MM])
        o.tt("dve", T.IMT[:], T.psq[:, :, 1, :], bc2(self.identf[:]), ALU.add, rd=[T.psq, self.identf], wr=[T.IMT])
        yield
        for r in range(2):
            o.mm(T.pst[:, r, 0, :], lhsT=T.MM[:, r, 1, :], rhs=T.MM[:, r, 0, :], rd=[T.MM], wr=[T.pst])
        o.cp("act", T.M2[:], T.pst[:, :, 0, :], rd=[T.pst], wr=[T.M2])
        yield
        for r in range(2):
            o.mm(T.pch[:, r, :], lhsT=T.MM[:, r, 0, :], rhs=T.MM[:, r, 1, :], start=True, stop=False, rd=[T.MM], wr=[T.pch])
            o.mm(T.pch[:, r, :], lhsT=T.M2[:, r, :], rhs=T.MM[:, r, 1, :], start=False, stop=True, rd=[T.MM, T.M2], wr=[T.pch])
        o.tt("dve", T.W[:], T.IMT[:], T.pch[:, 0:2, :], ALU.add, rd=[T.IMT, T.pch], wr=[T.W])
        yield
        for r in range(2):
            o.mm(T.psq[:, r, 0, :], lhsT=T.X0[:, r, :], rhs=T.W[:, r, :], rd=[T.X0, T.W], wr=[T.psq])
        o.cp("act", T.TT[:], T.psq[:, :, 0, :], rd=[T.psq], wr=[T.TT])
        yield
        TT = T.TT
        for r in range(2):
            hv = hv0 + r
            o.ts("pool", T.vb[:, r, :], vch[:, hv, :], sm["beta"][:, hv:hv + 1], None, ALU.mult, rd=[vch, sm["beta"]], wr=[T.vb])
            o.ts("pool", T.kbg[:, r, :], ktk[:, hk, :], sm["bg"][:, hv:hv + 1], None, ALU.mult, rd=[ktk, sm["bg"]], wr=[T.kbg])
            o.ts("pool", T.kdec[:, r, :], ktk[:, hk, :], sm["egl"][:, hv:hv + 1], None, ALU.mult, rd=[ktk, sm["egl"]], wr=[T.kdec])
        for r in range(2):
            o.mm(T.pmx[:, r, 0, :], lhsT=TT[:, r, :], rhs=T.vb[:, r, :], rd=[TT, T.vb], wr=[T.pmx])
            o.mm(T.pmx[:, r, 1, :], lhsT=T.kbg[:, r, :], rhs=TT[:, r, :], rd=[TT, T.kbg], wr=[T.pmx])
        o.cp("act", T.us[:], T.pmx[:, :, 0, :], rd=[T.pmx], wr=[T.us])
        o.cp("dve", T.wT[:], T.pmx[:, :, 1, :], rd=[T.pmx], wr=[T.wT])
        yield
        for r in range(2):
            hv = hv0 + r
            o.mm(T.pst[:, r, 0, :], lhsT=T.wT[:, r, :], rhs=Sb[hv][:], rd=[T.wT, Sb[hv]], wr=[T.pst])
            o.mm(T.pst[:, r, 1, :], lhsT=qTc[:, hk, :], rhs=Sb[hv][:], rd=[qTc, Sb[hv]], wr=[T.pst])
        o.tt("dve", T.vn[:], T.us[:], T.pst[:, :, 0, :], ALU.subtract, rd=[T.us, T.pst], wr=[T.vn])
        for r in range(2):
            hv = hv0 + r
            o.act(T.o1[:, r, :], T.pst[:, r, 1, :], AF.Copy, rd=[T.pst, sm["eg"]], wr=[T.o1], scale=sm["eg"][:, hv:hv + 1])
        yield
        for r in range(2):
            o.mm(T.pmx[:, r, 0, :], lhsT=T.AT[:, r, :], rhs=T.vn[:, r, :], rd=[T.AT, T.vn], wr=[T.pmx])
            o.mm(T.pmx[:, r, 1, :], lhsT=T.kdec[:, r, :], rhs=T.vn[:, r, :], rd=[T.kdec, T.vn], wr=[T.pmx])
        o.tt("dve", och[:, hv0:hv0 + 2, :], T.o1[:], T.pmx[:, :, 0, :], ALU.add, rd=[T.o1, T.pmx], wr=[och])
        for r in range(2):
            hv = hv0 + r
            o.stt("dve", Sf[hv][:], Sf[hv][:], sm["etot"][:, hv:hv + 1], T.pmx[:, r, 1, :], ALU.mult, ALU.add,
                  rd=[Sf[hv], sm["etot"], T.pmx], wr=[Sf[hv]])
            o.cp("pool", Sb[hv][:], Sf[hv][:], rd=[Sf[hv]], wr=[Sb[hv]])
        yield

    for di, d_ in enumerate(("f", "b")):
        tri = self.tri[d_]
        self.S.barrier()
        for hv in range(HV):
            o.memset("pool", Sf[hv][:], 0.0, wr=[Sf[hv]])
            o.memset("pool", Sb[hv][:], 0.0, wr=[Sb[hv]])
        for c in chunk_order(d_):
            tsl = slice(c * 128, (c + 1) * 128)
            gbc, kTc, qTc, ktk, vch, och = gbs.next(), kTs.next(), qTs.next(), kts.next(), vcs.next(), ochs.next()
            o.dma("sp", gbc[:], gbv[c], wr=[gbc])
            o.dma("sp", kTc[:], kTv[:, :, tsl], wr=[kTc])
            o.dma("sp", qTc[:], qTv[:, :, tsl], wr=[qTc])
            o.dma("sp", ktk[:], ktv[c], wr=[ktk])
            o.dma("sp", vch[:], vv[c], wr=[vch])
            g_in = gbc[:, di * HV:(di + 1) * HV]
            b_in = gbc[:, 64 + di * HV:64 + (di + 1) * HV]
            pm = slots[0].pst
            o.cp("dve", sm["gc"][:], g_in, rd=[gbc], wr=[sm["gc"]])
            o.ts("dve", sm["ngc"][:], g_in, -1.0, None, ALU.mult, rd=[gbc], wr=[sm["ngc"]])
            o.cp("dve", sm["beta"][:], b_in, rd=[gbc], wr=[sm["beta"]])
            o.ts("dve", sm["nbeta"][:], b_in, -1.0, None, ALU.mult, rd=[gbc], wr=[sm["nbeta"]])
            o.mm(pm[:, 0, 0, 0:HV], lhsT=tri[:], rhs=sm["gc"][:], rd=[tri, sm["gc"]], wr=[pm])
            o.mm(pm[:, 0, 0, HV:2 * HV], lhsT=self.onesf[:], rhs=sm["gc"][:], rd=[self.onesf, sm["gc"]], wr=[pm])
            o.cp("dve", sm["gcum"][:], pm[:, 0, 0, 0:HV], rd=[pm], wr=[sm["gcum"]])
            o.ts("dve", sm["ngcum"][:], pm[:, 0, 0, 0:HV], -1.0, None, ALU.mult, rd=[pm], wr=[sm["ngcum"]])
            o.tt("dve", sm["egl"][:], pm[:, 0, 0, HV:2 * HV], sm["gcum"][:], ALU.subtract, rd=[pm, sm["gcum"]], wr=[sm["egl"]])
            o.act(sm["etot"][:], pm[:, 0, 0, HV:2 * HV], AF.Exp, rd=[pm], wr=[sm["etot"]])
            o.act(sm["eg"][:], sm["gcum"][:], AF.Exp, rd=[sm["gcum"]], wr=[sm["eg"]])
            o.act(sm["egl"][:], sm["egl"][:], AF.Exp, rd=[sm["egl"]], wr=[sm["egl"]])
            o.tt("dve", sm["bg"][:], sm["eg"][:], sm["beta"][:], ALU.mult, rd=[sm["eg"], sm["beta"]], wr=[sm["bg"]])
            for hk0 in range(0, HK, 2):
                gens = [unit(slots[w_], hk0 + w_, d_, kTc, qTc, ktk, vch, och) for w_ in range(2)]
                alive = list(gens)
                while alive:
                    for g_ in list(alive):
                        try:
                            next(g_)
                        except StopIteration:
                            alive.remove(g_)
            if d_ == "f":
                o.dma("sp", ofv[c], och[:], rd=[och])
                continue
            o.dma("sp", of[:], ofv[c], wr=[of])
            o.dma("sp", szc[:], szv[c], wr=[szc])
            o.tt("pool", och[:], och[:], of[:], ALU.add, rd=[och, of], wr=[och])
            o.act(junk[:], och[:], AF.Square, rd=[och], wr=[junk])
            o.S.op("dve", lambda e: e.tensor_reduce(out=ssq[:], in_=junk[:], axis=AX.X, op=ALU.add), R_([junk]), R_([ssq]))
            o.act(rstd[:], ssq[:], AF.Ln, rd=[ssq], wr=[rstd], bias=self.eps_t[:, 0:1], scale=1.0 / 128)
            o.act(rstd[:], rstd[:], AF.Exp, rd=[rstd], wr=[rstd], scale=-0.5)
            o.tt("dve", och[:], och[:], rstd[:].unsqueeze(2).to_broadcast([128, HV, 128]), ALU.mult, rd=[och, rstd], wr=[och])
            o.tt("pool", och[:], och[:], nwb[:].unsqueeze(1).to_broadcast([128, HV, 128]), ALU.mult, rd=[och, nwb], wr=[och])
            o.tt("dve", un[:], och[:], szc[:], ALU.mult, rd=[och, szc], wr=[un])
            st = zst.next()
            for c8 in range(0, 32, 8):
                T = slots[(c8 // 8) % 2]
                ptb = T.psq[:].rearrange("p a b c -> p (a b c)").bitcast(BF16).rearrange("p (a b) -> p a b", b=128)
                for q in range(8):
                    o.tr(ptb[:, q, :], un[:, c8 + q, :], self.ident[:], rd=[un, self.ident], wr=[T.psq])
                o.cp("dve", st[:, c8:c8 + 8, :], ptb[:, 0:8, :], rd=[T.psq], wr=[st])
            o.dma("sp", zv[:, :, tsl], st[:], rd=[st])
    p.close()
```

```python
import numpy as np
from contextlib import ExitStack
import concourse.bass as bass
import concourse.mybir as mybir
from concourse.bass_utils import run_bass_kernel_spmd

F32 = mybir.dt.float32
BF16 = mybir.dt.bfloat16
AF = mybir.ActivationFunctionType
ALU = mybir.AluOpType
AX = mybir.AxisListType


class Res:
    __slots__ = ("name", "lw", "rd", "excl")

    def __init__(self, name=""):
        self.name = name
        self.excl = False
        self.lw = None
        self.rd = []


class Sched:
    ENGS = ("pe", "act", "dve", "pool", "sp")
    N_DMA_SEMS = 12

    def __init__(self, nc, stack):
        self.nc = nc
        self.eng = {"pe": nc.tensor, "act": nc.scalar, "dve": nc.vector,
                    "pool": nc.gpsimd, "sp": nc.sync}
        self.sem = {e: stack.enter_context(nc.semaphore("s_" + e)) for e in self.ENGS}
        self.cnt = {e: 0 for e in self.ENGS}
        self.dsem = {q: [stack.enter_context(nc.semaphore(f"d_{q}{i}")) for i in range(self.N_DMA_SEMS)]
                     for q in ("sp", "pool", "act")}
        self.dval = {q: [0] * self.N_DMA_SEMS for q in self.dsem}
        self.drr = {q: 0 for q in self.dsem}
        self.waited = {e: {} for e in self.ENGS}
        self.semobj = {}
        self.out_tokens = []
        self.nwaits = 0
        self.nops = 0

    def _wait(self, e, tok):
        key, val = tok
        w = self.waited[e]
        if w.get(key, 0) >= val:
            return
        w[key] = val
        self.eng[e].wait_ge(self.semobj[key], val)
        self.nwaits += 1

    def _deps(self, e, reads, writes):
        toks = {}
        def add(t):
            if t is None:
                return
            k, v = t
            if toks.get(k, 0) < v:
                toks[k] = v
        for r in reads:
            add(r.lw)
            if r.excl:
                for t in r.rd:
                    if t[0] != "c_" + e:
                        add(t)
        for w in writes:
            add(w.lw)
            for t in w.rd:
                add(t)
        for k, v in toks.items():
            if e == "pe" and k == "c_pe":
                continue
            self._wait(e, (k, v))

    def _commit(self, tok, reads, writes):
        for r in reads:
            r.rd.append(tok)
            if len(r.rd) > 64:
                m = {}
                for k, v in r.rd:
                    if m.get(k, 0) < v:
                        m[k] = v
                r.rd = list(m.items())
        for w in writes:
            w.lw = tok
            w.rd = []

    def op(self, e, fn, reads=(), writes=()):
        self._deps(e, reads, writes)
        ins = fn(self.eng[e])
        self.cnt[e] += 1
        key = "c_" + e
        self.semobj[key] = self.sem[e]
        ins.then_inc(self.sem[e], 1)
        tok = (key, self.cnt[e])
        self._commit(tok, reads, writes)
        self.nops += 1
        return tok

    def dma(self, q, out, in_, reads=(), writes=(), is_output=False, **kw):
        self._deps(q, reads, writes)
        i = self.drr[q]
        self.drr[q] = (i + 1) % self.N_DMA_SEMS
        key = f"d_{q}{i}"
        self.semobj[key] = self.dsem[q][i]
        if self.dval[q][i] > 0:
            self._wait(q, (key, self.dval[q][i]))
        ins = self.eng[q].dma_start(out=out, in_=in_, **kw)
        self.dval[q][i] += 16
        ins.then_inc(self.dsem[q][i], 16)
        tok = (key, self.dval[q][i])
        self._commit(tok, reads, writes)
        if is_output:
            self.out_tokens.append(tok)
        self.nops += 1
        return tok

    def barrier(self):
        toks = [("c_" + e, self.cnt[e]) for e in self.ENGS if self.cnt[e] > 0]
        for q in self.dsem:
            for i in range(self.N_DMA_SEMS):
                if self.dval[q][i] > 0:
                    toks.append((f"d_{q}{i}", self.dval[q][i]))
        for e in self.ENGS:
            for t in toks:
                if t[0] == "c_" + e:
                    continue
                self._wait(e, t)

    def finish(self):
        self.barrier()


class Tl:
    __slots__ = ("t", "res")

    def __init__(self, t):
        self.t = t
        self.res = Res()

    def __getitem__(self, k):
        return self.t[k]


class StopBuild(Exception):
    pass


class Ctx:
    def __init__(self, nc, S, stack):
        self.nc, self.S, self.stack = nc, S, stack
        self.n = 0

    def sub(self):
        st = ExitStack()
        c = Ctx2(self, st)
        self.open = getattr(self, "open", [])
        self.open.append(c)
        return c


class Ctx2:
    def __init__(self, parent, st):
        self.nc, self.S = parent.nc, parent.S
        self.st = st
        self.parent = parent

    def sb(self, shape, dt, name=None):
        self.parent.n += 1
        return Tl(self.st.enter_context(self.nc.sbuf_tensor(f"{name or 't'}_{self.parent.n}", list(shape), dt)))

    def ps(self, shape, dt=F32, name=None):
        self.parent.n += 1
        nbytes = int(np.prod(shape[1:])) * (4 if dt == F32 else 2)
        assert nbytes == 2048, ("psum tiles must be exactly one bank", shape)
        t = Tl(self.st.enter_context(self.nc.psum_tensor(f"{name or 'p'}_{self.parent.n}", list(shape), dt)))
        t.res.excl = True
        return t

    def close(self):
        self.S.barrier()
        self.st.close()
        self.parent.open.remove(self)
        import os
        self.parent.nclose = getattr(self.parent, "nclose", 0) + 1
        if int(os.environ.get("STOP", "0")) == self.parent.nclose:
            raise StopBuild()


def R_(tiles):
    return [t.res for t in tiles]


class Ops:
    def __init__(self, S):
        self.S = S

    def mm(self, out, lhsT, rhs, start=True, stop=True, rd=(), wr=()):
        return self.S.op("pe", lambda e: e.matmul(out, lhsT=lhsT, rhs=rhs, start=start, stop=stop), R_(rd), R_(wr))

    def tr(self, out, in_, ident, rd=(), wr=()):
        return self.S.op("pe", lambda e: e.transpose(out=out, in_=in_, identity=ident), R_(rd), R_(wr))

    def act(self, out, in_, func, rd=(), wr=(), **kw):
        return self.S.op("act", lambda e: e.activation(out=out, in_=in_, func=func, **kw), R_(rd), R_(wr))

    def tt(self, eng, out, in0, in1, op, rd=(), wr=()):
        return self.S.op(eng, lambda e: e.tensor_tensor(out=out, in0=in0, in1=in1, op=op), R_(rd), R_(wr))

    def ts(self, eng, out, in0, s1, s2, op0, op1=None, rd=(), wr=()):
        if op1 is None:
            return self.S.op(eng, lambda e: e.tensor_scalar(out=out, in0=in0, scalar1=s1, scalar2=None, op0=op0), R_(rd), R_(wr))
        return self.S.op(eng, lambda e: e.tensor_scalar(out=out, in0=in0, scalar1=s1, scalar2=s2, op0=op0, op1=op1), R_(rd), R_(wr))

    def stt(self, eng, out, in0, scalar, in1, op0, op1, rd=(), wr=()):
        return self.S.op(eng, lambda e: e.scalar_tensor_tensor(out=out, in0=in0, scalar=scalar, in1=in1, op0=op0, op1=op1), R_(rd), R_(wr))

    def cp(self, eng, out, in_, rd=(), wr=()):
        if eng == "act":
            return self.S.op("act", lambda e: e.copy(out=out, in_=in_), R_(rd), R_(wr))
        return self.S.op(eng, lambda e: e.tensor_copy(out=out, in_=in_), R_(rd), R_(wr))

    def recip(self, out, in_, rd=(), wr=()):
        return self.S.op("dve", lambda e: e.reciprocal(out=out, in_=in_), R_(rd), R_(wr))

    def memset(self, eng, out, val, wr=()):
        return self.S.op(eng, lambda e: e.memset(out, val), (), R_(wr))

    def dma(self, q, out, in_, rd=(), wr=(), **kw):
        return self.S.dma(q, out, in_, R_(rd), R_(wr), **kw)


class Rot:
    def __init__(self, tiles):
        self.tiles, self.i = tiles, 0

    def next(self):
        t = self.tiles[self.i % len(self.tiles)]
        self.i += 1
        return t


D = 2048
KC = D // 128
CTXL = 256
SEQL = 2048
NTOK = CTXL + SEQL
NT = NTOK // 128
EPS = 1e-6
DEPTH = 4
N_HEADS, N_KV = 16, 4
ATT_SCALE = 128 ** -0.5


def seq_blocks():
    return [(0, CTXL, True)] + [(CTXL + 512 * i, 512, False) for i in range(SEQL // 512)]


class Prog:
    def __init__(self, n_seq, layers, full_out=False):
        self.n_seq, self.layers, self.full_out = n_seq, layers, full_out
        self.nrow = n_seq + 1
        nc = self.nc = bass.Bass("TRN2", target_bir_lowering=False)
        self.inputs = {}
        self.stack = ExitStack()
        self.S = Sched(nc, self.stack)
        self.o = Ops(self.S)
        self.ctx = Ctx(nc, self.S, self.stack)
        self.g = Ctx2(self.ctx, self.stack)

    def inp(self, name, shape, dt=F32):
        t = self.nc.dram_tensor(name, list(shape), dt, kind="ExternalInput").ap()
        self.inputs[name] = t
        return t

    def scratch(self, name, shape, dt):
        return self.nc.dram_tensor(name, list(shape), dt).ap()

    def build(self):
        nc, S, o, g = self.nc, self.S, self.o, self.g
        ns = self.n_seq
        self.xin = self.inp("xin", [ns, D, NTOK])
        self.cT_d = self.inp("cT", [128, KC, self.nrow])
        wout = NTOK if self.full_out else SEQL
        self.yout = nc.dram_tensor("yout", [ns, D, wout], F32, kind="ExternalOutput").ap()
        self.xs = self.scratch("xs", [ns, D, NTOK], F32)
        self.zT = self.scratch("zT", [4096, NTOK], BF16)
        self.ident_d = self.inp("ident", [128, 128])
        self.ident = g.sb([128, 128], BF16, "ident")
        o.dma("pool", self.ident[:], self.ident_d[:, :], wr=[self.ident])
        self.identf = g.sb([128, 128], F32, "identf")
        o.dma("sp", self.identf[:], self.ident_d[:, :], wr=[self.identf])
        self.ones = g.sb([128, 128], BF16, "ones")
        o.memset("pool", self.ones[:], 1.0, wr=[self.ones])
        self.onesf = g.sb([128, 128], F32, "onesf")
        o.memset("pool", self.onesf[:], 1.0, wr=[self.onesf])
        self.eps_t = g.sb([128, 1], F32, "eps")
        o.memset("pool", self.eps_t[:], EPS, wr=[self.eps_t])
        self.m4_t = g.sb([128, 1], F32, "m4")
        o.memset("pool", self.m4_t[:], -4.0, wr=[self.m4_t])
        self.sc = g.sb([128, KC, self.nrow], F32, "sc")
        o.dma("sp", self.sc[:], self.cT_d[:, :, :], wr=[self.sc])
        o.act(self.sc[:], self.sc[:], AF.Silu, rd=[self.sc], wr=[self.sc])
        self.A = g.sb([128, KC, self.nrow], F32, "modA")
        self.SH = g.sb([128, KC, self.nrow], F32, "modS")
        self.G = g.sb([128, KC, self.nrow], F32, "modG")
        try:
            self.build_layers()
        except StopBuild:
            for c in reversed(list(self.ctx.open)):
                c.st.close()
        S.finish()
        self.stack.close()
        return nc

    def build_layers(self):
        nc, S, o, g = self.nc, self.S, self.o, self.g
        ns = self.n_seq
        first = True
        for li in self.layers:
            kind, j = li % 3, li // 3
            need_ctx = li < DEPTH - 1 or self.full_out
            last = (li == self.layers[-1])
            self.modulation(li)
            for s in range(ns):
                src = self.xin[s] if first else self.xs[s]
                if last:
                    dst = self.yout[s]
                    dst_off = 0 if self.full_out else CTXL
                else:
                    dst, dst_off = self.xs[s], 0
                if kind == 0:
                    self.attn_layer(li, j, s, src, dst, dst_off, need_ctx)
                elif kind == 1:
                    self.ssd_layer(li, j, s, src, dst, dst_off, need_ctx)
                else:
                    self.dn_layer(li, j, s, src, dst, dst_off, need_ctx)
            first = False

    def modulation(self, li):
        o, S = self.o, self.S
        nr = self.nrow
        p = self.ctx.sub()
        w_d = self.inp(f"ada_w{li}", [12, 128, 4, KC, 128])
        b_d = self.inp(f"ada_b{li}", [128, 48])
        pre_d = self.inp(f"pre_w{li}", [128, KC])
        post_d = self.inp(f"post_w{li}", [128, KC])
        bt = p.sb([128, 48], F32)
        pre = p.sb([128, KC], F32)
        post = p.sb([128, KC], F32)
        o.dma("sp", bt[:], b_d[:, :], wr=[bt])
        o.dma("sp", pre[:], pre_d[:, :], wr=[pre])
        o.dma("sp", post[:], post_d[:, :], wr=[post])
        wts = Rot([p.sb([128, 4, KC, 128], F32) for _ in range(2)])
        pm = p.ps([128, 128, 4], F32)
        for og in range(12):
            wt = wts.next()
            o.dma("sp", wt[:], w_d[og], wr=[wt])
            for jj in range(4):
                oc = og * 4 + jj
                for kc in range(KC):
                    o.mm(pm[:, oc, 0:nr], lhsT=wt[:, jj, kc, :], rhs=self.sc[:, kc, :],
                         start=(kc == 0), stop=(kc == KC - 1), rd=[wt, self.sc], wr=[pm])
        mt = p.sb([128, 48, nr], F32)
        o.tt("dve", mt[:], pm[:, 0:48, 0:nr], bt[:].unsqueeze(2).to_broadcast([128, 48, nr]), ALU.add, rd=[pm, bt], wr=[mt])
        o.cp("dve", self.SH[:], mt[:, 0:KC, :], rd=[mt], wr=[self.SH])
        o.stt("dve", self.A[:], mt[:, KC:2 * KC, :], 1.0, pre[:].unsqueeze(2).to_broadcast([128, KC, nr]),
              ALU.add, ALU.mult, rd=[mt, pre], wr=[self.A])
        o.tt("dve", self.G[:], mt[:, 2 * KC:3 * KC, :], post[:].unsqueeze(2).to_broadcast([128, KC, nr]), ALU.mult,
             rd=[mt, post], wr=[self.G])
        p.close()

    def rstd_from_ssq(self, p, pq, w, n, rstd):
        o = self.o
        o.act(rstd[:, :w], pq[:, :w], AF.Ln, rd=[pq], wr=[rstd], bias=self.eps_t[:, 0:1], scale=1.0 / n)
        o.act(rstd[:, :w], rstd[:, :w], AF.Exp, rd=[rstd], wr=[rstd], scale=-0.5)

    def phase1(self, p, s, src, hT):
        o = self.o
        xts = Rot([p.sb([128, KC, 512], F32, "xt") for _ in range(2)])
        sq = p.sb([128, KC, 512], BF16, "sq")
        pq = p.ps([128, 512], F32, "pq")
        rstd = p.sb([128, 512], F32, "rstd")
        tmps = Rot([p.sb([128, 512], F32, "tmp") for _ in range(2)])
        srcv = src.rearrange("(kc p) t -> p kc t", p=128)
        for (t0, w, is_ctx) in seq_blocks():
            row = self.n_seq if is_ctx else s
            xt = xts.next()
            o.dma("sp", xt[:, :, :w], srcv[:, :, t0:t0 + w], wr=[xt])
            o.act(sq[:, :, :w], xt[:, :, :w], AF.Square, rd=[xt], wr=[sq])
            for kc in range(KC):
                o.mm(pq[:, :w], lhsT=self.ones[:], rhs=sq[:, kc, :w], start=(kc == 0), stop=(kc == KC - 1),
                     rd=[self.ones, sq], wr=[pq])
            self.rstd_from_ssq(p, pq, w, D, rstd)
            for kc in range(KC):
                tmp = tmps.next()
                o.stt("dve", tmp[:, :w], xt[:, kc, :w], self.A[:, kc, row:row + 1], rstd[:, :w], ALU.mult, ALU.mult,
                      rd=[xt, self.A, rstd], wr=[tmp])
                o.act(hT[:, kc, t0:t0 + w], tmp[:, :w], AF.Identity, rd=[tmp, self.SH], wr=[hT],
                      bias=self.SH[:, kc, row:row + 1], scale=1.0)

    def gemm_f(self, p, hT, kcn, w_d, n_oc, epilogue, blocks=None, grp=4):
        o = self.o
        blocks = blocks or seq_blocks()
        wts = Rot([p.sb([128, grp, kcn, 128], BF16, "wt") for _ in range(2)])
        pss = Rot([p.ps([128, 512], F32, "pg") for _ in range(3)])
        pending, u = [], 0
        for og in range(0, n_oc, grp):
            wt = wts.next()
            gn = min(grp, n_oc - og)
            o.dma("pool", wt[:, :gn], w_d[og:og + gn].rearrange("g p k j -> p g k j"), wr=[wt])
            for jj in range(gn):
                for blk in blocks:
                    t0, w, _ = blk
                    ps = pss.next()
                    for kc in range(kcn):
                        o.mm(ps[:, :w], lhsT=wt[:, jj, kc, :], rhs=hT[:, kc, t0:t0 + w], start=(kc == 0),
                             stop=(kc == kcn - 1), rd=[wt, hT], wr=[ps])
                    d_ = epilogue(og + jj, blk, ps)
                    if d_ is not None:
                        pending.append((u + d_[0], d_[1]))
                    for it in [x for x in pending if x[0] <= u]:
                        pending.remove(it)
                        it[1]()
                    u += 1
        for it in pending:
            it[1]()

    def gemm_t(self, p, hT, kcn, w_d, n_g, epilogue, gw=512):
        o = self.o
        wts = Rot([p.sb([128, kcn, gw], BF16, "wtt") for _ in range(2)])
        pss = Rot([p.ps([128, 512], F32, "pgt") for _ in range(2)])
        for gi in range(n_g):
            wt = wts.next()
            o.dma("pool", wt[:], w_d[gi], wr=[wt])
            for tt in range(NT):
                ps = pss.next()
                for kc in range(kcn):
                    o.mm(ps[:, :gw], lhsT=hT[:, kc, tt * 128:(tt + 1) * 128], rhs=wt[:, kc, :], start=(kc == 0),
                         stop=(kc == kcn - 1), rd=[wt, hT], wr=[ps])
                epilogue(gi, tt, ps)

    def phase4(self, s, src, dst, dst_off, need_ctx, kcn, w_d):
        o = self.o
        p = self.ctx.sub()
        wos = [p.sb([128, kcn, 512], BF16, f"wo{q}") for q in range(4)]
        for q in range(4):
            for h0 in range(0, kcn, 8):
                hs = slice(h0, min(kcn, h0 + 8))
                o.dma("pool" if (q + h0 // 8) % 2 == 0 else "pool", wos[q][:, hs, :], w_d[:, hs, q * 512:(q + 1) * 512], wr=[wos[q]])
        bw = 256 if kcn <= 16 else 128
        zbs = Rot([p.sb([128, kcn, bw], BF16, "zb") for _ in range(2)])
        xbs = Rot([p.sb([128, KC, bw], F32, "xb") for _ in range(2)])
        ysb = p.sb([128, KC, bw], F32, "ysb")
        sqs = Rot([p.sb([128, bw], BF16, "sq4") for _ in range(2)])
        pys = Rot([p.ps([128, 512], F32, "py") for _ in range(3)])
        pq = p.ps([128, 512], F32, "pq4")
        rstd = p.sb([128, bw], F32, "rstd4")
        tmps = Rot([p.sb([128, bw], F32, "tmp4") for _ in range(2)])
        srcv = src.rearrange("(kc p) t -> p kc t", p=128)
        dstv = dst.rearrange("(kc p) t -> p kc t", p=128)
        zv = self.zT[0:kcn * 128, :].rearrange("(kc p) t -> p kc t", p=128)
        blocks = [(t0, bw, t0 < CTXL) for t0 in range(0, NTOK, bw)]
        import os
        nb = int(os.environ.get("P4N", "999"))
        for (t0, w, is_ctx) in blocks[:nb]:
            if is_ctx and not need_ctx:
                continue
            row = self.n_seq if is_ctx else s
            zb, xb = zbs.next(), xbs.next()
            xo = xb
            o.dma("sp", zb[:], zv[:, :, t0:t0 + w], wr=[zb])
            o.dma("sp", xb[:], srcv[:, :, t0:t0 + w], wr=[xb])
            stg_ = int(os.environ.get("P4S", "9"))
            if stg_ < 2:
                continue
            for oc in range(KC):
                py = pys.next()
                for kc in range(kcn):
                    wo = wos[oc // 4]
                    o.mm(py[:, :w], lhsT=wo[:, kc, (oc % 4) * 128:(oc % 4 + 1) * 128], rhs=zb[:, kc, :], start=(kc == 0),
                         stop=(kc == kcn - 1), rd=[wo, zb], wr=[py])
                sq = sqs.next()
                px = int(os.environ.get("P4X", "7"))
                if px & 1:
                    o.act(sq[:], py[:, :w], AF.Square, rd=[py], wr=[sq])
                if px & 2:
                    o.cp("dve", ysb[:, oc, :], py[:, :w], rd=[py], wr=[ysb])
                if px & 4:
                    o.mm(pq[:, :w], lhsT=self.ones[:], rhs=sq[:], start=(oc == 0), stop=(oc == KC - 1), rd=[self.ones, sq], wr=[pq])
            if stg_ < 3:
                continue
            self.rstd_from_ssq(p, pq, w, D, rstd)
            if stg_ < 4:
                continue
            for oc in range(KC):
                tmp = tmps.next()
                o.stt("dve", tmp[:], ysb[:, oc, :], self.G[:, oc, row:row + 1], rstd[:], ALU.mult, ALU.mult,
                      rd=[ysb, self.G, rstd], wr=[tmp])
                o.tt("pool", xo[:, oc, :], tmp[:], xb[:, oc, :], ALU.add, rd=[tmp, xb], wr=[xo])
            if stg_ < 5:
                continue
            o.dma("sp", dstv[:, :, t0 - dst_off:t0 - dst_off + w], xo[:], rd=[xo], is_output=True)
        p.close()

    def attn_consts(self):
        if hasattr(self, "cosT"):
            return
        o, g = self.o, self.g
        cos_d = self.inp("rope_cos", [128, SEQL])
        sin_d = self.inp("rope_sin", [128, SEQL])
        perm_d = self.inp("rope_perm", [128, 128])
        self.cos_d, self.sin_d, self.perm_d = cos_d, sin_d, perm_d
        self.qT_d = self.scratch("qT", [D, NTOK], BF16)
        self.kT_d = self.scratch("kT", [512, NTOK], BF16)
        self.gT_d = self.scratch("gT", [D, NTOK], BF16)
        self.V_d = self.scratch("Vd", [NTOK, 512], BF16)

    def attn_layer(self, li, j, s, src, dst, dst_off, need_ctx):
        o = self.o
        self.attn_consts()
        if s == 0:
            self.aw = dict(
                wq=self.inp(f"attn_wq{j}", [16, 128, KC, 128]),
                wk=self.inp(f"attn_wk{j}", [4, 128, KC, 128]),
                wg=self.inp(f"attn_wg{j}", [16, 128, KC, 128]),
                wv=self.inp(f"attn_wv{j}", [1, 128, KC, 512]),
                wo=self.inp(f"attn_wo{j}", [128, KC, D]),
                qn=self.inp(f"attn_qn{j}", [128, 1]),
                kn=self.inp(f"attn_kn{j}", [128, 1]),
            )
        aw = self.aw
        p = self.ctx.sub()
        hT = p.sb([128, KC, NTOK], BF16, "hT")
        p1 = self.ctx.sub()
        self.phase1(p1, s, src, hT)
        p1.close()
        self.cosT = p.sb([128, SEQL], F32, "cosT")
        self.sinT = p.sb([128, SEQL], F32, "sinT")
        self.perm = p.sb([128, 128], BF16, "perm")
        o.dma("sp", self.cosT[:], self.cos_d[:, :], wr=[self.cosT])
        o.dma("sp", self.sinT[:], self.sin_d[:, :], wr=[self.sinT])
        o.dma("pool", self.perm[:], self.perm_d[:, :], wr=[self.perm])
        qn = p.sb([128, 1], F32, "qn")
        kn = p.sb([128, 1], F32, "kn")
        o.dma("sp", qn[:], aw["qn"][:, :], wr=[qn])
        o.dma("sp", kn[:], aw["kn"][:, :], wr=[kn])
        o.ts("dve", qn[:], qn[:], ATT_SCALE, None, ALU.mult, rd=[qn], wr=[qn])
        sqs = Rot([p.sb([128, 512], BF16, "sqa") for _ in range(4)])
        pqs = Rot([p.ps([128, 512], F32, "pqa") for _ in range(2)])
        prs = Rot([p.ps([128, 512], F32, "pra") for _ in range(2)])
        rstds = Rot([p.sb([128, 512], F32, "rstda") for _ in range(4)])
        qns = Rot([p.sb([128, 512], F32, "qna") for _ in range(4)])
        qnbs = Rot([p.sb([128, 512], BF16, "qnb") for _ in range(4)])
        t1s = Rot([p.sb([128, 512], F32, "t1a") for _ in range(4)])
        stg = Rot([p.sb([128, NTOK], BF16, "stg") for _ in range(2)])
        cur = {}

        def qk_epi(dst_d, nw):
            def epi(oc, blk, ps):
                if blk[0] == 0:
                    cur["st"] = stg.next()
                st = cur["st"]
                return (1, lambda: epi2(oc, blk, ps, st))

            def epi2(oc, blk, ps, st):
                t0, w, is_ctx = blk
                sq, pq, rstd = sqs.next(), pqs.next(), rstds.next()
                o.act(sq[:, :w], ps[:, :w], AF.Square, rd=[ps], wr=[sq])
                o.mm(pq[:, :w], lhsT=self.ones[:], rhs=sq[:, :w], rd=[self.ones, sq], wr=[pq])
                self.rstd_from_ssq(p, pq, w, 128, rstd)
                if is_ctx:
                    o.stt("dve", st[:, t0:t0 + w], ps[:, :w], nw[:, 0:1], rstd[:, :w], ALU.mult, ALU.mult,
                          rd=[ps, nw, rstd], wr=[st])
                else:
                    qn_, qnb, pr, t1 = qns.next(), qnbs.next(), prs.next(), t1s.next()
                    o.stt("dve", qn_[:, :w], ps[:, :w], nw[:, 0:1], rstd[:, :w], ALU.mult, ALU.mult,
                          rd=[ps, nw, rstd], wr=[qn_])
                    o.cp("act", qnb[:, :w], qn_[:, :w], rd=[qn_], wr=[qnb])
                    o.mm(pr[:, :w], lhsT=self.perm[:], rhs=qnb[:, :w], rd=[self.perm, qnb], wr=[pr])
                    l0 = t0 - CTXL
                    o.tt("pool", t1[:, :w], qn_[:, :w], self.cosT[:, l0:l0 + w], ALU.mult, rd=[qn_, self.cosT], wr=[t1])
                    o.tt("dve", qn_[:, :w], pr[:, :w], self.sinT[:, l0:l0 + w], ALU.mult, rd=[pr, self.sinT], wr=[qn_])
                    o.tt("pool", st[:, t0:t0 + w], t1[:, :w], qn_[:, :w], ALU.add, rd=[t1, qn_], wr=[st])
                if t0 + w == NTOK:
                    o.dma("sp", dst_d[oc * 128:(oc + 1) * 128, :], st[:], rd=[st])
            return epi

        def g_epi(oc, blk, ps):
            t0, w, _ = blk
            if t0 == 0:
                cur["st"] = stg.next()
            st = cur["st"]
            o.act(st[:, t0:t0 + w], ps[:, :w], AF.Silu, rd=[ps], wr=[st])
            if t0 + w == NTOK:
                o.dma("sp", self.gT_d[oc * 128:(oc + 1) * 128, :], st[:], rd=[st])

        vst = Rot([p.sb([128, 512], BF16, "vst") for _ in range(3)])

        def v_epi(gi, tt, ps):
            st = vst.next()
            o.cp("dve", st[:], ps[:], rd=[ps], wr=[st])
            o.dma("sp", self.V_d[tt * 128:(tt + 1) * 128, :], st[:], rd=[st])

        pk = self.ctx.sub()
        self.gemm_f(pk, hT, KC, aw["wk"], 4, qk_epi(self.kT_d, kn))
        pk.close()
        pk = self.ctx.sub()
        self.gemm_t(pk, hT, KC, aw["wv"], 1, v_epi)
        pk.close()
        pk = self.ctx.sub()
        self.gemm_f(pk, hT, KC, aw["wq"], 16, qk_epi(self.qT_d, qn))
        pk.close()
        pk = self.ctx.sub()
        self.gemm_f(pk, hT, KC, aw["wg"], 16, g_epi)
        pk.close()
        p.close()
        p = self.ctx.sub()
        kT = p.sb([128, N_KV, NTOK], BF16, "kT")
        V = p.sb([128, NT, 512], BF16, "V")
        o.dma("sp", kT[:], self.kT_d.rearrange("(h p) t -> p h t", p=128), wr=[kT])
        o.dma("sp", V[:], self.V_d.rearrange("(c p) n -> p c n", p=128), wr=[V])
        qbs = Rot([p.sb([128, N_HEADS, 512], BF16, "qb") for _ in range(2)])
        gbs = Rot([p.sb([128, N_HEADS, 512], BF16, "gb") for _ in range(2)])
        zbs = Rot([p.sb([128, N_HEADS, 512], BF16, "zb3") for _ in range(2)])
        pss = Rot([p.ps([128, 512], F32, "ps3") for _ in range(2)])
        pos = Rot([p.ps([128, 512], F32, "po3") for _ in range(2)])
        pls = Rot([p.ps([128, 512], F32, "pl3") for _ in range(2)])
        pts = Rot([p.sb([128, 512], BF16, "pt3") for _ in range(3)])
        rss = Rot([p.sb([128, 512], F32, "rs3") for _ in range(2)])
        tos = Rot([p.sb([128, 512], F32, "to3") for _ in range(2)])
        qv = self.qT_d.rearrange("(h p) t -> p h t", p=128)
        gv = self.gT_d.rearrange("(h p) t -> p h t", p=128)
        zv = self.zT[0:D, :].rearrange("(h p) t -> p h t", p=128)
        for (t0, w, is_ctx) in seq_blocks():
            if is_ctx and not need_ctx:
                continue
            nkc = CTXL // 128 if is_ctx else NT
            qb, gb, zb = qbs.next(), gbs.next(), zbs.next()
            o.dma("sp", qb[:, :, :w], qv[:, :, t0:t0 + w], wr=[qb])
            o.dma("sp", gb[:, :, :w], gv[:, :, t0:t0 + w], wr=[gb])
            steps = [(h, c) for h in range(N_HEADS) for c in range(nkc)]
            acc = {}

            def emit_s(h, c):
                kvh = h // (N_HEADS // N_KV)
                ps, pt = pss.next(), pts.next()
                o.mm(ps[:, :w], lhsT=kT[:, kvh, c * 128:(c + 1) * 128], rhs=qb[:, h, :w], rd=[kT, qb], wr=[ps])
                o.act(pt[:, :w], ps[:, :w], AF.Exp, rd=[ps], wr=[pt], bias=self.m4_t[:, 0:1], scale=1.0)
                return pt

            def emit_pv(h, c, pt):
                kvh = h // (N_HEADS // N_KV)
                if c == 0:
                    acc[h] = (pos.next(), pls.next())
                po, pl = acc[h]
                o.mm(po[:, :w], lhsT=V[:, c, kvh * 128:(kvh + 1) * 128], rhs=pt[:, :w], start=(c == 0),
                     stop=(c == nkc - 1), rd=[V, pt], wr=[po])
                o.mm(pl[:, :w], lhsT=self.ones[:], rhs=pt[:, :w], start=(c == 0), stop=(c == nkc - 1),
                     rd=[self.ones, pt], wr=[pl])
                if c == nkc - 1:
                    rs, to = rss.next(), tos.next()
                    o.recip(rs[:, :w], pl[:, :w], rd=[pl], wr=[rs])
                    o.tt("dve", to[:, :w], po[:, :w], rs[:, :w], ALU.mult, rd=[po, rs], wr=[to])
                    o.tt("pool", zb[:, h, :w], to[:, :w], gb[:, h, :w], ALU.mult, rd=[to, gb], wr=[zb])

            pend = emit_s(*steps[0])
            for i_, (h, c) in enumerate(steps):
                nxt = emit_s(*steps[i_ + 1]) if i_ + 1 < len(steps) else None
                emit_pv(h, c, pend)
                pend = nxt
            o.dma("sp", zv[:, :, t0:t0 + w], zb[:, :, :w], rd=[zb])
        p.close()
        self.phase4(s, src, dst, dst_off, need_ctx, KC, aw["wo"])


def lhsT_tiles(w):
    K, N = w.shape
    return np.ascontiguousarray(w.reshape(K // 128, 128, N // 128, 128).transpose(2, 1, 0, 3))


def rhs_tiles(w, gw=512):
    K, N = w.shape
    return np.ascontiguousarray(w.reshape(K // 128, 128, N // gw, gw).transpose(2, 1, 0, 3))


def fm_vec(v):
    return np.ascontiguousarray(v.reshape(-1, 128).T)


def rope_tables():
    rr, cc = np.meshgrid(np.arange(SEQL // 64), np.arange(64), indexing="ij")
    row = rr.reshape(-1).astype(np.float32)
    col = cc.reshape(-1).astype(np.float32)
    inv = (1.0 / (np.float32(10000.0) ** (np.arange(32, dtype=np.float32) / np.float32(32)))).astype(np.float32)
    ang = np.concatenate([row[:, None] * inv, col[:, None] * inv], axis=-1).astype(np.float32)
    cos, sin = np.cos(ang).astype(np.float32), np.sin(ang).astype(np.float32)
    cosT = np.zeros((128, SEQL), np.float32)
    sinT = np.zeros((128, SEQL), np.float32)
    perm = np.zeros((128, 128), np.float32)
    for p in range(128):
        axis, half, f = p // 64, (p // 32) % 2, p % 32
        cosT[p] = cos[:, axis * 32 + f]
        sinT[p] = sin[:, axis * 32 + f] * (-1.0 if half == 0 else 1.0)
        partner = p + 32 if half == 0 else p - 32
        perm[partner, p] = 1.0
    return cosT, sinT, perm


def prep_consts():
    cosT, sinT, perm = rope_tables()
    c = {"ident": np.eye(128, dtype=np.float32), "rope_cos": cosT, "rope_sin": sinT, "rope_perm": perm}
    k = np.arange(128)[:, None]
    i = np.arange(128)[None, :]
    c["tri_f"] = (k <= i).astype(np.float32)
    c["tri_b"] = (k >= i).astype(np.float32)
    c["negm_f"] = np.where(i >= k, 0.0, NEG).astype(np.float32)
    c["negm_b"] = np.where(i <= k, 0.0, NEG).astype(np.float32)
    c["negs_f"] = np.where(i > k, 0.0, NEG).astype(np.float32)
    c["negs_b"] = np.where(i < k, 0.0, NEG).astype(np.float32)
    c["bdmask"] = ((k // 32) == (i // 32)).astype(np.float32)
    return c


def prep_weights(inp, layers):
    w = {}
    for li in layers:
        kind, j = li % 3, li // 3
        aw = inp["ada_w"][li]
        t = lhsT_tiles(aw)
        w[f"ada_w{li}"] = np.ascontiguousarray(t.reshape(12, 4, 128, KC, 128).transpose(0, 2, 1, 3, 4))
        w[f"ada_b{li}"] = fm_vec(inp["ada_b"][li])
        w[f"pre_w{li}"] = fm_vec(inp["pre_norm_w"][li])
        w[f"post_w{li}"] = fm_vec(inp["post_norm_w"][li])
        if kind == 0:
            wi = inp["attn_w_in"][j]
            w[f"attn_wq{j}"] = lhsT_tiles(wi[:, 0:2048])
            w[f"attn_wk{j}"] = lhsT_tiles(wi[:, 2048:2560])
            w[f"attn_wv{j}"] = rhs_tiles(wi[:, 2560:3072])
            w[f"attn_wg{j}"] = lhsT_tiles(wi[:, 3072:5120])
            wo = inp["attn_w_out"][j]
            w[f"attn_wo{j}"] = np.ascontiguousarray(wo.reshape(KC, 128, D).transpose(1, 0, 2))
            w[f"attn_qn{j}"] = np.ascontiguousarray(inp["attn_q_norm"][j].reshape(128, 1))
            w[f"attn_kn{j}"] = np.ascontiguousarray(inp["attn_k_norm"][j].reshape(128, 1))
        elif kind == 1:
            wi = inp["ssd_w_in"][j]
            w[f"ssd_wz{j}"] = rhs_tiles(wi[:, 0:4096])
            w[f"ssd_wx{j}"] = lhsT_tiles(wi[:, 4096:10240])
            w[f"ssd_wdt{j}"] = rhs_tiles(wi[:, 10240:10368], gw=128)
            cw = inp["ssd_conv_w"][j]
            w[f"ssd_cw{j}"] = np.ascontiguousarray(cw.reshape(5, 48, 128).transpose(2, 1, 0))
            w[f"ssd_cb{j}"] = fm_vec(inp["ssd_conv_b"][j])
            w[f"ssd_dtb{j}"] = np.ascontiguousarray(inp["ssd_dt_bias"][j].reshape(128))
            w[f"ssd_alog{j}"] = np.ascontiguousarray(inp["ssd_a_log"][j].reshape(128))
            w[f"ssd_d{j}"] = np.ascontiguousarray(inp["ssd_d"][j])
            w[f"ssd_nw{j}"] = np.ascontiguousarray(inp["ssd_norm_w"][j])
            w[f"ssd_wo{j}"] = np.ascontiguousarray(inp["ssd_w_out"][j].reshape(32, 128, D).transpose(1, 0, 2))
        else:
            wi = inp["dn_w_in"][j]
            w[f"dn_wx{j}"] = lhsT_tiles(wi[:, 0:8192])
            w[f"dn_wz{j}"] = rhs_tiles(wi[:, 8192:12288])
            w[f"dn_wab{j}"] = rhs_tiles(wi[:, 12288:12416], gw=128)
            cw = inp["dn_conv_w"][j]
            w[f"dn_cw{j}"] = np.ascontiguousarray(cw.reshape(5, 64, 128).transpose(2, 1, 0))
            w[f"dn_dtb{j}"] = np.ascontiguousarray(inp["dn_dt_bias"][j].reshape(64))
            w[f"dn_alog{j}"] = np.ascontiguousarray(inp["dn_a_log"][j].reshape(64))
            w[f"dn_nw{j}"] = np.ascontiguousarray(inp["dn_norm_w"][j])
            w[f"dn_wo{j}"] = np.ascontiguousarray(inp["dn_w_out"][j].reshape(32, 128, D).transpose(1, 0, 2))
    return w


def prep_seq(x, ctx, c, c_ctx):
    ns = x.shape[0]
    xin = np.empty((ns, D, NTOK), np.float32)
    xin[:, :, :CTXL] = ctx.transpose(0, 2, 1)
    xin[:, :, CTXL:] = x.transpose(0, 2, 1)
    rows = np.concatenate([c, c_ctx[None]], axis=0)
    cT = np.ascontiguousarray(rows.reshape(ns + 1, KC, 128).transpose(2, 1, 0))
    return xin, cT


_PROG_CACHE = {}


def run_prog(inp_np, per_core_seq, n_seq, layers, full_out, n_cores):
    prog = Prog(n_seq, layers, full_out)
    nc = prog.build()
    shared = dict(prep_consts())
    shared.update(prep_weights(inp_np, layers))
    in_maps = []
    for cidx in range(n_cores):
        m = {k: v for k, v in shared.items() if k in prog.inputs}
        xin, cT = per_core_seq[cidx]
        m["xin"], m["cT"] = xin, cT
        missing = set(prog.inputs) - set(m)
        assert not missing, missing
        in_maps.append(m)
    import os
    if os.environ.get("KTRACE"):
        res = run_bass_kernel_spmd(nc, in_maps, core_ids=list(range(n_cores)), trace=True)
        print("EXEC_NS", res.exec_time_ns, "ops", prog.S.nops, "waits", prog.S.nwaits, "cnt", prog.S.cnt, flush=True)
    else:
        res = run_bass_kernel_spmd(nc, in_maps, core_ids=list(range(n_cores)))
    return [r["yout"] for r in res.results], res


def kernel(**inputs):
    inp = {k: np.asarray(v) for k, v in inputs.items()}
    n_cores, ns = 8, 2
    per_core = []
    for cidx in range(n_cores):
        sl = slice(cidx * ns, (cidx + 1) * ns)
        per_core.append(prep_seq(inp["x"][sl], inp["ctx"][sl], inp["c"][sl], inp["c_ctx"]))
    outs, _ = run_prog(inp, per_core, ns, list(range(DEPTH)), False, n_cores)
    y = np.concatenate([o_.transpose(0, 2, 1) for o_ in outs], axis=0)
    bad = [int((~np.isfinite(y[b])).sum()) for b in range(y.shape[0])]
    if any(bad):
        print("KERNEL non-finite counts per batch element:", bad, flush=True)
    return np.ascontiguousarray(y.astype(np.float32))


SSD_DI, SSD_H, SSD_G, SSD_N, SSD_P = 4096, 64, 8, 128, 64
NEG = -30000.0


def scan_consts(self):
    if hasattr(self, "tri"):
        return
    o, g = self.o, self.g
    self.tri, self.negm, self.negs = {}, {}, {}
    bd_d = self.inp("bdmask", [128, 128])
    self.bd = g.sb([128, 128], F32, "bdmask")
    o.dma("sp", self.bd[:], bd_d[:, :], wr=[self.bd])
    for d_ in ("f", "b"):
        for nm, store in (("tri", self.tri), ("negm", self.negm), ("negs", self.negs)):
            dd = self.inp(f"{nm}_{d_}", [128, 128])
            t = g.sb([128, 128], F32, f"{nm}{d_}")
            o.dma("sp", t[:], dd[:, :], wr=[t])
            store[d_] = t


def ssd_layer(self, li, j, s, src, dst, dst_off, need_ctx):
    o = self.o
    scan_consts(self)
    if s == 0:
        self.sw = dict(
            wz=self.inp(f"ssd_wz{j}", [8, 128, KC, 512]),
            wx=self.inp(f"ssd_wx{j}", [48, 128, KC, 128]),
            wdt=self.inp(f"ssd_wdt{j}", [1, 128, KC, 128]),
            cw=self.inp(f"ssd_cw{j}", [128, 48, 5]),
            cb=self.inp(f"ssd_cb{j}", [128, 48]),
            dtb=self.inp(f"ssd_dtb{j}", [128]),
            alog=self.inp(f"ssd_alog{j}", [128]),
            dsk=self.inp(f"ssd_d{j}", [64]),
            nw=self.inp(f"ssd_nw{j}", [SSD_DI]),
            wo=self.inp(f"ssd_wo{j}", [128, 32, D]),
        )
        self.sz_d = self.scratch("ssd_sz", [NTOK, SSD_DI], BF16)
        self.x_d = self.scratch("ssd_x", [NTOK, SSD_DI], BF16)
        self.B_d = self.scratch("ssd_B", [NTOK, 1024], BF16)
        self.BT_d = self.scratch("ssd_BT", [1024, NTOK], BF16)
        self.CT_d = self.scratch("ssd_CT", [1024, NTOK], BF16)
        self.dt_d = self.scratch("ssd_dt", [NTOK, 128], F32)
        self.yf_d = self.scratch("ssd_yf", [NTOK, SSD_DI], F32)
    sw = self.sw
    p = self.ctx.sub()
    hT = p.sb([128, KC, NTOK], BF16, "hT")
    p1 = self.ctx.sub()
    self.phase1(p1, s, src, hT)
    p1.close()
    pk = self.ctx.sub()
    zst = Rot([pk.sb([128, 512], BF16, "zst") for _ in range(3)])

    def z_epi(gi, tt, ps):
        st = zst.next()
        o.act(st[:], ps[:], AF.Silu, rd=[ps], wr=[st])
        o.dma("sp", self.sz_d[tt * 128:(tt + 1) * 128, gi * 512:(gi + 1) * 512], st[:], rd=[st])

    self.gemm_t(pk, hT, KC, sw["wz"], 8, z_epi)
    pk.close()
    pk = self.ctx.sub()
    dtb = pk.sb([128, 128], F32, "dtb")
    o.dma("sp", dtb[:], sw["dtb"].partition_broadcast(128), wr=[dtb])
    dst_ = Rot([pk.sb([128, 128], F32, "dtst") for _ in range(3)])

    def dt_epi(gi, tt, ps):
        st = dst_.next()
        o.tt("dve", st[:], ps[:, :128], dtb[:], ALU.add, rd=[ps, dtb], wr=[st])
        o.act(st[:], st[:], AF.Exp, rd=[st], wr=[st])
        o.act(st[:], st[:], AF.Ln, rd=[st], wr=[st], bias=1.0, scale=1.0)
        o.dma("sp", self.dt_d[tt * 128:(tt + 1) * 128, :], st[:], rd=[st])

    self.gemm_t(pk, hT, KC, sw["wdt"], 1, dt_epi, gw=128)
    pk.close()
    pk = self.ctx.sub()
    cw = pk.sb([128, 48, 5], F32, "cw")
    cb = pk.sb([128, 48], F32, "cb")
    o.dma("sp", cw[:], sw["cw"][:, :, :], wr=[cw])
    o.dma("sp", cb[:], sw["cb"][:, :], wr=[cb])
    self.conv_gemm(pk, hT, sw["wx"], 48, cw, cb,
                   tok_dst=lambda oc: (self.x_d, oc * 128) if oc < 32 else ((self.B_d, (oc - 32) * 128) if oc < 40 else None),
                   fm_dst=lambda oc: (self.BT_d, (oc - 32) * 128) if 32 <= oc < 40 else ((self.CT_d, (oc - 40) * 128) if oc >= 40 else None))
    pk.close()
    p.close()
    ssd_scan_phase(self, s)
    self.phase4(s, src, dst, dst_off, need_ctx, 32, sw["wo"])


def conv_gemm(self, pk, hT, w_d, n_oc, cw, cb, tok_dst, fm_dst, post=None):
    o = self.o
    segs = [(0, CTXL), (CTXL, SEQL)]
    bufs = Rot([[pk.sb([128, n + 4], F32, "cvb") for (_, n) in segs] for _ in range(3)])
    for pair in bufs.tiles:
        for t in pair:
            o.memset("pool", t[:], 0.0, wr=[t])
    accs = Rot([pk.sb([128, NTOK], F32, "acc") for _ in range(2)])
    rows = Rot([pk.sb([128, NTOK], BF16, "rowb") for _ in range(2)])
    ptr = Rot([pk.ps([128, 8, 128], BF16, "ptr") for _ in range(2)])
    tst = Rot([pk.sb([128, NT, 128], BF16, "tst") for _ in range(2)])
    cur = {}

    def epi(oc, blk, ps):
        t0, w, is_ctx = blk
        if t0 == 0:
            cur["buf"] = bufs.next()
        bc, bl = cur["buf"]
        if is_ctx:
            o.cp("act", bc[:, 2:2 + w], ps[:, :w], rd=[ps], wr=[bc])
        else:
            l0 = t0 - CTXL
            o.cp("act", bl[:, 2 + l0:2 + l0 + w], ps[:, :w], rd=[ps], wr=[bl])
        if t0 + w != NTOK:
            return None
        return (4, lambda: fin(oc, bc, bl))

    def fin(oc, bc, bl):
        acc, row = accs.next(), rows.next()
        for (sb_, (s0, n), eng) in ((bc, segs[0], "dve"), (bl, segs[1], "dve")):
            o.ts(eng, acc[:, s0:s0 + n], sb_[:, 0:n], cw[:, oc, 0:1], cb[:, oc:oc + 1], ALU.mult, ALU.add,
                 rd=[sb_, cw, cb], wr=[acc])
            for k in range(1, 5):
                o.stt(eng, acc[:, s0:s0 + n], sb_[:, k:k + n], cw[:, oc, k:k + 1], acc[:, s0:s0 + n], ALU.mult, ALU.add,
                      rd=[sb_, cw, acc], wr=[acc])
        if post is None:
            o.act(row[:], acc[:], AF.Silu, rd=[acc], wr=[row])
        else:
            o.act(acc[:], acc[:], AF.Silu, rd=[acc], wr=[acc])
            post(oc, acc, row)
        fd = fm_dst(oc)
        if fd is not None:
            o.dma("sp", fd[0][fd[1]:fd[1] + 128, :], row[:], rd=[row])
        td = tok_dst(oc)
        if td is not None:
            st = tst.next()
            for t8 in range(0, NT, 8):
                pt = ptr.next()
                n8 = min(8, NT - t8)
                for q in range(n8):
                    tt = t8 + q
                    o.tr(pt[:, q, :], row[:, tt * 128:(tt + 1) * 128], self.ident[:], rd=[row, self.ident], wr=[pt])
                o.cp("dve", st[:, t8:t8 + n8, :], pt[:, :n8, :], rd=[pt], wr=[st])
            o.dma("sp", td[0].rearrange("(c p) n -> p c n", p=128)[:, :, td[1]:td[1] + 128], st[:], rd=[st])

    self.gemm_f(pk, hT, KC, w_d, n_oc, epi)


Prog.ssd_layer = ssd_layer
Prog.conv_gemm = conv_gemm


def chunk_order(direction):
    if direction == "f":
        return list(range(NT))
    nc_ = CTXL // 128
    return list(range(nc_ - 1, -1, -1)) + list(range(NT - 1, nc_ - 1, -1))


def ssd_scan_phase(self, s):
    o = self.o
    sw = self.sw
    p = self.ctx.sub()
    H, G, E, P = SSD_H, SSD_G, 8, SSD_P
    aneg = p.sb([128, 128], F32, "aneg")
    o.dma("sp", aneg[:], sw["alog"].partition_broadcast(128), wr=[aneg])
    o.act(aneg[:], aneg[:], AF.Exp, rd=[aneg], wr=[aneg])
    o.ts("dve", aneg[:], aneg[:], -1.0, None, ALU.mult, rd=[aneg], wr=[aneg])
    dsk = p.sb([128, H], F32, "dsk")
    o.dma("sp", dsk[:], sw["dsk"].partition_broadcast(128), wr=[dsk])
    nwb = p.sb([128, SSD_DI], F32, "nwb")
    o.dma("sp", nwb[:], sw["nw"].partition_broadcast(128), wr=[nwb])
    hst = [p.sb([128, E * P], F32, f"hst{g_}") for g_ in range(G)]
    hbf = [p.sb([128, E * P], BF16, f"hbf{g_}") for g_ in range(G)]
    xcs = Rot([p.sb([128, H, P], BF16, "xc") for _ in range(2)])
    dts = Rot([p.sb([128, 128], F32, "dtc") for _ in range(2)])
    bts = Rot([p.sb([128, G, 128], BF16, "btc") for _ in range(2)])
    cts = Rot([p.sb([128, G, 128], BF16, "ctc") for _ in range(2)])
    bks = Rot([p.sb([128, G * 128], BF16, "bkc") for _ in range(2)])
    xdt = p.sb([128, H, P], BF16, "xdt")
    xw = p.sb([128, H, P], BF16, "xw")
    da = p.sb([128, H], F32, "da")
    acum = p.sb([128, H], F32, "acum")
    nacum = p.sb([128, H], F32, "nacum")
    eA = p.sb([128, H], F32, "eA")
    wdec = p.sb([128, H], F32, "wdec")
    etot = p.sb([128, H], F32, "etot")
    ych = Rot([p.sb([128, H, P], F32, "ych") for _ in range(2)])
    cbs = Rot([p.sb([128, 128], F32, "cbs") for _ in range(2)])
    Es = Rot([p.sb([128, 128], F32, "E") for _ in range(3)])
    LTs = Rot([p.sb([128, 128], BF16, "LT") for _ in range(3)])
    tmps = Rot([p.sb([128, E, P], F32, "tmpy") for _ in range(2)])
    yf = p.sb([128, H, P], F32, "yf")
    szc = p.sb([128, SSD_DI], BF16, "szc")
    un = p.sb([128, SSD_DI], BF16, "un")
    ssq = p.sb([128, 1], F32, "ssq")
    rstd = p.sb([128, 1], F32, "rstd1")
    junk = p.sb([128, SSD_DI], BF16, "junk")
    zst = Rot([p.sb([128, 32, 128], BF16, "zst3") for _ in range(2)])
    pmisc = p.ps([128, 512], F32, "pmisc")
    pcb = p.ps([128, 512], F32, "pcb")
    pR = Rot([p.ps([128, 4, 128], F32, "pR") for _ in range(2)])
    pys = Rot([p.ps([128, 512], F32, "pyi") for _ in range(2)])
    pYg = p.ps([128, 512], F32, "pYg")
    pHn = p.ps([128, 512], F32, "pHn")
    xv = self.x_d.rearrange("(c p) (h q) -> c p h q", p=128, q=P)
    dtv = self.dt_d.rearrange("(c p) n -> c p n", p=128)
    btv = self.BT_d.rearrange("(g n) t -> n g t", n=128)
    ctv = self.CT_d.rearrange("(g n) t -> n g t", n=128)
    bkv = self.B_d.rearrange("(c p) n -> c p n", p=128)
    yfv = self.yf_d.rearrange("(c p) (h q) -> c p h q", p=128, q=P)
    szv = self.sz_d.rearrange("(c p) n -> c p n", p=128)
    zv = self.zT.rearrange("(c p) t -> p c t", p=128)
    for di, d_ in enumerate(("f", "b")):
        tri, negm = self.tri[d_], self.negm[d_]
        self.S.barrier()
        for g_ in range(G):
            o.memset("pool", hst[g_][:], 0.0, wr=[hst[g_]])
            o.memset("pool", hbf[g_][:], 0.0, wr=[hbf[g_]])
        for c in chunk_order(d_):
            xc, dtc, btc, ctc, bkc, yc = xcs.next(), dts.next(), bts.next(), cts.next(), bks.next(), ych.next()
            tsl = slice(c * 128, (c + 1) * 128)
            o.dma("sp", xc[:], xv[c], wr=[xc])
            o.dma("sp", dtc[:], dtv[c], wr=[dtc])
            o.dma("sp", btc[:], btv[:, :, tsl], wr=[btc])
            o.dma("sp", ctc[:], ctv[:, :, tsl], wr=[ctc])
            o.dma("sp", bkc[:], bkv[c], wr=[bkc])
            dtd = dtc[:, di * H:(di + 1) * H]
            o.tt("dve", da[:], dtd, aneg[:, di * H:(di + 1) * H], ALU.mult, rd=[dtc, aneg], wr=[da])
            o.mm(pmisc[:, 0:H], lhsT=tri[:], rhs=da[:], rd=[tri, da], wr=[pmisc])
            o.mm(pmisc[:, H:2 * H], lhsT=self.onesf[:], rhs=da[:], rd=[self.onesf, da], wr=[pmisc])
            o.cp("dve", acum[:], pmisc[:, 0:H], rd=[pmisc], wr=[acum])
            o.ts("dve", nacum[:], pmisc[:, 0:H], -1.0, None, ALU.mult, rd=[pmisc], wr=[nacum])
            o.tt("dve", wdec[:], pmisc[:, H:2 * H], acum[:], ALU.subtract, rd=[pmisc, acum], wr=[wdec])
            o.act(etot[:], pmisc[:, H:2 * H], AF.Exp, rd=[pmisc], wr=[etot])
            o.act(eA[:], acum[:], AF.Exp, rd=[acum], wr=[eA])
            o.act(wdec[:], wdec[:], AF.Exp, rd=[wdec], wr=[wdec])
            o.tt("dve", wdec[:], wdec[:], dtd, ALU.mult, rd=[wdec, dtc], wr=[wdec])
            o.tt("dve", xdt[:], xc[:], dtd.unsqueeze(2).to_broadcast([128, H, P]), ALU.mult, rd=[xc, dtc], wr=[xdt])
            o.tt("pool", xw[:], xc[:], wdec[:].unsqueeze(2).to_broadcast([128, H, P]), ALU.mult, rd=[xc, wdec], wr=[xw])
            st_ = {}

            def emit_R(g_, e4):
                if e4 == 0:
                    cb_ = cbs.next()
                    o.mm(pcb[:, 0:128], lhsT=btc[:, g_, :], rhs=ctc[:, g_, :], rd=[btc, ctc], wr=[pcb])
                    o.cp("act", cb_[:], pcb[:, 0:128], rd=[pcb], wr=[cb_])
                    st_[("cb", g_)] = cb_
                pr = pR.next()
                for q in range(4):
                    h = g_ * E + e4 + q
                    o.mm(pr[:, q, :], lhsT=da[:, h:h + 1].to_broadcast([128, 128]), rhs=tri[:], start=True, stop=False,
                         rd=[da, tri], wr=[pr])
                    o.mm(pr[:, q, :], lhsT=self.identf[:], rhs=negm[:], start=False, stop=True,
                         rd=[self.identf, negm], wr=[pr])
                return pr

            def emit_rest(g_, e4, pr):
                if e4 == 0:
                    st_[("py", g_)] = pys.next()
                py, cb_ = st_[("py", g_)], st_[("cb", g_)]
                for q in range(4):
                    h = g_ * E + e4 + q
                    E_, LT = Es.next(), LTs.next()
                    o.act(E_[:], pr[:, q, :], AF.Exp, rd=[pr, nacum], wr=[E_], bias=nacum[:, h:h + 1], scale=1.0)
                    o.tt("dve", LT[:], E_[:], cb_[:], ALU.mult, rd=[E_, cb_], wr=[LT])
                    o.mm(py[:, (e4 + q) * P:(e4 + q + 1) * P], lhsT=LT[:], rhs=xdt[:, h, :], rd=[LT, xdt], wr=[py])
                if e4 == 0:
                    return
                o.mm(pYg[:], lhsT=ctc[:, g_, :], rhs=hbf[g_][:], rd=[ctc, hbf[g_]], wr=[pYg])
                tmp = tmps.next()
                o.tt("dve", tmp[:], pYg[:].rearrange("p (e q) -> p e q", q=P),
                     eA[:, g_ * E:(g_ + 1) * E].unsqueeze(2).to_broadcast([128, E, P]), ALU.mult, rd=[pYg, eA], wr=[tmp])
                o.tt("dve", yc[:, g_ * E:(g_ + 1) * E, :], tmp[:], py[:].rearrange("p (e q) -> p e q", q=P), ALU.add,
                     rd=[tmp, py], wr=[yc])
                o.mm(pHn[:], lhsT=bkc[:, g_ * 128:(g_ + 1) * 128], rhs=xw[:, g_ * E:(g_ + 1) * E, :].rearrange("p e q -> p (e q)"),
                     rd=[bkc, xw], wr=[pHn])
                o.tt("pool", hst[g_][:].rearrange("p (e q) -> p e q", q=P), hst[g_][:].rearrange("p (e q) -> p e q", q=P),
                     etot[:, g_ * E:(g_ + 1) * E].unsqueeze(2).to_broadcast([128, E, P]), ALU.mult, rd=[hst[g_], etot], wr=[hst[g_]])
                o.tt("dve", hst[g_][:], hst[g_][:], pHn[:], ALU.add, rd=[hst[g_], pHn], wr=[hst[g_]])
                o.cp("act", hbf[g_][:], hst[g_][:], rd=[hst[g_]], wr=[hbf[g_]])

            batches = [(g_, e4) for g_ in range(G) for e4 in (0, 4)]
            pend = emit_R(*batches[0])
            for bi, (g_, e4) in enumerate(batches):
                nxt = emit_R(*batches[bi + 1]) if bi + 1 < len(batches) else None
                emit_rest(g_, e4, pend)
                pend = nxt
            if d_ == "f":
                o.dma("sp", yfv[c], yc[:], rd=[yc])
                continue
            o.dma("sp", yf[:], yfv[c], wr=[yf])
            o.dma("sp", szc[:], szv[c], wr=[szc])
            o.tt("pool", yc[:], yc[:], yf[:], ALU.add, rd=[yc, yf], wr=[yc])
            o.tt("pool", yf[:], xc[:], dsk[:].unsqueeze(2).to_broadcast([128, H, P]), ALU.mult, rd=[xc, dsk], wr=[yf])
            o.tt("pool", yc[:], yc[:], yf[:], ALU.add, rd=[yc, yf], wr=[yc])
            ycf = yc[:].rearrange("p h q -> p (h q)")
            o.tt("dve", ycf, ycf, szc[:], ALU.mult, rd=[yc, szc], wr=[yc])
            o.act(junk[:], ycf, AF.Square, rd=[yc], wr=[junk, ssq], accum_out=ssq[:, 0:1])
            o.act(rstd[:], ssq[:], AF.Ln, rd=[ssq], wr=[rstd], bias=self.eps_t[:, 0:1], scale=1.0 / SSD_DI)
            o.act(rstd[:], rstd[:], AF.Exp, rd=[rstd], wr=[rstd], scale=-0.5)
            o.stt("dve", un[:], ycf, rstd[:, 0:1], nwb[:], ALU.mult, ALU.mult, rd=[yc, rstd, nwb], wr=[un])
            st = zst.next()
            for c8 in range(0, 32, 8):
                pt = pys.next()
                ptb = pt[:].bitcast(BF16).rearrange("p (a b) -> p a b", b=128)
                for q in range(8):
                    o.tr(ptb[:, q, :], un[:, (c8 + q) * 128:(c8 + q + 1) * 128], self.ident[:], rd=[un, self.ident], wr=[pt])
                o.cp("dve", st[:, c8:c8 + 8, :], ptb[:, 0:8, :], rd=[pt], wr=[st])
            o.dma("sp", zv[:, :, tsl], st[:], rd=[st])
    p.close()


DN_HK, DN_HV, DN_DV = 16, 32, 4096


def dn_layer(self, li, j, s, src, dst, dst_off, need_ctx):
    o = self.o
    scan_consts(self)
    if s == 0:
        self.dw = dict(
            wz=self.inp(f"dn_wz{j}", [8, 128, KC, 512]),
            wx=self.inp(f"dn_wx{j}", [64, 128, KC, 128]),
            wab=self.inp(f"dn_wab{j}", [1, 128, KC, 128]),
            cw=self.inp(f"dn_cw{j}", [128, 64, 5]),
            dtb=self.inp(f"dn_dtb{j}", [64]),
            alog=self.inp(f"dn_alog{j}", [64]),
            nw=self.inp(f"dn_nw{j}", [128]),
            wo=self.inp(f"dn_wo{j}", [128, 32, D]),
        )
        self.dsz_d = self.scratch("dn_sz", [NTOK, DN_DV], BF16)
        self.dqT_d = self.scratch("dn_qT", [2048, NTOK], BF16)
        self.dkT_d = self.scratch("dn_kT", [2048, NTOK], BF16)
        self.dk_d = self.scratch("dn_k", [NTOK, 2048], BF16)
        self.dv_d = self.scratch("dn_v", [NTOK, DN_DV], BF16)
        self.dgb_d = self.scratch("dn_gb", [NTOK, 128], F32)
        self.dof_d = self.scratch("dn_of", [NTOK, DN_DV], F32)
    dw = self.dw
    p = self.ctx.sub()
    hT = p.sb([128, KC, NTOK], BF16, "hT")
    p1 = self.ctx.sub()
    self.phase1(p1, s, src, hT)
    p1.close()
    pk = self.ctx.sub()
    zst = Rot([pk.sb([128, 512], BF16, "zst") for _ in range(3)])

    def z_epi(gi, tt, ps):
        st = zst.next()
        o.act(st[:], ps[:], AF.Silu, rd=[ps], wr=[st])
        o.dma("sp", self.dsz_d[tt * 128:(tt + 1) * 128, gi * 512:(gi + 1) * 512], st[:], rd=[st])

    self.gemm_t(pk, hT, KC, dw["wz"], 8, z_epi)
    pk.close()
    pk = self.ctx.sub()
    dtb = pk.sb([128, 64], F32, "dtb")
    aneg = pk.sb([128, 64], F32, "aneg")
    o.dma("sp", dtb[:], dw["dtb"].partition_broadcast(128), wr=[dtb])
    o.dma("sp", aneg[:], dw["alog"].partition_broadcast(128), wr=[aneg])
    o.act(aneg[:], aneg[:], AF.Exp, rd=[aneg], wr=[aneg])
    o.ts("dve", aneg[:], aneg[:], -1.0, None, ALU.mult, rd=[aneg], wr=[aneg])
    gst = Rot([pk.sb([128, 128], F32, "gst") for _ in range(3)])

    def ab_epi(gi, tt, ps):
        st = gst.next()
        o.tt("dve", st[:, 0:64], ps[:, 0:64], dtb[:], ALU.add, rd=[ps, dtb], wr=[st])
        o.act(st[:, 0:64], st[:, 0:64], AF.Exp, rd=[st], wr=[st])
        o.act(st[:, 0:64], st[:, 0:64], AF.Ln, rd=[st], wr=[st], bias=1.0, scale=1.0)
        o.tt("dve", st[:, 0:64], st[:, 0:64], aneg[:], ALU.mult, rd=[st, aneg], wr=[st])
        o.act(st[:, 64:128], ps[:, 64:128], AF.Sigmoid, rd=[ps], wr=[st])
        o.dma("sp", self.dgb_d[tt * 128:(tt + 1) * 128, :], st[:], rd=[st])

    self.gemm_t(pk, hT, KC, dw["wab"], 1, ab_epi, gw=128)
    pk.close()
    pk = self.ctx.sub()
    cw = pk.sb([128, 64, 5], F32, "cw")
    cb = pk.sb([128, 64], F32, "cb")
    o.dma("sp", cw[:], dw["cw"][:, :, :], wr=[cw])
    o.memset("pool", cb[:], 0.0, wr=[cb])
    sqr = pk.sb([128, NTOK], BF16, "sqr")
    rst = pk.sb([128, NTOK], F32, "rst")
    pl2 = Rot([pk.ps([128, 512], F32, "pl2") for _ in range(2)])

    def post(oc, acc, row):
        if oc >= 32:
            o.cp("act", row[:], acc[:], rd=[acc], wr=[row])
            return
        o.act(sqr[:], acc[:], AF.Square, rd=[acc], wr=[sqr])
        for t0 in range(0, NTOK, 512):
            w = min(512, NTOK - t0)
            ps = pl2.next()
            o.mm(ps[:, :w], lhsT=self.ones[:], rhs=sqr[:, t0:t0 + w], rd=[self.ones, sqr], wr=[ps])
            o.act(rst[:, t0:t0 + w], ps[:, :w], AF.Ln, rd=[ps], wr=[rst], bias=self.eps_t[:, 0:1], scale=1.0)
        o.act(rst[:], rst[:], AF.Exp, rd=[rst], wr=[rst], scale=-0.5)
        sc = (128 ** -0.5) if oc < 16 else 1.0
        o.stt("dve", row[:], acc[:], sc, rst[:], ALU.mult, ALU.mult, rd=[acc, rst], wr=[row])

    self.conv_gemm(pk, hT, dw["wx"], 64, cw, cb,
                   tok_dst=lambda oc: None if oc < 16 else ((self.dk_d, (oc - 16) * 128) if oc < 32 else (self.dv_d, (oc - 32) * 128)),
                   fm_dst=lambda oc: (self.dqT_d, oc * 128) if oc < 16 else ((self.dkT_d, (oc - 16) * 128) if oc < 32 else None),
                   post=post)
    pk.close()
    p.close()
    dn_scan_phase(self, s)
    self.phase4(s, src, dst, dst_off, need_ctx, 32, dw["wo"])


Prog.dn_layer = dn_layer


def dn_scan_phase(self, s):
    o = self.o
    dw = self.dw
    p = self.ctx.sub()
    HV, HK = DN_HV, DN_HK
    nwb = p.sb([128, 128], F32, "dnw")
    o.dma("sp", nwb[:], dw["nw"].partition_broadcast(128), wr=[nwb])
    Sf = [p.sb([128, 128], F32, f"Sf{h}") for h in range(HV)]
    Sb = [p.sb([128, 128], BF16, f"Sb{h}") for h in range(HV)]
    gbs = Rot([p.sb([128, 128], F32, "gbc") for _ in range(2)])
    kTs = Rot([p.sb([128, HK, 128], BF16, "kTc") for _ in range(2)])
    qTs = Rot([p.sb([128, HK, 128], BF16, "qTc") for _ in range(2)])
    kts = Rot([p.sb([128, HK, 128], BF16, "ktk") for _ in range(2)])
    vcs = Rot([p.sb([128, HV, 128], BF16, "vch") for _ in range(2)])
    ochs = Rot([p.sb([128, HV, 128], F32, "och") for _ in range(2)])
    sm = {n: p.sb([128, HV], F32, "sm_" + n) for n in ("gc", "ngc", "gcum", "ngcum", "eg", "egl", "etot", "nbeta", "bg", "beta")}
    of = p.sb([128, HV, 128], F32, "of")
    szc = p.sb([128, HV, 128], BF16, "szc")
    ssq = p.sb([128, HV], F32, "ssqd")
    rstd = p.sb([128, HV], F32, "rstdd")
    un = p.sb([128, HV, 128], BF16, "und")
    junk = un
    zst = Rot([p.sb([128, 32, 128], BF16, "zstd") for _ in range(2)])

    class Slot:
        pass

    slots = []
    NSLOT = 2
    for w_ in range(NSLOT):
        T = Slot()
        T.psq = p.ps([128, 2, 2, 128], F32, "psq")
        T.pch = p.ps([128, 2, 2, 128], F32, "pch")
        T.pmx = p.ps([128, 2, 2, 128], F32, "pmx")
        T.pst = p.ps([128, 2, 2, 128], F32, "pst")
        T.kq = p.sb([128, 2, 128], F32, "kq")
        T.EL = p.sb([128, 2, 128], F32, "EL")
        T.EAT = p.sb([128, 2, 128], F32, "EAT")
        T.P = [p.sb([128, 2, 2, 128], F32, "Pm") for _ in range(2)]
        T.TTf = p.sb([128, 2, 128], F32, "TTf")
        T.TT = p.sb([128, 2, 128], F32, "TT")
        T.AT = p.sb([128, 2, 128], BF16, "AT")
        T.N = p.sb([128, 2, 128], F32, "Nn")
        T.NO = p.sb([128, 2, 128], F32, "NO")
        T.X0 = p.sb([128, 2, 128], F32, "X0")
        T.MM = p.sb([128, 2, 2, 128], F32, "MM")
        T.IMT = p.sb([128, 2, 128], F32, "IMT")
        T.M2 = p.sb([128, 2, 128], F32, "M2")
        T.W = p.sb([128, 2, 128], F32, "Wm")
        T.vb = p.sb([128, 2, 128], F32, "vb")
        T.kbg = p.sb([128, 2, 128], F32, "kbg")
        T.kdec = p.sb([128, 2, 128], BF16, "kdec")
        T.us = p.sb([128, 2, 128], F32, "us")
        T.wT = p.sb([128, 2, 128], BF16, "wTb")
        T.vn = p.sb([128, 2, 128], BF16, "vn")
        T.o1 = p.sb([128, 2, 128], F32, "o1")
        slots.append(T)

    gbv = self.dgb_d.rearrange("(c p) n -> c p n", p=128)
    kTv = self.dkT_d.rearrange("(h d) t -> d h t", d=128)
    qTv = self.dqT_d.rearrange("(h d) t -> d h t", d=128)
    ktv = self.dk_d.rearrange("(c p) (h d) -> c p h d", p=128, d=128)
    vv = self.dv_d.rearrange("(c p) (h d) -> c p h d", p=128, d=128)
    ofv = self.dof_d.rearrange("(c p) (h d) -> c p h d", p=128, d=128)
    szv = self.dsz_d.rearrange("(c p) (h d) -> c p h d", p=128, d=128)
    zv = self.zT.rearrange("(c p) t -> p c t", p=128)
    bc2 = lambda ap: ap.unsqueeze(1).to_broadcast([128, 2, 128])

    def unit(T, hk, d_, kTc, qTc, ktk, vch, och):
        tri, negm = self.tri[d_], self.negm[d_]
        negsL = self.negs["b" if d_ == "f" else "f"]
        hv0 = 2 * hk
        o.mm(T.pmx[:, 0, 0, :], lhsT=kTc[:, hk, :], rhs=kTc[:, hk, :], rd=[kTc], wr=[T.pmx])
        o.mm(T.pmx[:, 0, 1, :], lhsT=kTc[:, hk, :], rhs=qTc[:, hk, :], rd=[kTc, qTc], wr=[T.pmx])
        o.cp("act", T.kq[:], T.pmx[:, 0, :, :], rd=[T.pmx], wr=[T.kq])
        for r in range(2):
            hv = hv0 + r
            o.mm(T.psq[:, r, 0, :], lhsT=sm["ngc"][:, hv:hv + 1].to_broadcast([128, 128]), rhs=tri[:], start=True, stop=False,
                 rd=[sm["ngc"], tri], wr=[T.psq])
            o.mm(T.psq[:, r, 0, :], lhsT=self.identf[:], rhs=negsL[:], start=False, stop=True, rd=[self.identf, negsL], wr=[T.psq])
            o.mm(T.psq[:, r, 1, :], lhsT=sm["gc"][:, hv:hv + 1].to_broadcast([128, 128]), rhs=tri[:], start=True, stop=False,
                 rd=[sm["gc"], tri], wr=[T.psq])
            o.mm(T.psq[:, r, 1, :], lhsT=self.identf[:], rhs=negm[:], start=False, stop=True, rd=[self.identf, negm], wr=[T.psq])
        for r in range(2):
            hv = hv0 + r
            o.act(T.EL[:, r, :], T.psq[:, r, 0, :], AF.Exp, rd=[T.psq, sm["gcum"]], wr=[T.EL], bias=sm["gcum"][:, hv:hv + 1], scale=1.0)
            o.act(T.EAT[:, r, :], T.psq[:, r, 1, :], AF.Exp, rd=[T.psq, sm["ngcum"]], wr=[T.EAT], bias=sm["ngcum"][:, hv:hv + 1], scale=1.0)
        for r in range(2):
            hv = hv0 + r
            o.stt("dve", T.N[:, r, :], T.kq[:, 0, :], sm["nbeta"][:, hv:hv + 1], T.EL[:, r, :], ALU.mult, ALU.mult,
                  rd=[T.kq, sm["nbeta"], T.EL], wr=[T.N])
        o.tt("dve", T.AT[:], bc2(T.kq[:, 1, :]), T.EAT[:], ALU.mult, rd=[T.kq, T.EAT], wr=[T.AT])
        yield
        ND = T.P[0]
        o.tt("dve", ND[:, :, 0, :], T.N[:], bc2(self.bd[:]), ALU.mult, rd=[T.N, self.bd], wr=[ND])
        o.tt("pool", T.NO[:], T.N[:], ND[:, :, 0, :], ALU.subtract, rd=[T.N, ND], wr=[T.NO])
        for r in range(2):
            o.tr(T.pch[:, r, 0, :], ND[:, r, 0, :], self.identf[:], rd=[ND, self.identf], wr=[T.pch])
        o.cp("act", ND[:, :, 1, :], T.pch[:, :, 0, :], rd=[T.pch], wr=[ND])
        o.tt("dve", T.TTf[:], T.pch[:, :, 0, :], bc2(self.identf[:]), ALU.add, rd=[T.pch, self.identf], wr=[T.TTf])
        yield
        Pc = ND
        NL = 4
        for k in range(1, NL + 1):
            Pn = T.P[k % 2]
            for r in range(2):
                o.mm(T.psq[:, r, 0, :], lhsT=Pc[:, r, 1, :], rhs=Pc[:, r, 0, :], rd=[Pc], wr=[T.psq])
                if k < NL:
                    o.mm(T.psq[:, r, 1, :], lhsT=Pc[:, r, 0, :], rhs=Pc[:, r, 1, :], rd=[Pc], wr=[T.psq])
            if k < NL:
                o.cp("act" if k % 2 else "dve", Pn[:], T.psq[:], rd=[T.psq], wr=[Pn])
            else:
                o.cp("act", Pn[:, :, 0, :], T.psq[:, :, 0, :], rd=[T.psq], wr=[Pn])
            yield
            for r in range(2):
                o.mm(T.pch[:, r, 0, :], lhsT=Pn[:, r, 0, :], rhs=T.TTf[:, r, :], rd=[Pn, T.TTf], wr=[T.pch])
            o.tt("dve", T.TTf[:], T.TTf[:], T.pch[:, :, 0, :], ALU.add, rd=[T.TTf, T.pch], wr=[T.TTf])
            Pc = Pn
            yield
        X0T = T.TTf
        for r in range(2):
            o.tr(T.pch[:, r, 0, :], X0T[:, r, :], self.identf[:], rd=[X0T, self.identf], wr=[T.pch])
        o.cp("act", T.X0[:], T.pch[:, :, 0, :], rd=[T.pch], wr=[T.X0])
        for r in range(2):
            o.mm(T.psq[:, r, 0, :], lhsT=X0T[:, r, :], rhs=T.NO[:, r, :], rd=[X0T, T.NO], wr=[T.psq])
            o.mm(T.psq[:, r, 1, :], lhsT=T.NO[:, r, :], rhs=X0T[:, r, :], rd=[X0T, T.NO], wr=[T.psq])
        o.cp("dve", T.MM[:], T.psq[:], rd=[T.psq], wr=[T.MM])
        o.tt("dve", T.IMT[:], T.psq[:, :, 1, :], bc2(self.identf[:]), ALU.add, rd=[T.psq, self.identf], wr=[T.IMT])
        yield
        for r in range(2):
            o.mm(T.pst[:, r, 0, :], lhsT=T.MM[:, r, 1, :], rhs=T.MM[:, r, 0, :], rd=[T.MM], wr=[T.pst])
        o.cp("act", T.M2[:], T.pst[:, :, 0, :], rd=[T.pst], wr=[T.M2])
        yield
        for r in range(2):
            o.mm(T.pch[:, r, 0, :], lhsT=T.MM[:, r, 0, :], rhs=T.MM[:, r, 1, :], start=True, stop=False, rd=[T.MM], wr=[T.pch])
            o.mm(T.pch[:, r, 0, :], lhsT=T.M2[:, r, :], rhs=T.MM[:, r, 1, :], start=False, stop=True, rd=[T.MM, T.M2], wr=[T.pch])
        o.tt("dve", T.W[:], T.IMT[:], T.pch[:, :, 0, :], ALU.add, rd=[T.IMT, T.pch], wr=[T.W])
        yield
        for r in range(2):
            o.mm(T.psq[:, r, 0, :], lhsT=T.X0[:, r, :], rhs=T.W[:, r, :], rd=[T.X0, T.W], wr=[T.psq])
        o.cp("act", T.TT[:], T.psq[:, :, 0, :], rd=[T.psq], wr=[T.TT])
        yield
        TT = T.TT
        for r in range(2):
            hv = hv0 + r
            o.ts("pool", T.vb[:, r, :], vch[:, hv, :], sm["beta"][:, hv:hv + 1], None, ALU.mult, rd=[vch, sm["beta"]], wr=[T.vb])
            o.ts("pool", T.kbg[:, r, :], ktk[:, hk, :], sm["bg"][:, hv:hv + 1], None, ALU.mult, rd=[ktk, sm["bg"]], wr=[T.kbg])
            o.ts("pool", T.kdec[:, r, :], ktk[:, hk, :], sm["egl"][:, hv:hv + 1], None, ALU.mult, rd=[ktk, sm["egl"]], wr=[T.kdec])
        for r in range(2):
            o.mm(T.pmx[:, r, 0, :], lhsT=TT[:, r, :], rhs=T.vb[:, r, :], rd=[TT, T.vb], wr=[T.pmx])
            o.mm(T.pmx[:, r, 1, :], lhsT=T.kbg[:, r, :], rhs=TT[:, r, :], rd=[TT, T.kbg], wr=[T.pmx])
        o.cp("act", T.us[:], T.pmx[:, :, 0, :], rd=[T.pmx], wr=[T.us])
        o.cp("dve", T.wT[:], T.pmx[:, :, 1, :], rd=[T.pmx], wr=[T.wT])
        yield
        for r in range(2):
            hv = hv0 + r
            o.mm(T.pst[:, r, 0, :], lhsT=T.wT[:, r, :], rhs=Sb[hv][:], rd=[T.wT, Sb[hv]], wr=[T.pst])
            o.mm(T.pst[:, r, 1, :], lhsT=qTc[:, hk, :], rhs=Sb[hv][:], rd=[qTc, Sb[hv]], wr=[T.pst])
        o.tt("dve", T.vn[:], T.us[:], T.pst[:, :, 0, :], ALU.subtract, rd=[T.us, T.pst], wr=[T.vn])
        for r in range(2):
            hv = hv0 + r
            o.act(T.o1[:, r, :], T.pst[:, r, 1, :], AF.Copy, rd=[T.pst, sm["eg"]], wr=[T.o1], scale=sm["eg"][:, hv:hv + 1])
        yield
        for r in range(2):
            o.mm(T.pmx[:, r, 0, :], lhsT=T.AT[:, r, :], rhs=T.vn[:, r, :], rd=[T.AT, T.vn], wr=[T.pmx])
            o.mm(T.pmx[:, r, 1, :], lhsT=T.kdec[:, r, :], rhs=T.vn[:, r, :], rd=[T.kdec, T.vn], wr=[T.pmx])
        o.tt("dve", och[:, hv0:hv0 + 2, :], T.o1[:], T.pmx[:, :, 0, :], ALU.add, rd=[T.o1, T.pmx], wr=[och])
        for r in range(2):
            hv = hv0 + r
            o.stt("dve", Sf[hv][:], Sf[hv][:], sm["etot"][:, hv:hv + 1], T.pmx[:, r, 1, :], ALU.mult, ALU.add,
                  rd=[Sf[hv], sm["etot"], T.pmx], wr=[Sf[hv]])
            o.cp("pool", Sb[hv][:], Sf[hv][:], rd=[Sf[hv]], wr=[Sb[hv]])
        yield

    for di, d_ in enumerate(("f", "b")):
        tri = self.tri[d_]
        self.S.barrier()
        for hv in range(HV):
            o.memset("pool", Sf[hv][:], 0.0, wr=[Sf[hv]])
            o.memset("pool", Sb[hv][:], 0.0, wr=[Sb[hv]])
        for c in chunk_order(d_):
            tsl = slice(c * 128, (c + 1) * 128)
            gbc, kTc, qTc, ktk, vch, och = gbs.next(), kTs.next(), qTs.next(), kts.next(), vcs.next(), ochs.next()
            o.dma("sp", gbc[:], gbv[c], wr=[gbc])
            o.dma("sp", kTc[:], kTv[:, :, tsl], wr=[kTc])
            o.dma("sp", qTc[:], qTv[:, :, tsl], wr=[qTc])
            o.dma("sp", ktk[:], ktv[c], wr=[ktk])
            o.dma("sp", vch[:], vv[c], wr=[vch])
            g_in = gbc[:, di * HV:(di + 1) * HV]
            b_in = gbc[:, 64 + di * HV:64 + (di + 1) * HV]
            pm = slots[0].pst
            o.cp("dve", sm["gc"][:], g_in, rd=[gbc], wr=[sm["gc"]])
            o.ts("dve", sm["ngc"][:], g_in, -1.0, None, ALU.mult, rd=[gbc], wr=[sm["ngc"]])
            o.cp("dve", sm["beta"][:], b_in, rd=[gbc], wr=[sm["beta"]])
            o.ts("dve", sm["nbeta"][:], b_in, -1.0, None, ALU.mult, rd=[gbc], wr=[sm["nbeta"]])
            o.mm(pm[:, 0, 0, 0:HV], lhsT=tri[:], rhs=sm["gc"][:], rd=[tri, sm["gc"]], wr=[pm])
            o.mm(pm[:, 0, 0, HV:2 * HV], lhsT=self.onesf[:], rhs=sm["gc"][:], rd=[self.onesf, sm["gc"]], wr=[pm])
            o.cp("dve", sm["gcum"][:], pm[:, 0, 0, 0:HV], rd=[pm], wr=[sm["gcum"]])
            o.ts("dve", sm["ngcum"][:], pm[:, 0, 0, 0:HV], -1.0, None, ALU.mult, rd=[pm], wr=[sm["ngcum"]])
            o.tt("dve", sm["egl"][:], pm[:, 0, 0, HV:2 * HV], sm["gcum"][:], ALU.subtract, rd=[pm, sm["gcum"]], wr=[sm["egl"]])
            o.act(sm["etot"][:], pm[:, 0, 0, HV:2 * HV], AF.Exp, rd=[pm], wr=[sm["etot"]])
            o.act(sm["eg"][:], sm["gcum"][:], AF.Exp, rd=[sm["gcum"]], wr=[sm["eg"]])
            o.act(sm["egl"][:], sm["egl"][:], AF.Exp, rd=[sm["egl"]], wr=[sm["egl"]])
            o.tt("dve", sm["bg"][:], sm["eg"][:], sm["beta"][:], ALU.mult, rd=[sm["eg"], sm["beta"]], wr=[sm["bg"]])
            for hk0 in range(0, HK, NSLOT):
                gens = [unit(slots[w_], hk0 + w_, d_, kTc, qTc, ktk, vch, och) for w_ in range(min(NSLOT, HK - hk0))]
                alive = list(gens)
                while alive:
                    for g_ in list(alive):
                        try:
                            next(g_)
                        except StopIteration:
                            alive.remove(g_)
            if d_ == "f":
                o.dma("sp", ofv[c], och[:], rd=[och])
                continue
            o.dma("sp", of[:], ofv[c], wr=[of])
            o.dma("sp", szc[:], szv[c], wr=[szc])
            o.tt("pool", och[:], och[:], of[:], ALU.add, rd=[och, of], wr=[och])
            o.act(junk[:], och[:], AF.Square, rd=[och], wr=[junk])
            o.S.op("dve", lambda e: e.tensor_reduce(out=ssq[:], in_=junk[:], axis=AX.X, op=ALU.add), R_([junk]), R_([ssq]))
            o.act(rstd[:], ssq[:], AF.Ln, rd=[ssq], wr=[rstd], bias=self.eps_t[:, 0:1], scale=1.0 / 128)
            o.act(rstd[:], rstd[:], AF.Exp, rd=[rstd], wr=[rstd], scale=-0.5)
            o.tt("dve", och[:], och[:], rstd[:].unsqueeze(2).to_broadcast([128, HV, 128]), ALU.mult, rd=[och, rstd], wr=[och])
            o.tt("pool", och[:], och[:], nwb[:].unsqueeze(1).to_broadcast([128, HV, 128]), ALU.mult, rd=[och, nwb], wr=[och])
            o.tt("dve", un[:], och[:], szc[:], ALU.mult, rd=[och, szc], wr=[un])
            st = zst.next()
            for c8 in range(0, 32, 8):
                T = slots[(c8 // 8) % 2]
                ptb = T.psq[:].rearrange("p a b c -> p (a b c)").bitcast(BF16).rearrange("p (a b) -> p a b", b=128)
                for q in range(8):
                    o.tr(ptb[:, q, :], un[:, c8 + q, :], self.ident[:], rd=[un, self.ident], wr=[T.psq])
                o.cp("dve", st[:, c8:c8 + 8, :], ptb[:, 0:8, :], rd=[T.psq], wr=[st])
            o.dma("sp", zv[:, :, tsl], st[:], rd=[st])
    p.close()
```

```python
import numpy as np
from contextlib import ExitStack
import concourse.bass as bass
import concourse.mybir as mybir
from concourse.bass_utils import run_bass_kernel_spmd

F32 = mybir.dt.float32
BF16 = mybir.dt.bfloat16
AF = mybir.ActivationFunctionType
ALU = mybir.AluOpType
AX = mybir.AxisListType


class Res:
    __slots__ = ("name", "lw", "rd", "excl")

    def __init__(self, name=""):
        self.name = name
        self.excl = False
        self.lw = None
        self.rd = []


class Sched:
    ENGS = ("pe", "act", "dve", "pool", "sp")
    N_DMA_SEMS = 12

    def __init__(self, nc, stack):
        self.nc = nc
        self.eng = {"pe": nc.tensor, "act": nc.scalar, "dve": nc.vector,
                    "pool": nc.gpsimd, "sp": nc.sync}
        self.sem = {e: stack.enter_context(nc.semaphore("s_" + e)) for e in self.ENGS}
        self.cnt = {e: 0 for e in self.ENGS}
        self.dsem = {q: [stack.enter_context(nc.semaphore(f"d_{q}{i}")) for i in range(self.N_DMA_SEMS)]
                     for q in ("sp", "pool", "act")}
        self.dval = {q: [0] * self.N_DMA_SEMS for q in self.dsem}
        self.drr = {q: 0 for q in self.dsem}
        self.waited = {e: {} for e in self.ENGS}
        self.semobj = {}
        self.out_tokens = []
        self.nwaits = 0
        self.nops = 0

    def _wait(self, e, tok):
        key, val = tok
        w = self.waited[e]
        if w.get(key, 0) >= val:
            return
        w[key] = val
        self.eng[e].wait_ge(self.semobj[key], val)
        self.nwaits += 1

    def _deps(self, e, reads, writes):
        toks = {}
        def add(t):
            if t is None:
                return
            k, v = t
            if toks.get(k, 0) < v:
                toks[k] = v
        for r in reads:
            add(r.lw)
            if r.excl:
                for t in r.rd:
                    if t[0] != "c_" + e:
                        add(t)
        for w in writes:
            add(w.lw)
            for t in w.rd:
                add(t)
        for k, v in toks.items():
            if e == "pe" and k == "c_pe":
                continue
            self._wait(e, (k, v))

    def _commit(self, tok, reads, writes):
        for r in reads:
            r.rd.append(tok)
            if len(r.rd) > 64:
                m = {}
                for k, v in r.rd:
                    if m.get(k, 0) < v:
                        m[k] = v
                r.rd = list(m.items())
        for w in writes:
            w.lw = tok
            w.rd = []

    def op(self, e, fn, reads=(), writes=()):
        self._deps(e, reads, writes)
        ins = fn(self.eng[e])
        self.cnt[e] += 1
        key = "c_" + e
        self.semobj[key] = self.sem[e]
        ins.then_inc(self.sem[e], 1)
        tok = (key, self.cnt[e])
        self._commit(tok, reads, writes)
        self.nops += 1
        return tok

    def dma(self, q, out, in_, reads=(), writes=(), is_output=False, **kw):
        self._deps(q, reads, writes)
        i = self.drr[q]
        self.drr[q] = (i + 1) % self.N_DMA_SEMS
        key = f"d_{q}{i}"
        self.semobj[key] = self.dsem[q][i]
        if self.dval[q][i] > 0:
            self._wait(q, (key, self.dval[q][i]))
        ins = self.eng[q].dma_start(out=out, in_=in_, **kw)
        self.dval[q][i] += 16
        ins.then_inc(self.dsem[q][i], 16)
        tok = (key, self.dval[q][i])
        self._commit(tok, reads, writes)
        if is_output:
            self.out_tokens.append(tok)
        self.nops += 1
        return tok

    def barrier(self):
        toks = [("c_" + e, self.cnt[e]) for e in self.ENGS if self.cnt[e] > 0]
        for q in self.dsem:
            for i in range(self.N_DMA_SEMS):
                if self.dval[q][i] > 0:
                    toks.append((f"d_{q}{i}", self.dval[q][i]))
        for e in self.ENGS:
            for t in toks:
                if t[0] == "c_" + e:
                    continue
                self._wait(e, t)

    def finish(self):
        self.barrier()


class Tl:
    __slots__ = ("t", "res")

    def __init__(self, t):
        self.t = t
        self.res = Res()

    def __getitem__(self, k):
        return self.t[k]


class StopBuild(Exception):
    pass


class Ctx:
    def __init__(self, nc, S, stack):
        self.nc, self.S, self.stack = nc, S, stack
        self.n = 0

    def sub(self):
        st = ExitStack()
        c = Ctx2(self, st)
        self.open = getattr(self, "open", [])
        self.open.append(c)
        return c


class Ctx2:
    def __init__(self, parent, st):
        self.nc, self.S = parent.nc, parent.S
        self.st = st
        self.parent = parent

    def sb(self, shape, dt, name=None):
        self.parent.n += 1
        return Tl(self.st.enter_context(self.nc.sbuf_tensor(f"{name or 't'}_{self.parent.n}", list(shape), dt)))

    def ps(self, shape, dt=F32, name=None):
        self.parent.n += 1
        nbytes = int(np.prod(shape[1:])) * (4 if dt == F32 else 2)
        assert nbytes == 2048, ("psum tiles must be exactly one bank", shape)
        t = Tl(self.st.enter_context(self.nc.psum_tensor(f"{name or 'p'}_{self.parent.n}", list(shape), dt)))
        t.res.excl = True
        return t

    def close(self):
        self.S.barrier()
        self.st.close()
        self.parent.open.remove(self)
        import os
        self.parent.nclose = getattr(self.parent, "nclose", 0) + 1
        if int(os.environ.get("STOP", "0")) == self.parent.nclose:
            raise StopBuild()


def R_(tiles):
    return [t.res for t in tiles]


class Ops:
    def __init__(self, S):
        self.S = S

    def mm(self, out, lhsT, rhs, start=True, stop=True, rd=(), wr=()):
        return self.S.op("pe", lambda e: e.matmul(out, lhsT=lhsT, rhs=rhs, start=start, stop=stop), R_(rd), R_(wr))

    def tr(self, out, in_, ident, rd=(), wr=()):
        return self.S.op("pe", lambda e: e.transpose(out=out, in_=in_, identity=ident), R_(rd), R_(wr))

    def act(self, out, in_, func, rd=(), wr=(), **kw):
        return self.S.op("act", lambda e: e.activation(out=out, in_=in_, func=func, **kw), R_(rd), R_(wr))

    def tt(self, eng, out, in0, in1, op, rd=(), wr=()):
        return self.S.op(eng, lambda e: e.tensor_tensor(out=out, in0=in0, in1=in1, op=op), R_(rd), R_(wr))

    def ts(self, eng, out, in0, s1, s2, op0, op1=None, rd=(), wr=()):
        if op1 is None:
            return self.S.op(eng, lambda e: e.tensor_scalar(out=out, in0=in0, scalar1=s1, scalar2=None, op0=op0), R_(rd), R_(wr))
        return self.S.op(eng, lambda e: e.tensor_scalar(out=out, in0=in0, scalar1=s1, scalar2=s2, op0=op0, op1=op1), R_(rd), R_(wr))

    def stt(self, eng, out, in0, scalar, in1, op0, op1, rd=(), wr=()):
        return self.S.op(eng, lambda e: e.scalar_tensor_tensor(out=out, in0=in0, scalar=scalar, in1=in1, op0=op0, op1=op1), R_(rd), R_(wr))

    def cp(self, eng, out, in_, rd=(), wr=()):
        if eng == "act":
            return self.S.op("act", lambda e: e.copy(out=out, in_=in_), R_(rd), R_(wr))
        return self.S.op(eng, lambda e: e.tensor_copy(out=out, in_=in_), R_(rd), R_(wr))

    def recip(self, out, in_, rd=(), wr=()):
        return self.S.op("dve", lambda e: e.reciprocal(out=out, in_=in_), R_(rd), R_(wr))

    def memset(self, eng, out, val, wr=()):
        return self.S.op(eng, lambda e: e.memset(out, val), (), R_(wr))

    def dma(self, q, out, in_, rd=(), wr=(), **kw):
        return self.S.dma(q, out, in_, R_(rd), R_(wr), **kw)


class Rot:
    def __init__(self, tiles):
        self.tiles, self.i = tiles, 0

    def next(self):
        t = self.tiles[self.i % len(self.tiles)]
        self.i += 1
        return t


D = 2048
KC = D // 128
CTXL = 256
SEQL = 2048
NTOK = CTXL + SEQL
NT = NTOK // 128
EPS = 1e-6
DEPTH = 4
N_HEADS, N_KV = 16, 4
ATT_SCALE = 128 ** -0.5


def seq_blocks():
    return [(0, CTXL, True)] + [(CTXL + 512 * i, 512, False) for i in range(SEQL // 512)]


class Prog:
    def __init__(self, n_seq, layers, full_out=False):
        self.n_seq, self.layers, self.full_out = n_seq, layers, full_out
        self.nrow = n_seq + 1
        nc = self.nc = bass.Bass("TRN2", target_bir_lowering=False)
        self.inputs = {}
        self.stack = ExitStack()
        self.S = Sched(nc, self.stack)
        self.o = Ops(self.S)
        self.ctx = Ctx(nc, self.S, self.stack)
        self.g = Ctx2(self.ctx, self.stack)

    def inp(self, name, shape, dt=F32):
        t = self.nc.dram_tensor(name, list(shape), dt, kind="ExternalInput").ap()
        self.inputs[name] = t
        return t

    def scratch(self, name, shape, dt):
        return self.nc.dram_tensor(name, list(shape), dt).ap()

    def build(self):
        nc, S, o, g = self.nc, self.S, self.o, self.g
        ns = self.n_seq
        self.xin = self.inp("xin", [ns, D, NTOK])
        self.cT_d = self.inp("cT", [128, KC, self.nrow])
        wout = NTOK if self.full_out else SEQL
        self.yout = nc.dram_tensor("yout", [ns, D, wout], F32, kind="ExternalOutput").ap()
        self.xs = self.scratch("xs", [ns, D, NTOK], F32)
        self.zT = self.scratch("zT", [4096, NTOK], BF16)
        self.ident_d = self.inp("ident", [128, 128])
        self.ident = g.sb([128, 128], BF16, "ident")
        o.dma("pool", self.ident[:], self.ident_d[:, :], wr=[self.ident])
        self.identf = g.sb([128, 128], F32, "identf")
        o.dma("sp", self.identf[:], self.ident_d[:, :], wr=[self.identf])
        self.ones = g.sb([128, 128], BF16, "ones")
        o.memset("pool", self.ones[:], 1.0, wr=[self.ones])
        self.onesf = g.sb([128, 128], F32, "onesf")
        o.memset("pool", self.onesf[:], 1.0, wr=[self.onesf])
        self.eps_t = g.sb([128, 1], F32, "eps")
        o.memset("pool", self.eps_t[:], EPS, wr=[self.eps_t])
        self.m4_t = g.sb([128, 1], F32, "m4")
        o.memset("pool", self.m4_t[:], -4.0, wr=[self.m4_t])
        self.sc = g.sb([128, KC, self.nrow], F32, "sc")
        o.dma("sp", self.sc[:], self.cT_d[:, :, :], wr=[self.sc])
        o.act(self.sc[:], self.sc[:], AF.Silu, rd=[self.sc], wr=[self.sc])
        self.A = g.sb([128, KC, self.nrow], F32, "modA")
        self.SH = g.sb([128, KC, self.nrow], F32, "modS")
        self.G = g.sb([128, KC, self.nrow], F32, "modG")
        try:
            self.build_layers()
        except StopBuild:
            for c in reversed(list(self.ctx.open)):
                c.st.close()
        S.finish()
        self.stack.close()
        return nc

    def build_layers(self):
        nc, S, o, g = self.nc, self.S, self.o, self.g
        ns = self.n_seq
        first = True
        for li in self.layers:
            kind, j = li % 3, li // 3
            need_ctx = li < DEPTH - 1 or self.full_out
            last = (li == self.layers[-1])
            self.modulation(li)
            for s in range(ns):
                src = self.xin[s] if first else self.xs[s]
                if last:
                    dst = self.yout[s]
                    dst_off = 0 if self.full_out else CTXL
                else:
                    dst, dst_off = self.xs[s], 0
                if kind == 0:
                    self.attn_layer(li, j, s, src, dst, dst_off, need_ctx)
                elif kind == 1:
                    self.ssd_layer(li, j, s, src, dst, dst_off, need_ctx)
                else:
                    self.dn_layer(li, j, s, src, dst, dst_off, need_ctx)
            first = False

    def modulation(self, li):
        o, S = self.o, self.S
        nr = self.nrow
        p = self.ctx.sub()
        w_d = self.inp(f"ada_w{li}", [12, 128, 4, KC, 128])
        b_d = self.inp(f"ada_b{li}", [128, 48])
        pre_d = self.inp(f"pre_w{li}", [128, KC])
        post_d = self.inp(f"post_w{li}", [128, KC])
        bt = p.sb([128, 48], F32)
        pre = p.sb([128, KC], F32)
        post = p.sb([128, KC], F32)
        o.dma("sp", bt[:], b_d[:, :], wr=[bt])
        o.dma("sp", pre[:], pre_d[:, :], wr=[pre])
        o.dma("sp", post[:], post_d[:, :], wr=[post])
        wts = Rot([p.sb([128, 4, KC, 128], F32) for _ in range(2)])
        pm = p.ps([128, 128, 4], F32)
        for og in range(12):
            wt = wts.next()
            o.dma("sp", wt[:], w_d[og], wr=[wt])
            for jj in range(4):
                oc = og * 4 + jj
                for kc in range(KC):
                    o.mm(pm[:, oc, 0:nr], lhsT=wt[:, jj, kc, :], rhs=self.sc[:, kc, :],
                         start=(kc == 0), stop=(kc == KC - 1), rd=[wt, self.sc], wr=[pm])
        mt = p.sb([128, 48, nr], F32)
        o.tt("dve", mt[:], pm[:, 0:48, 0:nr], bt[:].unsqueeze(2).to_broadcast([128, 48, nr]), ALU.add, rd=[pm, bt], wr=[mt])
        o.cp("dve", self.SH[:], mt[:, 0:KC, :], rd=[mt], wr=[self.SH])
        o.stt("dve", self.A[:], mt[:, KC:2 * KC, :], 1.0, pre[:].unsqueeze(2).to_broadcast([128, KC, nr]),
              ALU.add, ALU.mult, rd=[mt, pre], wr=[self.A])
        o.tt("dve", self.G[:], mt[:, 2 * KC:3 * KC, :], post[:].unsqueeze(2).to_broadcast([128, KC, nr]), ALU.mult,
             rd=[mt, post], wr=[self.G])
        p.close()

    def rstd_from_ssq(self, p, pq, w, n, rstd):
        o = self.o
        o.act(rstd[:, :w], pq[:, :w], AF.Ln, rd=[pq], wr=[rstd], bias=self.eps_t[:, 0:1], scale=1.0 / n)
        o.act(rstd[:, :w], rstd[:, :w], AF.Exp, rd=[rstd], wr=[rstd], scale=-0.5)

    def phase1(self, p, s, src, hT):
        o = self.o
        xts = Rot([p.sb([128, KC, 512], F32, "xt") for _ in range(2)])
        sq = p.sb([128, KC, 512], BF16, "sq")
        pq = p.ps([128, 512], F32, "pq")
        rstd = p.sb([128, 512], F32, "rstd")
        tmps = Rot([p.sb([128, 512], F32, "tmp") for _ in range(2)])
        srcv = src.rearrange("(kc p) t -> p kc t", p=128)
        for (t0, w, is_ctx) in seq_blocks():
            row = self.n_seq if is_ctx else s
            xt = xts.next()
            o.dma("sp", xt[:, :, :w], srcv[:, :, t0:t0 + w], wr=[xt])
            o.act(sq[:, :, :w], xt[:, :, :w], AF.Square, rd=[xt], wr=[sq])
            for kc in range(KC):
                o.mm(pq[:, :w], lhsT=self.ones[:], rhs=sq[:, kc, :w], start=(kc == 0), stop=(kc == KC - 1),
                     rd=[self.ones, sq], wr=[pq])
            self.rstd_from_ssq(p, pq, w, D, rstd)
            for kc in range(KC):
                tmp = tmps.next()
                o.stt("dve", tmp[:, :w], xt[:, kc, :w], self.A[:, kc, row:row + 1], rstd[:, :w], ALU.mult, ALU.mult,
                      rd=[xt, self.A, rstd], wr=[tmp])
                o.act(hT[:, kc, t0:t0 + w], tmp[:, :w], AF.Identity, rd=[tmp, self.SH], wr=[hT],
                      bias=self.SH[:, kc, row:row + 1], scale=1.0)

    def gemm_f(self, p, hT, kcn, w_d, n_oc, epilogue, blocks=None, grp=4):
        o = self.o
        blocks = blocks or seq_blocks()
        wts = Rot([p.sb([128, grp, kcn, 128], BF16, "wt") for _ in range(2)])
        pss = Rot([p.ps([128, 512], F32, "pg") for _ in range(3)])
        pending, u = [], 0
        for og in range(0, n_oc, grp):
            wt = wts.next()
            gn = min(grp, n_oc - og)
            o.dma("pool", wt[:, :gn], w_d[og:og + gn].rearrange("g p k j -> p g k j"), wr=[wt])
            for jj in range(gn):
                for blk in blocks:
                    t0, w, _ = blk
                    ps = pss.next()
                    for kc in range(kcn):
                        o.mm(ps[:, :w], lhsT=wt[:, jj, kc, :], rhs=hT[:, kc, t0:t0 + w], start=(kc == 0),
                             stop=(kc == kcn - 1), rd=[wt, hT], wr=[ps])
                    d_ = epilogue(og + jj, blk, ps)
                    if d_ is not None:
                        pending.append((u + d_[0], d_[1]))
                    for it in [x for x in pending if x[0] <= u]:
                        pending.remove(it)
                        it[1]()
                    u += 1
        for it in pending:
            it[1]()

    def gemm_t(self, p, hT, kcn, w_d, n_g, epilogue, gw=512):
        o = self.o
        wts = Rot([p.sb([128, kcn, gw], BF16, "wtt") for _ in range(2)])
        pss = Rot([p.ps([128, 512], F32, "pgt") for _ in range(2)])
        for gi in range(n_g):
            wt = wts.next()
            o.dma("pool", wt[:], w_d[gi], wr=[wt])
            for tt in range(NT):
                ps = pss.next()
                for kc in range(kcn):
                    o.mm(ps[:, :gw], lhsT=hT[:, kc, tt * 128:(tt + 1) * 128], rhs=wt[:, kc, :], start=(kc == 0),
                         stop=(kc == kcn - 1), rd=[wt, hT], wr=[ps])
                epilogue(gi, tt, ps)

    def phase4(self, s, src, dst, dst_off, need_ctx, kcn, w_d):
        o = self.o
        p = self.ctx.sub()
        wos = [p.sb([128, kcn, 512], BF16, f"wo{q}") for q in range(4)]
        for q in range(4):
            for h0 in range(0, kcn, 8):
                hs = slice(h0, min(kcn, h0 + 8))
                o.dma("pool" if (q + h0 // 8) % 2 == 0 else "pool", wos[q][:, hs, :], w_d[:, hs, q * 512:(q + 1) * 512], wr=[wos[q]])
        bw = 256 if kcn <= 16 else 128
        zbs = Rot([p.sb([128, kcn, bw], BF16, "zb") for _ in range(2)])
        xbs = Rot([p.sb([128, KC, bw], F32, "xb") for _ in range(2)])
        ysb = p.sb([128, KC, bw], F32, "ysb")
        sqs = Rot([p.sb([128, bw], BF16, "sq4") for _ in range(2)])
        pys = Rot([p.ps([128, 512], F32, "py") for _ in range(3)])
        pq = p.ps([128, 512], F32, "pq4")
        rstd = p.sb([128, bw], F32, "rstd4")
        tmps = Rot([p.sb([128, bw], F32, "tmp4") for _ in range(2)])
        srcv = src.rearrange("(kc p) t -> p kc t", p=128)
        dstv = dst.rearrange("(kc p) t -> p kc t", p=128)
        zv = self.zT[0:kcn * 128, :].rearrange("(kc p) t -> p kc t", p=128)
        blocks = [(t0, bw, t0 < CTXL) for t0 in range(0, NTOK, bw)]
        import os
        nb = int(os.environ.get("P4N", "999"))
        for (t0, w, is_ctx) in blocks[:nb]:
            if is_ctx and not need_ctx:
                continue
            row = self.n_seq if is_ctx else s
            zb, xb = zbs.next(), xbs.next()
            xo = xb
            o.dma("sp", zb[:], zv[:, :, t0:t0 + w], wr=[zb])
            o.dma("sp", xb[:], srcv[:, :, t0:t0 + w], wr=[xb])
            stg_ = int(os.environ.get("P4S", "9"))
            if stg_ < 2:
                continue
            for oc in range(KC):
                py = pys.next()
                for kc in range(kcn):
                    wo = wos[oc // 4]
                    o.mm(py[:, :w], lhsT=wo[:, kc, (oc % 4) * 128:(oc % 4 + 1) * 128], rhs=zb[:, kc, :], start=(kc == 0),
                         stop=(kc == kcn - 1), rd=[wo, zb], wr=[py])
                sq = sqs.next()
                px = int(os.environ.get("P4X", "7"))
                if px & 1:
                    o.act(sq[:], py[:, :w], AF.Square, rd=[py], wr=[sq])
                if px & 2:
                    o.cp("dve", ysb[:, oc, :], py[:, :w], rd=[py], wr=[ysb])
                if px & 4:
                    o.mm(pq[:, :w], lhsT=self.ones[:], rhs=sq[:], start=(oc == 0), stop=(oc == KC - 1), rd=[self.ones, sq], wr=[pq])
            if stg_ < 3:
                continue
            self.rstd_from_ssq(p, pq, w, D, rstd)
            if stg_ < 4:
                continue
            for oc in range(KC):
                tmp = tmps.next()
                o.stt("dve", tmp[:], ysb[:, oc, :], self.G[:, oc, row:row + 1], rstd[:], ALU.mult, ALU.mult,
                      rd=[ysb, self.G, rstd], wr=[tmp])
                o.tt("pool", xo[:, oc, :], tmp[:], xb[:, oc, :], ALU.add, rd=[tmp, xb], wr=[xo])
            if stg_ < 5:
                continue
            o.dma("sp", dstv[:, :, t0 - dst_off:t0 - dst_off + w], xo[:], rd=[xo], is_output=True)
        p.close()

    def attn_consts(self):
        if hasattr(self, "cosT"):
            return
        o, g = self.o, self.g
        cos_d = self.inp("rope_cos", [128, SEQL])
        sin_d = self.inp("rope_sin", [128, SEQL])
        perm_d = self.inp("rope_perm", [128, 128])
        self.cos_d, self.sin_d, self.perm_d = cos_d, sin_d, perm_d
        self.qT_d = self.scratch("qT", [D, NTOK], BF16)
        self.kT_d = self.scratch("kT", [512, NTOK], BF16)
        self.gT_d = self.scratch("gT", [D, NTOK], BF16)
        self.V_d = self.scratch("Vd", [NTOK, 512], BF16)

    def attn_layer(self, li, j, s, src, dst, dst_off, need_ctx):
        o = self.o
        self.attn_consts()
        if s == 0:
            self.aw = dict(
                wq=self.inp(f"attn_wq{j}", [16, 128, KC, 128]),
                wk=self.inp(f"attn_wk{j}", [4, 128, KC, 128]),
                wg=self.inp(f"attn_wg{j}", [16, 128, KC, 128]),
                wv=self.inp(f"attn_wv{j}", [1, 128, KC, 512]),
                wo=self.inp(f"attn_wo{j}", [128, KC, D]),
                qn=self.inp(f"attn_qn{j}", [128, 1]),
                kn=self.inp(f"attn_kn{j}", [128, 1]),
            )
        aw = self.aw
        p = self.ctx.sub()
        hT = p.sb([128, KC, NTOK], BF16, "hT")
        p1 = self.ctx.sub()
        self.phase1(p1, s, src, hT)
        p1.close()
        self.cosT = p.sb([128, SEQL], F32, "cosT")
        self.sinT = p.sb([128, SEQL], F32, "sinT")
        self.perm = p.sb([128, 128], BF16, "perm")
        o.dma("sp", self.cosT[:], self.cos_d[:, :], wr=[self.cosT])
        o.dma("sp", self.sinT[:], self.sin_d[:, :], wr=[self.sinT])
        o.dma("pool", self.perm[:], self.perm_d[:, :], wr=[self.perm])
        qn = p.sb([128, 1], F32, "qn")
        kn = p.sb([128, 1], F32, "kn")
        o.dma("sp", qn[:], aw["qn"][:, :], wr=[qn])
        o.dma("sp", kn[:], aw["kn"][:, :], wr=[kn])
        o.ts("dve", qn[:], qn[:], ATT_SCALE, None, ALU.mult, rd=[qn], wr=[qn])
        sqs = Rot([p.sb([128, 512], BF16, "sqa") for _ in range(4)])
        pqs = Rot([p.ps([128, 512], F32, "pqa") for _ in range(2)])
        prs = Rot([p.ps([128, 512], F32, "pra") for _ in range(2)])
        rstds = Rot([p.sb([128, 512], F32, "rstda") for _ in range(4)])
        qns = Rot([p.sb([128, 512], F32, "qna") for _ in range(4)])
        qnbs = Rot([p.sb([128, 512], BF16, "qnb") for _ in range(4)])
        t1s = Rot([p.sb([128, 512], F32, "t1a") for _ in range(4)])
        stg = Rot([p.sb([128, NTOK], BF16, "stg") for _ in range(2)])
        cur = {}

        def qk_epi(dst_d, nw):
            def epi(oc, blk, ps):
                if blk[0] == 0:
                    cur["st"] = stg.next()
                st = cur["st"]
                return (1, lambda: epi2(oc, blk, ps, st))

            def epi2(oc, blk, ps, st):
                t0, w, is_ctx = blk
                sq, pq, rstd = sqs.next(), pqs.next(), rstds.next()
                o.act(sq[:, :w], ps[:, :w], AF.Square, rd=[ps], wr=[sq])
                o.mm(pq[:, :w], lhsT=self.ones[:], rhs=sq[:, :w], rd=[self.ones, sq], wr=[pq])
                self.rstd_from_ssq(p, pq, w, 128, rstd)
                if is_ctx:
                    o.stt("dve", st[:, t0:t0 + w], ps[:, :w], nw[:, 0:1], rstd[:, :w], ALU.mult, ALU.mult,
                          rd=[ps, nw, rstd], wr=[st])
                else:
                    qn_, qnb, pr, t1 = qns.next(), qnbs.next(), prs.next(), t1s.next()
                    o.stt("dve", qn_[:, :w], ps[:, :w], nw[:, 0:1], rstd[:, :w], ALU.mult, ALU.mult,
                          rd=[ps, nw, rstd], wr=[qn_])
                    o.cp("act", qnb[:, :w], qn_[:, :w], rd=[qn_], wr=[qnb])
                    o.mm(pr[:, :w], lhsT=self.perm[:], rhs=qnb[:, :w], rd=[self.perm, qnb], wr=[pr])
                    l0 = t0 - CTXL
                    o.tt("pool", t1[:, :w], qn_[:, :w], self.cosT[:, l0:l0 + w], ALU.mult, rd=[qn_, self.cosT], wr=[t1])
                    o.tt("dve", qn_[:, :w], pr[:, :w], self.sinT[:, l0:l0 + w], ALU.mult, rd=[pr, self.sinT], wr=[qn_])
                    o.tt("pool", st[:, t0:t0 + w], t1[:, :w], qn_[:, :w], ALU.add, rd=[t1, qn_], wr=[st])
                if t0 + w == NTOK:
                    o.dma("sp", dst_d[oc * 128:(oc + 1) * 128, :], st[:], rd=[st])
            return epi

        def g_epi(oc, blk, ps):
            t0, w, _ = blk
            if t0 == 0:
                cur["st"] = stg.next()
            st = cur["st"]
            o.act(st[:, t0:t0 + w], ps[:, :w], AF.Silu, rd=[ps], wr=[st])
            if t0 + w == NTOK:
                o.dma("sp", self.gT_d[oc * 128:(oc + 1) * 128, :], st[:], rd=[st])

        vst = Rot([p.sb([128, 512], BF16, "vst") for _ in range(3)])

        def v_epi(gi, tt, ps):
            st = vst.next()
            o.cp("dve", st[:], ps[:], rd=[ps], wr=[st])
            o.dma("sp", self.V_d[tt * 128:(tt + 1) * 128, :], st[:], rd=[st])

        pk = self.ctx.sub()
        self.gemm_f(pk, hT, KC, aw["wk"], 4, qk_epi(self.kT_d, kn))
        pk.close()
        pk = self.ctx.sub()
        self.gemm_t(pk, hT, KC, aw["wv"], 1, v_epi)
        pk.close()
        pk = self.ctx.sub()
        self.gemm_f(pk, hT, KC, aw["wq"], 16, qk_epi(self.qT_d, qn))
        pk.close()
        pk = self.ctx.sub()
        self.gemm_f(pk, hT, KC, aw["wg"], 16, g_epi)
        pk.close()
        p.close()
        p = self.ctx.sub()
        kT = p.sb([128, N_KV, NTOK], BF16, "kT")
        V = p.sb([128, NT, 512], BF16, "V")
        o.dma("sp", kT[:], self.kT_d.rearrange("(h p) t -> p h t", p=128), wr=[kT])
        o.dma("sp", V[:], self.V_d.rearrange("(c p) n -> p c n", p=128), wr=[V])
        qbs = Rot([p.sb([128, N_HEADS, 512], BF16, "qb") for _ in range(2)])
        gbs = Rot([p.sb([128, N_HEADS, 512], BF16, "gb") for _ in range(2)])
        zbs = Rot([p.sb([128, N_HEADS, 512], BF16, "zb3") for _ in range(2)])
        pss = Rot([p.ps([128, 512], F32, "ps3") for _ in range(3)])
        pos = Rot([p.ps([128, 512], F32, "po3") for _ in range(2)])
        pls = Rot([p.ps([128, 512], F32, "pl3") for _ in range(2)])
        pts = Rot([p.sb([128, 512], BF16, "pt3") for _ in range(5)])
        rss = Rot([p.sb([128, 512], F32, "rs3") for _ in range(2)])
        tos = Rot([p.sb([128, 512], F32, "to3") for _ in range(2)])
        qv = self.qT_d.rearrange("(h p) t -> p h t", p=128)
        gv = self.gT_d.rearrange("(h p) t -> p h t", p=128)
        zv = self.zT[0:D, :].rearrange("(h p) t -> p h t", p=128)
        for (t0, w, is_ctx) in seq_blocks():
            if is_ctx and not need_ctx:
                continue
            nkc = CTXL // 128 if is_ctx else NT
            qb, gb, zb = qbs.next(), gbs.next(), zbs.next()
            o.dma("sp", qb[:, :, :w], qv[:, :, t0:t0 + w], wr=[qb])
            o.dma("sp", gb[:, :, :w], gv[:, :, t0:t0 + w], wr=[gb])
            steps = [(h, c) for h in range(N_HEADS) for c in range(nkc)]
            acc = {}

            def emit_s(h, c):
                kvh = h // (N_HEADS // N_KV)
                ps, pt = pss.next(), pts.next()
                o.mm(ps[:, :w], lhsT=kT[:, kvh, c * 128:(c + 1) * 128], rhs=qb[:, h, :w], rd=[kT, qb], wr=[ps])
                o.act(pt[:, :w], ps[:, :w], AF.Exp, rd=[ps], wr=[pt], bias=self.m4_t[:, 0:1], scale=1.0)
                return pt

            def emit_pv(h, c, pt):
                kvh = h // (N_HEADS // N_KV)
                if c == 0:
                    acc[h] = (pos.next(), pls.next())
                po, pl = acc[h]
                o.mm(po[:, :w], lhsT=V[:, c, kvh * 128:(kvh + 1) * 128], rhs=pt[:, :w], start=(c == 0),
                     stop=(c == nkc - 1), rd=[V, pt], wr=[po])
                o.mm(pl[:, :w], lhsT=self.ones[:], rhs=pt[:, :w], start=(c == 0), stop=(c == nkc - 1),
                     rd=[self.ones, pt], wr=[pl])
                if c == nkc - 1:
                    rs, to = rss.next(), tos.next()
                    o.recip(rs[:, :w], pl[:, :w], rd=[pl], wr=[rs])
                    o.tt("dve", to[:, :w], po[:, :w], rs[:, :w], ALU.mult, rd=[po, rs], wr=[to])
                    o.tt("pool", zb[:, h, :w], to[:, :w], gb[:, h, :w], ALU.mult, rd=[to, gb], wr=[zb])

            LA = 2
            q_ = [emit_s(*steps[i_]) for i_ in range(min(LA, len(steps)))]
            for i_, (h, c) in enumerate(steps):
                if i_ + LA < len(steps):
                    q_.append(emit_s(*steps[i_ + LA]))
                emit_pv(h, c, q_.pop(0))
            o.dma("sp", zv[:, :, t0:t0 + w], zb[:, :, :w], rd=[zb])
        p.close()
        self.phase4(s, src, dst, dst_off, need_ctx, KC, aw["wo"])


def lhsT_tiles(w):
    K, N = w.shape
    return np.ascontiguousarray(w.reshape(K // 128, 128, N // 128, 128).transpose(2, 1, 0, 3))


def rhs_tiles(w, gw=512):
    K, N = w.shape
    return np.ascontiguousarray(w.reshape(K // 128, 128, N // gw, gw).transpose(2, 1, 0, 3))


def fm_vec(v):
    return np.ascontiguousarray(v.reshape(-1, 128).T)


def rope_tables():
    rr, cc = np.meshgrid(np.arange(SEQL // 64), np.arange(64), indexing="ij")
    row = rr.reshape(-1).astype(np.float32)
    col = cc.reshape(-1).astype(np.float32)
    inv = (1.0 / (np.float32(10000.0) ** (np.arange(32, dtype=np.float32) / np.float32(32)))).astype(np.float32)
    ang = np.concatenate([row[:, None] * inv, col[:, None] * inv], axis=-1).astype(np.float32)
    cos, sin = np.cos(ang).astype(np.float32), np.sin(ang).astype(np.float32)
    cosT = np.zeros((128, SEQL), np.float32)
    sinT = np.zeros((128, SEQL), np.float32)
    perm = np.zeros((128, 128), np.float32)
    for p in range(128):
        axis, half, f = p // 64, (p // 32) % 2, p % 32
        cosT[p] = cos[:, axis * 32 + f]
        sinT[p] = sin[:, axis * 32 + f] * (-1.0 if half == 0 else 1.0)
        partner = p + 32 if half == 0 else p - 32
        perm[partner, p] = 1.0
    return cosT, sinT, perm


def prep_consts():
    cosT, sinT, perm = rope_tables()
    c = {"ident": np.eye(128, dtype=np.float32), "rope_cos": cosT, "rope_sin": sinT, "rope_perm": perm}
    k = np.arange(128)[:, None]
    i = np.arange(128)[None, :]
    c["tri_f"] = (k <= i).astype(np.float32)
    c["tri_b"] = (k >= i).astype(np.float32)
    c["negm_f"] = np.where(i >= k, 0.0, NEG).astype(np.float32)
    c["negm_b"] = np.where(i <= k, 0.0, NEG).astype(np.float32)
    c["negs_f"] = np.where(i > k, 0.0, NEG).astype(np.float32)
    c["negs_b"] = np.where(i < k, 0.0, NEG).astype(np.float32)
    c["bdmask"] = ((k // 32) == (i // 32)).astype(np.float32)
    return c


def prep_weights(inp, layers):
    w = {}
    for li in layers:
        kind, j = li % 3, li // 3
        aw = inp["ada_w"][li]
        t = lhsT_tiles(aw)
        w[f"ada_w{li}"] = np.ascontiguousarray(t.reshape(12, 4, 128, KC, 128).transpose(0, 2, 1, 3, 4))
        w[f"ada_b{li}"] = fm_vec(inp["ada_b"][li])
        w[f"pre_w{li}"] = fm_vec(inp["pre_norm_w"][li])
        w[f"post_w{li}"] = fm_vec(inp["post_norm_w"][li])
        if kind == 0:
            wi = inp["attn_w_in"][j]
            w[f"attn_wq{j}"] = lhsT_tiles(wi[:, 0:2048])
            w[f"attn_wk{j}"] = lhsT_tiles(wi[:, 2048:2560])
            w[f"attn_wv{j}"] = rhs_tiles(wi[:, 2560:3072])
            w[f"attn_wg{j}"] = lhsT_tiles(wi[:, 3072:5120])
            wo = inp["attn_w_out"][j]
            w[f"attn_wo{j}"] = np.ascontiguousarray(wo.reshape(KC, 128, D).transpose(1, 0, 2))
            w[f"attn_qn{j}"] = np.ascontiguousarray(inp["attn_q_norm"][j].reshape(128, 1))
            w[f"attn_kn{j}"] = np.ascontiguousarray(inp["attn_k_norm"][j].reshape(128, 1))
        elif kind == 1:
            wi = inp["ssd_w_in"][j]
            w[f"ssd_wz{j}"] = rhs_tiles(wi[:, 0:4096])
            w[f"ssd_wx{j}"] = lhsT_tiles(wi[:, 4096:10240])
            w[f"ssd_wdt{j}"] = rhs_tiles(wi[:, 10240:10368], gw=128)
            cw = inp["ssd_conv_w"][j]
            w[f"ssd_cw{j}"] = np.ascontiguousarray(cw.reshape(5, 48, 128).transpose(2, 1, 0))
            w[f"ssd_cb{j}"] = fm_vec(inp["ssd_conv_b"][j])
            w[f"ssd_dtb{j}"] = np.ascontiguousarray(inp["ssd_dt_bias"][j].reshape(128))
            w[f"ssd_alog{j}"] = np.ascontiguousarray(inp["ssd_a_log"][j].reshape(128))
            w[f"ssd_d{j}"] = np.ascontiguousarray(inp["ssd_d"][j])
            w[f"ssd_nw{j}"] = np.ascontiguousarray(inp["ssd_norm_w"][j])
            w[f"ssd_wo{j}"] = np.ascontiguousarray(inp["ssd_w_out"][j].reshape(32, 128, D).transpose(1, 0, 2))
        else:
            wi = inp["dn_w_in"][j]
            w[f"dn_wx{j}"] = lhsT_tiles(wi[:, 0:8192])
            w[f"dn_wz{j}"] = rhs_tiles(wi[:, 8192:12288])
            w[f"dn_wab{j}"] = rhs_tiles(wi[:, 12288:12416], gw=128)
            cw = inp["dn_conv_w"][j]
            w[f"dn_cw{j}"] = np.ascontiguousarray(cw.reshape(5, 64, 128).transpose(2, 1, 0))
            w[f"dn_dtb{j}"] = np.ascontiguousarray(inp["dn_dt_bias"][j].reshape(64))
            w[f"dn_alog{j}"] = np.ascontiguousarray(inp["dn_a_log"][j].reshape(64))
            w[f"dn_nw{j}"] = np.ascontiguousarray(inp["dn_norm_w"][j])
            w[f"dn_wo{j}"] = np.ascontiguousarray(inp["dn_w_out"][j].reshape(32, 128, D).transpose(1, 0, 2))
    return w


def prep_seq(x, ctx, c, c_ctx):
    ns = x.shape[0]
    xin = np.empty((ns, D, NTOK), np.float32)
    xin[:, :, :CTXL] = ctx.transpose(0, 2, 1)
    xin[:, :, CTXL:] = x.transpose(0, 2, 1)
    rows = np.concatenate([c, c_ctx[None]], axis=0)
    cT = np.ascontiguousarray(rows.reshape(ns + 1, KC, 128).transpose(2, 1, 0))
    return xin, cT


_PROG_CACHE = {}


def run_prog(inp_np, per_core_seq, n_seq, layers, full_out, n_cores):
    prog = Prog(n_seq, layers, full_out)
    nc = prog.build()
    shared = dict(prep_consts())
    shared.update(prep_weights(inp_np, layers))
    in_maps = []
    for cidx in range(n_cores):
        m = {k: v for k, v in shared.items() if k in prog.inputs}
        xin, cT = per_core_seq[cidx]
        m["xin"], m["cT"] = xin, cT
        missing = set(prog.inputs) - set(m)
        assert not missing, missing
        in_maps.append(m)
    import os
    if os.environ.get("KTRACE"):
        res = run_bass_kernel_spmd(nc, in_maps, core_ids=list(range(n_cores)), trace=True)
        print("EXEC_NS", res.exec_time_ns, "ops", prog.S.nops, "waits", prog.S.nwaits, "cnt", prog.S.cnt, flush=True)
    else:
        res = run_bass_kernel_spmd(nc, in_maps, core_ids=list(range(n_cores)))
    return [r["yout"] for r in res.results], res


def kernel(**inputs):
    inp = {k: np.asarray(v) for k, v in inputs.items()}
    n_cores, ns = 8, 2
    per_core = []
    for cidx in range(n_cores):
        sl = slice(cidx * ns, (cidx + 1) * ns)
        per_core.append(prep_seq(inp["x"][sl], inp["ctx"][sl], inp["c"][sl], inp["c_ctx"]))
    outs, _ = run_prog(inp, per_core, ns, list(range(DEPTH)), False, n_cores)
    y = np.concatenate([o_.transpose(0, 2, 1) for o_ in outs], axis=0)
    bad = [int((~np.isfinite(y[b])).sum()) for b in range(y.shape[0])]
    if any(bad):
        print("KERNEL non-finite counts per batch element:", bad, flush=True)
    return np.ascontiguousarray(y.astype(np.float32))


SSD_DI, SSD_H, SSD_G, SSD_N, SSD_P = 4096, 64, 8, 128, 64
NEG = -30000.0


def scan_consts(self):
    if hasattr(self, "tri"):
        return
    o, g = self.o, self.g
    self.tri, self.negm, self.negs = {}, {}, {}
    bd_d = self.inp("bdmask", [128, 128])
    self.bd = g.sb([128, 128], F32, "bdmask")
    o.dma("sp", self.bd[:], bd_d[:, :], wr=[self.bd])
    for d_ in ("f", "b"):
        for nm, store in (("tri", self.tri), ("negm", self.negm), ("negs", self.negs)):
            dd = self.inp(f"{nm}_{d_}", [128, 128])
            t = g.sb([128, 128], F32, f"{nm}{d_}")
            o.dma("sp", t[:], dd[:, :], wr=[t])
            store[d_] = t
            tb = g.sb([128, 128], BF16, f"{nm}{d_}b")
            o.dma("pool", tb[:], dd[:, :], wr=[tb])
            store[d_ + "16"] = tb


def ssd_layer(self, li, j, s, src, dst, dst_off, need_ctx):
    o = self.o
    scan_consts(self)
    if s == 0:
        self.sw = dict(
            wz=self.inp(f"ssd_wz{j}", [8, 128, KC, 512]),
            wx=self.inp(f"ssd_wx{j}", [48, 128, KC, 128]),
            wdt=self.inp(f"ssd_wdt{j}", [1, 128, KC, 128]),
            cw=self.inp(f"ssd_cw{j}", [128, 48, 5]),
            cb=self.inp(f"ssd_cb{j}", [128, 48]),
            dtb=self.inp(f"ssd_dtb{j}", [128]),
            alog=self.inp(f"ssd_alog{j}", [128]),
            dsk=self.inp(f"ssd_d{j}", [64]),
            nw=self.inp(f"ssd_nw{j}", [SSD_DI]),
            wo=self.inp(f"ssd_wo{j}", [128, 32, D]),
        )
        self.sz_d = self.scratch("ssd_sz", [NTOK, SSD_DI], BF16)
        self.x_d = self.scratch("ssd_x", [NTOK, SSD_DI], BF16)
        self.B_d = self.scratch("ssd_B", [NTOK, 1024], BF16)
        self.BT_d = self.scratch("ssd_BT", [1024, NTOK], BF16)
        self.CT_d = self.scratch("ssd_CT", [1024, NTOK], BF16)
        self.dt_d = self.scratch("ssd_dt", [NTOK, 128], F32)
        self.yf_d = self.scratch("ssd_yf", [NTOK, SSD_DI], F32)
    sw = self.sw
    p = self.ctx.sub()
    hT = p.sb([128, KC, NTOK], BF16, "hT")
    p1 = self.ctx.sub()
    self.phase1(p1, s, src, hT)
    p1.close()
    pk = self.ctx.sub()
    zst = Rot([pk.sb([128, 512], BF16, "zst") for _ in range(3)])

    def z_epi(gi, tt, ps):
        st = zst.next()
        o.act(st[:], ps[:], AF.Silu, rd=[ps], wr=[st])
        o.dma("sp", self.sz_d[tt * 128:(tt + 1) * 128, gi * 512:(gi + 1) * 512], st[:], rd=[st])

    self.gemm_t(pk, hT, KC, sw["wz"], 8, z_epi)
    pk.close()
    pk = self.ctx.sub()
    dtb = pk.sb([128, 128], F32, "dtb")
    o.dma("sp", dtb[:], sw["dtb"].partition_broadcast(128), wr=[dtb])
    dst_ = Rot([pk.sb([128, 128], F32, "dtst") for _ in range(3)])

    def dt_epi(gi, tt, ps):
        st = dst_.next()
        o.tt("dve", st[:], ps[:, :128], dtb[:], ALU.add, rd=[ps, dtb], wr=[st])
        o.act(st[:], st[:], AF.Exp, rd=[st], wr=[st])
        o.act(st[:], st[:], AF.Ln, rd=[st], wr=[st], bias=1.0, scale=1.0)
        o.dma("sp", self.dt_d[tt * 128:(tt + 1) * 128, :], st[:], rd=[st])

    self.gemm_t(pk, hT, KC, sw["wdt"], 1, dt_epi, gw=128)
    pk.close()
    pk = self.ctx.sub()
    cw = pk.sb([128, 48, 5], F32, "cw")
    cb = pk.sb([128, 48], F32, "cb")
    o.dma("sp", cw[:], sw["cw"][:, :, :], wr=[cw])
    o.dma("sp", cb[:], sw["cb"][:, :], wr=[cb])
    self.conv_gemm(pk, hT, sw["wx"], 48, cw, cb,
                   tok_dst=lambda oc: (self.x_d, oc * 128) if oc < 32 else ((self.B_d, (oc - 32) * 128) if oc < 40 else None),
                   fm_dst=lambda oc: (self.BT_d, (oc - 32) * 128) if 32 <= oc < 40 else ((self.CT_d, (oc - 40) * 128) if oc >= 40 else None))
    pk.close()
    p.close()
    ssd_scan_phase(self, s)
    self.phase4(s, src, dst, dst_off, need_ctx, 32, sw["wo"])


def conv_gemm(self, pk, hT, w_d, n_oc, cw, cb, tok_dst, fm_dst, post=None):
    o = self.o
    segs = [(0, CTXL), (CTXL, SEQL)]
    bufs = Rot([[pk.sb([128, n + 4], F32, "cvb") for (_, n) in segs] for _ in range(3)])
    for pair in bufs.tiles:
        for t in pair:
            o.memset("pool", t[:], 0.0, wr=[t])
    accs = Rot([pk.sb([128, NTOK], F32, "acc") for _ in range(2)])
    rows = Rot([pk.sb([128, NTOK], BF16, "rowb") for _ in range(2)])
    ptr = Rot([pk.ps([128, 8, 128], BF16, "ptr") for _ in range(2)])
    tst = Rot([pk.sb([128, NT, 128], BF16, "tst") for _ in range(2)])
    cur = {}

    def epi(oc, blk, ps):
        t0, w, is_ctx = blk
        if t0 == 0:
            cur["buf"] = bufs.next()
        bc, bl = cur["buf"]
        if is_ctx:
            o.cp("act", bc[:, 2:2 + w], ps[:, :w], rd=[ps], wr=[bc])
        else:
            l0 = t0 - CTXL
            o.cp("act", bl[:, 2 + l0:2 + l0 + w], ps[:, :w], rd=[ps], wr=[bl])
        if t0 + w != NTOK:
            return None
        return (4, lambda: fin(oc, bc, bl))

    def fin(oc, bc, bl):
        acc, row = accs.next(), rows.next()
        for (sb_, (s0, n), eng) in ((bc, segs[0], "dve"), (bl, segs[1], "dve")):
            o.ts(eng, acc[:, s0:s0 + n], sb_[:, 0:n], cw[:, oc, 0:1], cb[:, oc:oc + 1], ALU.mult, ALU.add,
                 rd=[sb_, cw, cb], wr=[acc])
            for k in range(1, 5):
                o.stt(eng, acc[:, s0:s0 + n], sb_[:, k:k + n], cw[:, oc, k:k + 1], acc[:, s0:s0 + n], ALU.mult, ALU.add,
                      rd=[sb_, cw, acc], wr=[acc])
        if post is None:
            o.act(row[:], acc[:], AF.Silu, rd=[acc], wr=[row])
        else:
            o.act(acc[:], acc[:], AF.Silu, rd=[acc], wr=[acc])
            post(oc, acc, row)
        fd = fm_dst(oc)
        if fd is not None:
            o.dma("sp", fd[0][fd[1]:fd[1] + 128, :], row[:], rd=[row])
        td = tok_dst(oc)
        if td is not None:
            st = tst.next()
            for t8 in range(0, NT, 8):
                pt = ptr.next()
                n8 = min(8, NT - t8)
                for q in range(n8):
                    tt = t8 + q
                    o.tr(pt[:, q, :], row[:, tt * 128:(tt + 1) * 128], self.ident[:], rd=[row, self.ident], wr=[pt])
                o.cp("dve", st[:, t8:t8 + n8, :], pt[:, :n8, :], rd=[pt], wr=[st])
            o.dma("sp", td[0].rearrange("(c p) n -> p c n", p=128)[:, :, td[1]:td[1] + 128], st[:], rd=[st])

    self.gemm_f(pk, hT, KC, w_d, n_oc, epi)


Prog.ssd_layer = ssd_layer
Prog.conv_gemm = conv_gemm


def chunk_order(direction):
    if direction == "f":
        return list(range(NT))
    nc_ = CTXL // 128
    return list(range(nc_ - 1, -1, -1)) + list(range(NT - 1, nc_ - 1, -1))


def ssd_scan_phase(self, s):
    o = self.o
    sw = self.sw
    p = self.ctx.sub()
    H, G, E, P = SSD_H, SSD_G, 8, SSD_P
    aneg = p.sb([128, 128], F32, "aneg")
    o.dma("sp", aneg[:], sw["alog"].partition_broadcast(128), wr=[aneg])
    o.act(aneg[:], aneg[:], AF.Exp, rd=[aneg], wr=[aneg])
    o.ts("dve", aneg[:], aneg[:], -1.0, None, ALU.mult, rd=[aneg], wr=[aneg])
    dsk = p.sb([128, H], F32, "dsk")
    o.dma("sp", dsk[:], sw["dsk"].partition_broadcast(128), wr=[dsk])
    nwb = p.sb([128, SSD_DI], F32, "nwb")
    o.dma("sp", nwb[:], sw["nw"].partition_broadcast(128), wr=[nwb])
    hst = [p.sb([128, E * P], F32, f"hst{g_}") for g_ in range(G)]
    hbf = [p.sb([128, E * P], BF16, f"hbf{g_}") for g_ in range(G)]
    xcs = Rot([p.sb([128, H, P], BF16, "xc") for _ in range(2)])
    dts = Rot([p.sb([128, 128], F32, "dtc") for _ in range(2)])
    bts = Rot([p.sb([128, G, 128], BF16, "btc") for _ in range(2)])
    cts = Rot([p.sb([128, G, 128], BF16, "ctc") for _ in range(2)])
    bks = Rot([p.sb([128, G * 128], BF16, "bkc") for _ in range(2)])
    xdt = p.sb([128, H, P], BF16, "xdt")
    xw = p.sb([128, H, P], BF16, "xw")
    da = p.sb([128, H], F32, "da")
    dah = p.sb([128, H], BF16, "dah")
    dal = p.sb([128, H], BF16, "dal")
    acum = p.sb([128, H], F32, "acum")
    nacum = p.sb([128, H], F32, "nacum")
    eA = p.sb([128, H], F32, "eA")
    wdec = p.sb([128, H], F32, "wdec")
    etot = p.sb([128, H], F32, "etot")
    ych = Rot([p.sb([128, H, P], F32, "ych") for _ in range(2)])
    cbs = Rot([p.sb([128, 128], F32, "cbs") for _ in range(2)])
    Es = Rot([p.sb([128, 128], F32, "E") for _ in range(3)])
    LTs = Rot([p.sb([128, 128], BF16, "LT") for _ in range(3)])
    tmps = Rot([p.sb([128, E, P], F32, "tmpy") for _ in range(2)])
    yf = p.sb([128, H, P], F32, "yf")
    szc = p.sb([128, SSD_DI], BF16, "szc")
    un = p.sb([128, SSD_DI], BF16, "un")
    ssq = p.sb([128, 1], F32, "ssq")
    rstd = p.sb([128, 1], F32, "rstd1")
    junk = p.sb([128, SSD_DI], BF16, "junk")
    zst = Rot([p.sb([128, 32, 128], BF16, "zst3") for _ in range(2)])
    pmisc = p.ps([128, 512], F32, "pmisc")
    pcb = p.ps([128, 512], F32, "pcb")
    pR = Rot([p.ps([128, 4, 128], F32, "pR") for _ in range(2)])
    pys = Rot([p.ps([128, 512], F32, "pyi") for _ in range(2)])
    pYg = p.ps([128, 512], F32, "pYg")
    pHn = p.ps([128, 512], F32, "pHn")
    xv = self.x_d.rearrange("(c p) (h q) -> c p h q", p=128, q=P)
    dtv = self.dt_d.rearrange("(c p) n -> c p n", p=128)
    btv = self.BT_d.rearrange("(g n) t -> n g t", n=128)
    ctv = self.CT_d.rearrange("(g n) t -> n g t", n=128)
    bkv = self.B_d.rearrange("(c p) n -> c p n", p=128)
    yfv = self.yf_d.rearrange("(c p) (h q) -> c p h q", p=128, q=P)
    szv = self.sz_d.rearrange("(c p) n -> c p n", p=128)
    zv = self.zT.rearrange("(c p) t -> p c t", p=128)
    for di, d_ in enumerate(("f", "b")):
        tri, negm = self.tri[d_], self.negm[d_]
        trib, negmb = self.tri[d_ + "16"], self.negm[d_ + "16"]
        self.S.barrier()
        for g_ in range(G):
            o.memset("pool", hst[g_][:], 0.0, wr=[hst[g_]])
            o.memset("pool", hbf[g_][:], 0.0, wr=[hbf[g_]])
        for c in chunk_order(d_):
            xc, dtc, btc, ctc, bkc, yc = xcs.next(), dts.next(), bts.next(), cts.next(), bks.next(), ych.next()
            tsl = slice(c * 128, (c + 1) * 128)
            o.dma("sp", xc[:], xv[c], wr=[xc])
            o.dma("sp", dtc[:], dtv[c], wr=[dtc])
            o.dma("sp", btc[:], btv[:, :, tsl], wr=[btc])
            o.dma("sp", ctc[:], ctv[:, :, tsl], wr=[ctc])
            o.dma("sp", bkc[:], bkv[c], wr=[bkc])
            dtd = dtc[:, di * H:(di + 1) * H]
            o.tt("dve", da[:], dtd, aneg[:, di * H:(di + 1) * H], ALU.mult, rd=[dtc, aneg], wr=[da])
            o.cp("dve", dah[:], da[:], rd=[da], wr=[dah])
            o.tt("dve", dal[:], da[:], dah[:], ALU.subtract, rd=[da, dah], wr=[dal])
            o.mm(pmisc[:, 0:H], lhsT=tri[:], rhs=da[:], rd=[tri, da], wr=[pmisc])
            o.mm(pmisc[:, H:2 * H], lhsT=self.onesf[:], rhs=da[:], rd=[self.onesf, da], wr=[pmisc])
            o.cp("dve", acum[:], pmisc[:, 0:H], rd=[pmisc], wr=[acum])
            o.ts("dve", nacum[:], pmisc[:, 0:H], -1.0, None, ALU.mult, rd=[pmisc], wr=[nacum])
            o.tt("dve", wdec[:], pmisc[:, H:2 * H], acum[:], ALU.subtract, rd=[pmisc, acum], wr=[wdec])
            o.act(etot[:], pmisc[:, H:2 * H], AF.Exp, rd=[pmisc], wr=[etot])
            o.act(eA[:], acum[:], AF.Exp, rd=[acum], wr=[eA])
            o.act(wdec[:], wdec[:], AF.Exp, rd=[wdec], wr=[wdec])
            o.tt("dve", wdec[:], wdec[:], dtd, ALU.mult, rd=[wdec, dtc], wr=[wdec])
            o.tt("dve", xdt[:], xc[:], dtd.unsqueeze(2).to_broadcast([128, H, P]), ALU.mult, rd=[xc, dtc], wr=[xdt])
            o.tt("pool", xw[:], xc[:], wdec[:].unsqueeze(2).to_broadcast([128, H, P]), ALU.mult, rd=[xc, wdec], wr=[xw])
            st_ = {}

            def emit_R(g_, e4):
                if e4 == 0:
                    cb_ = cbs.next()
                    o.mm(pcb[:, 0:128], lhsT=btc[:, g_, :], rhs=ctc[:, g_, :], rd=[btc, ctc], wr=[pcb])
                    o.cp("act", cb_[:], pcb[:, 0:128], rd=[pcb], wr=[cb_])
                    st_[("cb", g_)] = cb_
                pr = pR.next()
                for q in range(4):
                    h = g_ * E + e4 + q
                    o.mm(pr[:, q, :], lhsT=dah[:, h:h + 1].to_broadcast([128, 128]), rhs=trib[:], start=True, stop=False,
                         rd=[dah, trib], wr=[pr])
                    o.mm(pr[:, q, :], lhsT=dal[:, h:h + 1].to_broadcast([128, 128]), rhs=trib[:], start=False, stop=False,
                         rd=[dal, trib], wr=[pr])
                    o.mm(pr[:, q, :], lhsT=self.ident[:], rhs=negmb[:], start=False, stop=True,
                         rd=[self.ident, negmb], wr=[pr])
                return pr

            def emit_rest(g_, e4, pr):
                if e4 == 0:
                    st_[("py", g_)] = pys.next()
                py, cb_ = st_[("py", g_)], st_[("cb", g_)]
                for q in range(4):
                    h = g_ * E + e4 + q
                    E_, LT = Es.next(), LTs.next()
                    o.act(E_[:], pr[:, q, :], AF.Exp, rd=[pr, nacum], wr=[E_], bias=nacum[:, h:h + 1], scale=1.0)
                    o.tt("dve", LT[:], E_[:], cb_[:], ALU.mult, rd=[E_, cb_], wr=[LT])
                    o.mm(py[:, (e4 + q) * P:(e4 + q + 1) * P], lhsT=LT[:], rhs=xdt[:, h, :], rd=[LT, xdt], wr=[py])
                if e4 == 0:
                    return
                o.mm(pYg[:], lhsT=ctc[:, g_, :], rhs=hbf[g_][:], rd=[ctc, hbf[g_]], wr=[pYg])
                tmp = tmps.next()
                o.tt("dve", tmp[:], pYg[:].rearrange("p (e q) -> p e q", q=P),
                     eA[:, g_ * E:(g_ + 1) * E].unsqueeze(2).to_broadcast([128, E, P]), ALU.mult, rd=[pYg, eA], wr=[tmp])
                o.tt("dve", yc[:, g_ * E:(g_ + 1) * E, :], tmp[:], py[:].rearrange("p (e q) -> p e q", q=P), ALU.add,
                     rd=[tmp, py], wr=[yc])
                o.mm(pHn[:], lhsT=bkc[:, g_ * 128:(g_ + 1) * 128], rhs=xw[:, g_ * E:(g_ + 1) * E, :].rearrange("p e q -> p (e q)"),
                     rd=[bkc, xw], wr=[pHn])
                o.tt("pool", hst[g_][:].rearrange("p (e q) -> p e q", q=P), hst[g_][:].rearrange("p (e q) -> p e q", q=P),
                     etot[:, g_ * E:(g_ + 1) * E].unsqueeze(2).to_broadcast([128, E, P]), ALU.mult, rd=[hst[g_], etot], wr=[hst[g_]])
                o.tt("dve", hst[g_][:], hst[g_][:], pHn[:], ALU.add, rd=[hst[g_], pHn], wr=[hst[g_]])
                o.cp("act", hbf[g_][:], hst[g_][:], rd=[hst[g_]], wr=[hbf[g_]])

            batches = [(g_, e4) for g_ in range(G) for e4 in (0, 4)]
            pend = emit_R(*batches[0])
            for bi, (g_, e4) in enumerate(batches):
                nxt = emit_R(*batches[bi + 1]) if bi + 1 < len(batches) else None
                emit_rest(g_, e4, pend)
                pend = nxt
            if d_ == "f":
                o.dma("sp", yfv[c], yc[:], rd=[yc])
                continue
            o.dma("sp", yf[:], yfv[c], wr=[yf])
            o.dma("sp", szc[:], szv[c], wr=[szc])
            o.tt("pool", yc[:], yc[:], yf[:], ALU.add, rd=[yc, yf], wr=[yc])
            o.tt("pool", yf[:], xc[:], dsk[:].unsqueeze(2).to_broadcast([128, H, P]), ALU.mult, rd=[xc, dsk], wr=[yf])
            o.tt("pool", yc[:], yc[:], yf[:], ALU.add, rd=[yc, yf], wr=[yc])
            ycf = yc[:].rearrange("p h q -> p (h q)")
            o.tt("dve", ycf, ycf, szc[:], ALU.mult, rd=[yc, szc], wr=[yc])
            o.act(junk[:], ycf, AF.Square, rd=[yc], wr=[junk, ssq], accum_out=ssq[:, 0:1])
            o.act(rstd[:], ssq[:], AF.Ln, rd=[ssq], wr=[rstd], bias=self.eps_t[:, 0:1], scale=1.0 / SSD_DI)
            o.act(rstd[:], rstd[:], AF.Exp, rd=[rstd], wr=[rstd], scale=-0.5)
            o.stt("dve", un[:], ycf, rstd[:, 0:1], nwb[:], ALU.mult, ALU.mult, rd=[yc, rstd, nwb], wr=[un])
            st = zst.next()
            for c8 in range(0, 32, 8):
                pt = pys.next()
                ptb = pt[:].bitcast(BF16).rearrange("p (a b) -> p a b", b=128)
                for q in range(8):
                    o.tr(ptb[:, q, :], un[:, (c8 + q) * 128:(c8 + q + 1) * 128], self.ident[:], rd=[un, self.ident], wr=[pt])
                o.cp("dve", st[:, c8:c8 + 8, :], ptb[:, 0:8, :], rd=[pt], wr=[st])
            o.dma("sp", zv[:, :, tsl], st[:], rd=[st])
    p.close()


DN_HK, DN_HV, DN_DV = 16, 32, 4096


def dn_layer(self, li, j, s, src, dst, dst_off, need_ctx):
    o = self.o
    scan_consts(self)
    if s == 0:
        self.dw = dict(
            wz=self.inp(f"dn_wz{j}", [8, 128, KC, 512]),
            wx=self.inp(f"dn_wx{j}", [64, 128, KC, 128]),
            wab=self.inp(f"dn_wab{j}", [1, 128, KC, 128]),
            cw=self.inp(f"dn_cw{j}", [128, 64, 5]),
            dtb=self.inp(f"dn_dtb{j}", [64]),
            alog=self.inp(f"dn_alog{j}", [64]),
            nw=self.inp(f"dn_nw{j}", [128]),
            wo=self.inp(f"dn_wo{j}", [128, 32, D]),
        )
        self.dsz_d = self.scratch("dn_sz", [NTOK, DN_DV], BF16)
        self.dqT_d = self.scratch("dn_qT", [2048, NTOK], BF16)
        self.dkT_d = self.scratch("dn_kT", [2048, NTOK], BF16)
        self.dk_d = self.scratch("dn_k", [NTOK, 2048], BF16)
        self.dv_d = self.scratch("dn_v", [NTOK, DN_DV], BF16)
        self.dgb_d = self.scratch("dn_gb", [NTOK, 128], F32)
        self.dof_d = self.scratch("dn_of", [NTOK, DN_DV], F32)
    dw = self.dw
    p = self.ctx.sub()
    hT = p.sb([128, KC, NTOK], BF16, "hT")
    p1 = self.ctx.sub()
    self.phase1(p1, s, src, hT)
    p1.close()
    pk = self.ctx.sub()
    zst = Rot([pk.sb([128, 512], BF16, "zst") for _ in range(3)])

    def z_epi(gi, tt, ps):
        st = zst.next()
        o.act(st[:], ps[:], AF.Silu, rd=[ps], wr=[st])
        o.dma("sp", self.dsz_d[tt * 128:(tt + 1) * 128, gi * 512:(gi + 1) * 512], st[:], rd=[st])

    self.gemm_t(pk, hT, KC, dw["wz"], 8, z_epi)
    pk.close()
    pk = self.ctx.sub()
    dtb = pk.sb([128, 64], F32, "dtb")
    aneg = pk.sb([128, 64], F32, "aneg")
    o.dma("sp", dtb[:], dw["dtb"].partition_broadcast(128), wr=[dtb])
    o.dma("sp", aneg[:], dw["alog"].partition_broadcast(128), wr=[aneg])
    o.act(aneg[:], aneg[:], AF.Exp, rd=[aneg], wr=[aneg])
    o.ts("dve", aneg[:], aneg[:], -1.0, None, ALU.mult, rd=[aneg], wr=[aneg])
    gst = Rot([pk.sb([128, 128], F32, "gst") for _ in range(3)])

    def ab_epi(gi, tt, ps):
        st = gst.next()
        o.tt("dve", st[:, 0:64], ps[:, 0:64], dtb[:], ALU.add, rd=[ps, dtb], wr=[st])
        o.act(st[:, 0:64], st[:, 0:64], AF.Exp, rd=[st], wr=[st])
        o.act(st[:, 0:64], st[:, 0:64], AF.Ln, rd=[st], wr=[st], bias=1.0, scale=1.0)
        o.tt("dve", st[:, 0:64], st[:, 0:64], aneg[:], ALU.mult, rd=[st, aneg], wr=[st])
        o.act(st[:, 64:128], ps[:, 64:128], AF.Sigmoid, rd=[ps], wr=[st])
        o.dma("sp", self.dgb_d[tt * 128:(tt + 1) * 128, :], st[:], rd=[st])

    self.gemm_t(pk, hT, KC, dw["wab"], 1, ab_epi, gw=128)
    pk.close()
    pk = self.ctx.sub()
    cw = pk.sb([128, 64, 5], F32, "cw")
    cb = pk.sb([128, 64], F32, "cb")
    o.dma("sp", cw[:], dw["cw"][:, :, :], wr=[cw])
    o.memset("pool", cb[:], 0.0, wr=[cb])
    sqr = pk.sb([128, NTOK], BF16, "sqr")
    rst = pk.sb([128, NTOK], F32, "rst")
    pl2 = Rot([pk.ps([128, 512], F32, "pl2") for _ in range(2)])

    def post(oc, acc, row):
        if oc >= 32:
            o.cp("act", row[:], acc[:], rd=[acc], wr=[row])
            return
        o.act(sqr[:], acc[:], AF.Square, rd=[acc], wr=[sqr])
        for t0 in range(0, NTOK, 512):
            w = min(512, NTOK - t0)
            ps = pl2.next()
            o.mm(ps[:, :w], lhsT=self.ones[:], rhs=sqr[:, t0:t0 + w], rd=[self.ones, sqr], wr=[ps])
            o.act(rst[:, t0:t0 + w], ps[:, :w], AF.Ln, rd=[ps], wr=[rst], bias=self.eps_t[:, 0:1], scale=1.0)
        o.act(rst[:], rst[:], AF.Exp, rd=[rst], wr=[rst], scale=-0.5)
        sc = (128 ** -0.5) if oc < 16 else 1.0
        o.stt("dve", row[:], acc[:], sc, rst[:], ALU.mult, ALU.mult, rd=[acc, rst], wr=[row])

    self.conv_gemm(pk, hT, dw["wx"], 64, cw, cb,
                   tok_dst=lambda oc: None if oc < 16 else ((self.dk_d, (oc - 16) * 128) if oc < 32 else (self.dv_d, (oc - 32) * 128)),
                   fm_dst=lambda oc: (self.dqT_d, oc * 128) if oc < 16 else ((self.dkT_d, (oc - 16) * 128) if oc < 32 else None),
                   post=post)
    pk.close()
    p.close()
    dn_scan_phase(self, s)
    self.phase4(s, src, dst, dst_off, need_ctx, 32, dw["wo"])


Prog.dn_layer = dn_layer


def dn_scan_phase(self, s):
    o = self.o
    dw = self.dw
    p = self.ctx.sub()
    HV, HK = DN_HV, DN_HK
    nwb = p.sb([128, 128], F32, "dnw")
    o.dma("sp", nwb[:], dw["nw"].partition_broadcast(128), wr=[nwb])
    Sf = [p.sb([128, 128], F32, f"Sf{h}") for h in range(HV)]
    Sb = [p.sb([128, 128], BF16, f"Sb{h}") for h in range(HV)]
    gbs = Rot([p.sb([128, 128], F32, "gbc") for _ in range(2)])
    kTs = Rot([p.sb([128, HK, 128], BF16, "kTc") for _ in range(2)])
    qTs = Rot([p.sb([128, HK, 128], BF16, "qTc") for _ in range(2)])
    kts = Rot([p.sb([128, HK, 128], BF16, "ktk") for _ in range(2)])
    vcs = Rot([p.sb([128, HV, 128], BF16, "vch") for _ in range(2)])
    ochs = Rot([p.sb([128, HV, 128], F32, "och") for _ in range(2)])
    sm = {n: p.sb([128, HV], F32, "sm_" + n) for n in ("gc", "ngc", "gcum", "ngcum", "eg", "egl", "etot", "nbeta", "bg", "beta")}
    for n in ("gh", "gl", "ngh", "ngl"):
        sm[n] = p.sb([128, HV], BF16, "sm_" + n)
    of = p.sb([128, HV, 128], F32, "of")
    szc = p.sb([128, HV, 128], BF16, "szc")
    ssq = p.sb([128, HV], F32, "ssqd")
    rstd = p.sb([128, HV], F32, "rstdd")
    un = p.sb([128, HV, 128], BF16, "und")
    junk = un
    zst = Rot([p.sb([128, 32, 128], BF16, "zstd") for _ in range(2)])

    class Slot:
        pass

    slots = []
    NSLOT = 2
    for w_ in range(NSLOT):
        T = Slot()
        T.psq = p.ps([128, 2, 2, 128], F32, "psq")
        T.pch = p.ps([128, 2, 2, 128], F32, "pch")
        T.pmx = p.ps([128, 2, 2, 128], F32, "pmx")
        T.pst = p.ps([128, 2, 2, 128], F32, "pst")
        T.kq = p.sb([128, 2, 128], F32, "kq")
        T.EL = p.sb([128, 2, 128], F32, "EL")
        T.EAT = p.sb([128, 2, 128], F32, "EAT")
        T.P = [p.sb([128, 2, 2, 128], F32, "Pm") for _ in range(2)]
        T.TTf = p.sb([128, 2, 128], F32, "TTf")
        T.TT = p.sb([128, 2, 128], F32, "TT")
        T.AT = p.sb([128, 2, 128], BF16, "AT")
        T.N = p.sb([128, 2, 128], F32, "Nn")
        T.NO = p.sb([128, 2, 128], F32, "NO")
        T.X0 = p.sb([128, 2, 128], F32, "X0")
        T.MM = p.sb([128, 2, 2, 128], F32, "MM")
        T.IMT = p.sb([128, 2, 128], F32, "IMT")
        T.M2 = p.sb([128, 2, 128], F32, "M2")
        T.W = p.sb([128, 2, 128], F32, "Wm")
        T.vb = p.sb([128, 2, 128], F32, "vb")
        T.kbg = p.sb([128, 2, 128], F32, "kbg")
        T.kdec = p.sb([128, 2, 128], BF16, "kdec")
        T.us = p.sb([128, 2, 128], F32, "us")
        T.wT = p.sb([128, 2, 128], BF16, "wTb")
        T.vn = p.sb([128, 2, 128], BF16, "vn")
        T.o1 = p.sb([128, 2, 128], F32, "o1")
        slots.append(T)

    gbv = self.dgb_d.rearrange("(c p) n -> c p n", p=128)
    kTv = self.dkT_d.rearrange("(h d) t -> d h t", d=128)
    qTv = self.dqT_d.rearrange("(h d) t -> d h t", d=128)
    ktv = self.dk_d.rearrange("(c p) (h d) -> c p h d", p=128, d=128)
    vv = self.dv_d.rearrange("(c p) (h d) -> c p h d", p=128, d=128)
    ofv = self.dof_d.rearrange("(c p) (h d) -> c p h d", p=128, d=128)
    szv = self.dsz_d.rearrange("(c p) (h d) -> c p h d", p=128, d=128)
    zv = self.zT.rearrange("(c p) t -> p c t", p=128)
    bc2 = lambda ap: ap.unsqueeze(1).to_broadcast([128, 2, 128])

    def unit(T, hk, d_, kTc, qTc, ktk, vch, och):
        tri, negm = self.tri[d_], self.negm[d_]
        trib, negmb = self.tri[d_ + "16"], self.negm[d_ + "16"]
        negsLb = self.negs[("b" if d_ == "f" else "f") + "16"]
        hv0 = 2 * hk
        o.mm(T.pmx[:, 0, 0, :], lhsT=kTc[:, hk, :], rhs=kTc[:, hk, :], rd=[kTc], wr=[T.pmx])
        o.mm(T.pmx[:, 0, 1, :], lhsT=kTc[:, hk, :], rhs=qTc[:, hk, :], rd=[kTc, qTc], wr=[T.pmx])
        o.cp("act", T.kq[:], T.pmx[:, 0, :, :], rd=[T.pmx], wr=[T.kq])
        for r in range(2):
            hv = hv0 + r
            for (col, hi, lo, msk) in ((0, "ngh", "ngl", negsLb), (1, "gh", "gl", negmb)):
                o.mm(T.psq[:, r, col, :], lhsT=sm[hi][:, hv:hv + 1].to_broadcast([128, 128]), rhs=trib[:], start=True, stop=False,
                     rd=[sm[hi], trib], wr=[T.psq])
                o.mm(T.psq[:, r, col, :], lhsT=sm[lo][:, hv:hv + 1].to_broadcast([128, 128]), rhs=trib[:], start=False, stop=False,
                     rd=[sm[lo], trib], wr=[T.psq])
                o.mm(T.psq[:, r, col, :], lhsT=self.ident[:], rhs=msk[:], start=False, stop=True, rd=[self.ident, msk], wr=[T.psq])
        for r in range(2):
            hv = hv0 + r
            o.act(T.EL[:, r, :], T.psq[:, r, 0, :], AF.Exp, rd=[T.psq, sm["gcum"]], wr=[T.EL], bias=sm["gcum"][:, hv:hv + 1], scale=1.0)
            o.act(T.EAT[:, r, :], T.psq[:, r, 1, :], AF.Exp, rd=[T.psq, sm["ngcum"]], wr=[T.EAT], bias=sm["ngcum"][:, hv:hv + 1], scale=1.0)
        for r in range(2):
            hv = hv0 + r
            o.stt("dve", T.N[:, r, :], T.kq[:, 0, :], sm["nbeta"][:, hv:hv + 1], T.EL[:, r, :], ALU.mult, ALU.mult,
                  rd=[T.kq, sm["nbeta"], T.EL], wr=[T.N])
        o.tt("dve", T.AT[:], bc2(T.kq[:, 1, :]), T.EAT[:], ALU.mult, rd=[T.kq, T.EAT], wr=[T.AT])
        yield
        ND = T.P[0]
        o.tt("dve", ND[:, :, 0, :], T.N[:], bc2(self.bd[:]), ALU.mult, rd=[T.N, self.bd], wr=[ND])
        o.tt("pool", T.NO[:], T.N[:], ND[:, :, 0, :], ALU.subtract, rd=[T.N, ND], wr=[T.NO])
        for r in range(2):
            o.tr(T.pch[:, r, 0, :], ND[:, r, 0, :], self.identf[:], rd=[ND, self.identf], wr=[T.pch])
        o.cp("act", ND[:, :, 1, :], T.pch[:, :, 0, :], rd=[T.pch], wr=[ND])
        o.tt("dve", T.TTf[:], T.pch[:, :, 0, :], bc2(self.identf[:]), ALU.add, rd=[T.pch, self.identf], wr=[T.TTf])
        yield
        Pc = ND
        NL = 4
        for k in range(1, NL + 1):
            Pn = T.P[k % 2]
            for r in range(2):
                o.mm(T.psq[:, r, 0, :], lhsT=Pc[:, r, 1, :], rhs=Pc[:, r, 0, :], rd=[Pc], wr=[T.psq])
                if k < NL:
                    o.mm(T.psq[:, r, 1, :], lhsT=Pc[:, r, 0, :], rhs=Pc[:, r, 1, :], rd=[Pc], wr=[T.psq])
            if k < NL:
                o.cp("act" if k % 2 else "dve", Pn[:], T.psq[:], rd=[T.psq], wr=[Pn])
            else:
                o.cp("act", Pn[:, :, 0, :], T.psq[:, :, 0, :], rd=[T.psq], wr=[Pn])
            yield
            for r in range(2):
                o.mm(T.pch[:, r, 0, :], lhsT=Pn[:, r, 0, :], rhs=T.TTf[:, r, :], rd=[Pn, T.TTf], wr=[T.pch])
            o.tt("dve", T.TTf[:], T.TTf[:], T.pch[:, :, 0, :], ALU.add, rd=[T.TTf, T.pch], wr=[T.TTf])
            Pc = Pn
            yield
        X0T = T.TTf
        for r in range(2):
            o.tr(T.pch[:, r, 0, :], X0T[:, r, :], self.identf[:], rd=[X0T, self.identf], wr=[T.pch])
        o.cp("act", T.X0[:], T.pch[:, :, 0, :], rd=[T.pch], wr=[T.X0])
        for r in range(2):
            o.mm(T.psq[:, r, 0, :], lhsT=X0T[:, r, :], rhs=T.NO[:, r, :], rd=[X0T, T.NO], wr=[T.psq])
            o.mm(T.psq[:, r, 1, :], lhsT=T.NO[:, r, :], rhs=X0T[:, r, :], rd=[X0T, T.NO], wr=[T.psq])
        o.cp("dve", T.MM[:], T.psq[:], rd=[T.psq], wr=[T.MM])
        o.tt("dve", T.IMT[:], T.psq[:, :, 1, :], bc2(self.identf[:]), ALU.add, rd=[T.psq, self.identf], wr=[T.IMT])
        yield
        for r in range(2):
            o.mm(T.pst[:, r, 0, :], lhsT=T.MM[:, r, 1, :], rhs=T.MM[:, r, 0, :], rd=[T.MM], wr=[T.pst])
        o.cp("act", T.M2[:], T.pst[:, :, 0, :], rd=[T.pst], wr=[T.M2])
        yield
        for r in range(2):
            o.mm(T.pch[:, r, 0, :], lhsT=T.MM[:, r, 0, :], rhs=T.MM[:, r, 1, :], start=True, stop=False, rd=[T.MM], wr=[T.pch])
            o.mm(T.pch[:, r, 0, :], lhsT=T.M2[:, r, :], rhs=T.MM[:, r, 1, :], start=False, stop=True, rd=[T.MM, T.M2], wr=[T.pch])
        o.tt("dve", T.W[:], T.IMT[:], T.pch[:, :, 0, :], ALU.add, rd=[T.IMT, T.pch], wr=[T.W])
        yield
        for r in range(2):
            o.mm(T.psq[:, r, 0, :], lhsT=T.X0[:, r, :], rhs=T.W[:, r, :], rd=[T.X0, T.W], wr=[T.psq])
        o.cp("act", T.TT[:], T.psq[:, :, 0, :], rd=[T.psq], wr=[T.TT])
        yield
        TT = T.TT
        for r in range(2):
            hv = hv0 + r
            o.ts("pool", T.vb[:, r, :], vch[:, hv, :], sm["beta"][:, hv:hv + 1], None, ALU.mult, rd=[vch, sm["beta"]], wr=[T.vb])
            o.ts("pool", T.kbg[:, r, :], ktk[:, hk, :], sm["bg"][:, hv:hv + 1], None, ALU.mult, rd=[ktk, sm["bg"]], wr=[T.kbg])
            o.ts("pool", T.kdec[:, r, :], ktk[:, hk, :], sm["egl"][:, hv:hv + 1], None, ALU.mult, rd=[ktk, sm["egl"]], wr=[T.kdec])
        for r in range(2):
            o.mm(T.pmx[:, r, 0, :], lhsT=TT[:, r, :], rhs=T.vb[:, r, :], rd=[TT, T.vb], wr=[T.pmx])
            o.mm(T.pmx[:, r, 1, :], lhsT=T.kbg[:, r, :], rhs=TT[:, r, :], rd=[TT, T.kbg], wr=[T.pmx])
        o.cp("act", T.us[:], T.pmx[:, :, 0, :], rd=[T.pmx], wr=[T.us])
        o.cp("dve", T.wT[:], T.pmx[:, :, 1, :], rd=[T.pmx], wr=[T.wT])
        yield
        for r in range(2):
            hv = hv0 + r
            o.mm(T.pst[:, r, 0, :], lhsT=T.wT[:, r, :], rhs=Sb[hv][:], rd=[T.wT, Sb[hv]], wr=[T.pst])
            o.mm(T.pst[:, r, 1, :], lhsT=qTc[:, hk, :], rhs=Sb[hv][:], rd=[qTc, Sb[hv]], wr=[T.pst])
        o.tt("dve", T.vn[:], T.us[:], T.pst[:, :, 0, :], ALU.subtract, rd=[T.us, T.pst], wr=[T.vn])
        for r in range(2):
            hv = hv0 + r
            o.act(T.o1[:, r, :], T.pst[:, r, 1, :], AF.Copy, rd=[T.pst, sm["eg"]], wr=[T.o1], scale=sm["eg"][:, hv:hv + 1])
        yield
        for r in range(2):
            o.mm(T.pmx[:, r, 0, :], lhsT=T.AT[:, r, :], rhs=T.vn[:, r, :], rd=[T.AT, T.vn], wr=[T.pmx])
            o.mm(T.pmx[:, r, 1, :], lhsT=T.kdec[:, r, :], rhs=T.vn[:, r, :], rd=[T.kdec, T.vn], wr=[T.pmx])
        o.tt("dve", och[:, hv0:hv0 + 2, :], T.o1[:], T.pmx[:, :, 0, :], ALU.add, rd=[T.o1, T.pmx], wr=[och])
        for r in range(2):
            hv = hv0 + r
            o.stt("dve", Sf[hv][:], Sf[hv][:], sm["etot"][:, hv:hv + 1], T.pmx[:, r, 1, :], ALU.mult, ALU.add,
                  rd=[Sf[hv], sm["etot"], T.pmx], wr=[Sf[hv]])
            o.cp("pool", Sb[hv][:], Sf[hv][:], rd=[Sf[hv]], wr=[Sb[hv]])
        yield

    for di, d_ in enumerate(("f", "b")):
        tri = self.tri[d_]
        self.S.barrier()
        for hv in range(HV):
            o.memset("pool", Sf[hv][:], 0.0, wr=[Sf[hv]])
            o.memset("pool", Sb[hv][:], 0.0, wr=[Sb[hv]])
        for c in chunk_order(d_):
            tsl = slice(c * 128, (c + 1) * 128)
            gbc, kTc, qTc, ktk, vch, och = gbs.next(), kTs.next(), qTs.next(), kts.next(), vcs.next(), ochs.next()
            o.dma("sp", gbc[:], gbv[c], wr=[gbc])
            o.dma("sp", kTc[:], kTv[:, :, tsl], wr=[kTc])
            o.dma("sp", qTc[:], qTv[:, :, tsl], wr=[qTc])
            o.dma("sp", ktk[:], ktv[c], wr=[ktk])
            o.dma("sp", vch[:], vv[c], wr=[vch])
            g_in = gbc[:, di * HV:(di + 1) * HV]
            b_in = gbc[:, 64 + di * HV:64 + (di + 1) * HV]
            pm = slots[0].pst
            o.cp("dve", sm["gc"][:], g_in, rd=[gbc], wr=[sm["gc"]])
            o.ts("dve", sm["ngc"][:], g_in, -1.0, None, ALU.mult, rd=[gbc], wr=[sm["ngc"]])
            o.cp("dve", sm["gh"][:], sm["gc"][:], rd=[sm["gc"]], wr=[sm["gh"]])
            o.tt("dve", sm["gl"][:], sm["gc"][:], sm["gh"][:], ALU.subtract, rd=[sm["gc"], sm["gh"]], wr=[sm["gl"]])
            o.ts("dve", sm["ngh"][:], sm["gh"][:], -1.0, None, ALU.mult, rd=[sm["gh"]], wr=[sm["ngh"]])
            o.ts("dve", sm["ngl"][:], sm["gl"][:], -1.0, None, ALU.mult, rd=[sm["gl"]], wr=[sm["ngl"]])
            o.cp("dve", sm["beta"][:], b_in, rd=[gbc], wr=[sm["beta"]])
            o.ts("dve", sm["nbeta"][:], b_in, -1.0, None, ALU.mult, rd=[gbc], wr=[sm["nbeta"]])
            o.mm(pm[:, 0, 0, 0:HV], lhsT=tri[:], rhs=sm["gc"][:], rd=[tri, sm["gc"]], wr=[pm])
            o.mm(pm[:, 0, 0, HV:2 * HV], lhsT=self.onesf[:], rhs=sm["gc"][:], rd=[self.onesf, sm["gc"]], wr=[pm])
            o.cp("dve", sm["gcum"][:], pm[:, 0, 0, 0:HV], rd=[pm], wr=[sm["gcum"]])
            o.ts("dve", sm["ngcum"][:], pm[:, 0, 0, 0:HV], -1.0, None, ALU.mult, rd=[pm], wr=[sm["ngcum"]])
            o.tt("dve", sm["egl"][:], pm[:, 0, 0, HV:2 * HV], sm["gcum"][:], ALU.subtract, rd=[pm, sm["gcum"]], wr=[sm["egl"]])
            o.act(sm["etot"][:], pm[:, 0, 0, HV:2 * HV], AF.Exp, rd=[pm], wr=[sm["etot"]])
            o.act(sm["eg"][:], sm["gcum"][:], AF.Exp, rd=[sm["gcum"]], wr=[sm["eg"]])
            o.act(sm["egl"][:], sm["egl"][:], AF.Exp, rd=[sm["egl"]], wr=[sm["egl"]])
            o.tt("dve", sm["bg"][:], sm["eg"][:], sm["beta"][:], ALU.mult, rd=[sm["eg"], sm["beta"]], wr=[sm["bg"]])
            for hk0 in range(0, HK, NSLOT):
                gens = [unit(slots[w_], hk0 + w_, d_, kTc, qTc, ktk, vch, och) for w_ in range(min(NSLOT, HK - hk0))]
                alive = list(gens)
                while alive:
                    for g_ in list(alive):
                        try:
                            next(g_)
                        except StopIteration:
                            alive.remove(g_)
            if d_ == "f":
                o.dma("sp", ofv[c], och[:], rd=[och])
                continue
            o.dma("sp", of[:], ofv[c], wr=[of])
            o.dma("sp", szc[:], szv[c], wr=[szc])
            o.tt("pool", och[:], och[:], of[:], ALU.add, rd=[och, of], wr=[och])
            o.act(junk[:], och[:], AF.Square, rd=[och], wr=[junk])
            o.S.op("dve", lambda e: e.tensor_reduce(out=ssq[:], in_=junk[:], axis=AX.X, op=ALU.add), R_([junk]), R_([ssq]))
            o.act(rstd[:], ssq[:], AF.Ln, rd=[ssq], wr=[rstd], bias=self.eps_t[:, 0:1], scale=1.0 / 128)
            o.act(rstd[:], rstd[:], AF.Exp, rd=[rstd], wr=[rstd], scale=-0.5)
            o.tt("dve", och[:], och[:], rstd[:].unsqueeze(2).to_broadcast([128, HV, 128]), ALU.mult, rd=[och, rstd], wr=[och])
            o.tt("pool", och[:], och[:], nwb[:].unsqueeze(1).to_broadcast([128, HV, 128]), ALU.mult, rd=[och, nwb], wr=[och])
            o.tt("dve", un[:], och[:], szc[:], ALU.mult, rd=[och, szc], wr=[un])
            st = zst.next()
            for c8 in range(0, 32, 8):
                T = slots[(c8 // 8) % 2]
                ptb = T.psq[:].rearrange("p a b c -> p (a b c)").bitcast(BF16).rearrange("p (a b) -> p a b", b=128)
                for q in range(8):
                    o.tr(ptb[:, q, :], un[:, c8 + q, :], self.ident[:], rd=[un, self.ident], wr=[T.psq])
                o.cp("dve", st[:, c8:c8 + 8, :], ptb[:, 0:8, :], rd=[T.psq], wr=[st])
            o.dma("sp", zv[:, :, tsl], st[:], rd=[st])
    p.close()
```
